# Optimizing a Trainium2 kernel written in Bass

```python
import math
import jax
import jax.numpy as jnp
from jax import lax
import numpy as np

D_MODEL = 1024
BATCH = 4
SEQ = 8192
DEPTH = 4

GRID_W = 64
CTX_LEN = 256
N_MIXERS = 3
MLP_HIDDEN = 4 * D_MODEL
NORM_EPS = 1e-6

HY_ORDER = 2
HY_EMB = 33
HY_FILTER_WIDTH = 64
HY_SHORT = 3
HY_FAST_DECAY = 0.3
HY_SLOW_DECAY = 1.5
HY_TARGET = 1e-2

GLA_HEADS = 4
GLA_DK = D_MODEL // 2
GLA_DV = D_MODEL
GLA_HK = GLA_DK // GLA_HEADS
GLA_HV = GLA_DV // GLA_HEADS
GLA_GATE_RANK = 16
GLA_GATE_NORM = 16.0
GLA_CHUNK = 64

MLA_HEADS = 8
MLA_Q_RANK = 384
MLA_KV_RANK = 256
MLA_NOPE = 128
MLA_ROPE = 64
MLA_V = 128
ROPE_THETA = 10000.0
Q_BLOCK = 128

N_HYENA = (DEPTH + 2) // 3
N_GLA = (DEPTH + 1) // 3
N_MLA = DEPTH // 3

kernel_name = 'hybrid_hyena_gla_mla_diffusion_trunk'

F32 = jnp.float32


def rmsnorm(x, g):
    xf = x.astype(F32)
    y = xf * lax.rsqrt(jnp.mean(xf * xf, axis=-1, keepdims=True) + NORM_EPS)
    return (y * g.astype(F32)).astype(x.dtype)


def modulate(h, shift, scale):
    return h * (1 + scale) + shift


def sq_relu_mlp(h, w1, w2):
    return jnp.square(jax.nn.relu(h @ w1)) @ w2


def short_conv(u, w, b):
    L = u.shape[1]
    pad = HY_SHORT // 2
    up = jnp.pad(u, ((0, 0), (pad, HY_SHORT - 1 - pad), (0, 0)))
    out = up[:, 0:L] * w[0]
    for j in range(1, HY_SHORT):
        out = out + up[:, j:j + L] * w[j]
    return out + b


def hyena_filter_spectra(L, f_w1, f_b1, f_w2, f_b2, f_w3, f_b3, f_w4, freq):
    t = jnp.linspace(0.0, 1.0, L, dtype=F32)[:, None]
    bands = (HY_EMB - 1) // 2
    w = 2.0 * math.pi * jnp.arange(L, dtype=F32)[:, None] / L
    f = jnp.linspace(1e-4, bands - 1, bands, dtype=F32)[None, :]
    z = jnp.concatenate([t, jnp.cos(f * w), -jnp.sin(f * w)], axis=-1)
    fr = freq.astype(F32)
    h = jnp.sin(fr * (z @ f_w1.astype(F32) + f_b1.astype(F32)))
    h = jnp.sin(fr * (h @ f_w2.astype(F32) + f_b2.astype(F32)))
    h = jnp.sin(fr * (h @ f_w3.astype(F32) + f_b3.astype(F32)))
    h = (h @ f_w4.astype(F32)).reshape(L, HY_ORDER, 2, D_MODEL)
    max_decay = math.log(HY_TARGET) / HY_FAST_DECAY
    min_decay = math.log(HY_TARGET) / HY_SLOW_DECAY
    deltas = jnp.linspace(min_decay, max_decay, D_MODEL, dtype=F32)
    h = h * jnp.exp(-t * jnp.abs(deltas))[:, None, None, :]
    fwd, bwd = h[:, :, 0], h[:, :, 1]
    two_sided = jnp.concatenate(
        [fwd, jnp.zeros((1, HY_ORDER, D_MODEL), F32), jnp.flip(bwd[1:], axis=0)], axis=0)
    two_sided = two_sided / jnp.sum(jnp.abs(two_sided), axis=0, keepdims=True)
    return jnp.fft.rfft(two_sided, axis=0)


def fft_long_conv(u, spec, skip):
    L = u.shape[1]
    y = jnp.fft.irfft(jnp.fft.rfft(u, n=2 * L, axis=1) * spec[None], n=2 * L, axis=1)[:, :L]
    return y + u * skip


def hyena_mixer(u, w_in, b_in, conv_w, conv_b, f_w1, f_b1, f_w2, f_b2, f_w3, f_b3, f_w4,
                freq, bias, w_out, b_out):
    L = u.shape[1]
    z = short_conv(u @ w_in + b_in, conv_w, conv_b).astype(F32)
    parts = jnp.split(z, HY_ORDER + 1, axis=-1)
    y, gates = parts[0], parts[1:]
    spec = hyena_filter_spectra(L, f_w1, f_b1, f_w2, f_b2, f_w3, f_b3, f_w4, freq)
    skip = bias.astype(F32)
    for n in range(HY_ORDER):
        y = gates[n] * fft_long_conv(y, spec[:, n], skip[n])
    return y.astype(u.dtype) @ w_out + b_out


def gla_inputs(u, w_in, gk_w2, gk_b):
    B, L, _ = u.shape
    z = u @ w_in
    q, k, v, og, r = jnp.split(
        z, [GLA_DK, 2 * GLA_DK, 2 * GLA_DK + GLA_DV, 2 * GLA_DK + 2 * GLA_DV], axis=-1)
    r = r.reshape(B, L, 2, GLA_GATE_RANK)
    gk = jnp.einsum('blzr,zrd->blzd', r, gk_w2) + gk_b
    g = (jax.nn.log_sigmoid(gk.astype(F32)) / GLA_GATE_NORM).reshape(B, L, 2, GLA_HEADS, GLA_HK)
    q = q.astype(F32).reshape(B, L, GLA_HEADS, GLA_HK) * (GLA_HK ** -0.5)
    k = k.astype(F32).reshape(B, L, GLA_HEADS, GLA_HK)
    v = v.astype(F32).reshape(B, L, GLA_HEADS, GLA_HV)
    return q, k, v, g[:, :, 0], g[:, :, 1], og


def gla_chunk_scan(q, k, v, g, s0):
    B, L, H, dk = q.shape
    dv = v.shape[-1]
    n_chunks = L // GLA_CHUNK

    def to_chunks(a):
        return a.reshape(B, n_chunks, GLA_CHUNK, H, a.shape[-1]).transpose(1, 0, 3, 2, 4)

    q, k, v, g = to_chunks(q), to_chunks(k), to_chunks(v), to_chunks(g)
    b = jnp.cumsum(g, axis=3)
    b_last = b[..., -1:, :]
    q_t = q * jnp.exp(b)
    k_t = k * jnp.exp(-b)
    k_end = k * jnp.exp(b_last - b)
    mask = jnp.tril(jnp.ones((GLA_CHUNK, GLA_CHUNK), dtype=bool))
    att = jnp.where(mask, jnp.einsum('nbhcd,nbhsd->nbhcs', q_t, k_t), 0.0)
    o_intra = jnp.einsum('nbhcs,nbhsv->nbhcv', att, v)
    chunk_decay = jnp.exp(b_last[..., 0, :])

    def step(S, inp):
        qn, kn, vn, dn = inp
        o = jnp.einsum('bhcd,bhdv->bhcv', qn, S)
        S = S * dn[..., None] + jnp.einsum('bhcd,bhcv->bhdv', kn, vn)
        return S, o

    s_fin, o_inter = lax.scan(step, s0, (q_t, k_end, v, chunk_decay))
    o = (o_intra + o_inter).transpose(1, 0, 3, 2, 4).reshape(B, L, H, dv)
    return o, s_fin


def gla_direction(q, k, v, g, s0, reverse):
    if reverse:
        q, k, v, g = (jnp.flip(a, axis=1) for a in (q, k, v, g))
    o, s_fin = gla_chunk_scan(q, k, v, g, s0)
    if reverse:
        o = jnp.flip(o, axis=1)
    return o, s_fin


def gla_out(o, og, onorm, wo):
    B, L = o.shape[:2]
    o = rmsnorm(o, onorm).reshape(B, L, GLA_DV).astype(og.dtype)
    return (o * jax.nn.silu(og)) @ wo


def gla_mixer(h_lat, h_ctx, w_in, gk_w2, gk_b, onorm, wo, ctx_out):
    ql, kl, vl, gfl, gbl, ogl = gla_inputs(h_lat, w_in, gk_w2, gk_b)
    qc, kc, vc, gfc, gbc, ogc = gla_inputs(h_ctx, w_in, gk_w2, gk_b)
    s0 = jnp.zeros((h_lat.shape[0], GLA_HEADS, GLA_HK, GLA_HV), F32)
    oc_f, s_f = gla_direction(qc, kc, vc, gfc, s0, False)
    oc_b, s_b = gla_direction(qc, kc, vc, gbc, s0, True)
    ol_f, _ = gla_direction(ql, kl, vl, gfl, s_f, False)
    ol_b, _ = gla_direction(ql, kl, vl, gbl, s_b, True)
    y_lat = gla_out(ol_f + ol_b, ogl, onorm, wo)
    y_ctx = gla_out(oc_f + oc_b, ogc, onorm, wo) if ctx_out else None
    return y_lat, y_ctx


def rotate_half_axis(xh, ang):
    cos = jnp.cos(ang)[None, :, None, :]
    sin = jnp.sin(ang)[None, :, None, :]
    x1, x2 = jnp.split(xh, 2, axis=-1)
    return jnp.concatenate([x1 * cos - x2 * sin, x1 * sin + x2 * cos], axis=-1)


def axial_rope(x, ang_row, ang_col):
    xf = x.astype(F32)
    half = x.shape[-1] // 2
    out = jnp.concatenate([rotate_half_axis(xf[..., :half], ang_row),
                           rotate_half_axis(xf[..., half:], ang_col)], axis=-1)
    return out.astype(x.dtype)


def mla_qkv(u, angles, w_down, qnorm, w_uq, kvnorm, w_ukv):
    B, L, _ = u.shape
    cq, ckv, k_rope = jnp.split(u @ w_down, [MLA_Q_RANK, MLA_Q_RANK + MLA_KV_RANK], axis=-1)
    q = (rmsnorm(cq, qnorm) @ w_uq).reshape(B, L, MLA_HEADS, MLA_NOPE + MLA_ROPE)
    kv = (rmsnorm(ckv, kvnorm) @ w_ukv).reshape(B, L, MLA_HEADS, MLA_NOPE + MLA_V)
    q_nope, q_rope = q[..., :MLA_NOPE], q[..., MLA_NOPE:]
    k_nope, v = kv[..., :MLA_NOPE], kv[..., MLA_NOPE:]
    k_rope = k_rope[:, :, None, :]
    if angles is not None:
        q_rope = axial_rope(q_rope, angles[0], angles[1])
        k_rope = axial_rope(k_rope, angles[0], angles[1])
    q = jnp.concatenate([q_nope, q_rope], axis=-1)
    k = jnp.concatenate([k_nope, jnp.broadcast_to(k_rope, (B, L, MLA_HEADS, MLA_ROPE))], axis=-1)
    return q, k, v


def block_attention(q, k, v):
    B, S, H, dq = q.shape
    scale = (MLA_NOPE + MLA_ROPE) ** -0.5
    n_blocks = S // Q_BLOCK
    qb = q.reshape(B, n_blocks, Q_BLOCK, H, dq).transpose(1, 0, 2, 3, 4)

    def one_block(qi):
        s = jnp.einsum('bqhd,bkhd->bhqk', qi, k, preferred_element_type=F32) * scale
        p = jax.nn.softmax(s, axis=-1)
        return jnp.einsum('bhqk,bkhv->bqhv', p.astype(v.dtype), v)

    o = lax.map(one_block, qb)
    return o.transpose(1, 0, 2, 3, 4).reshape(B, S, H * v.shape[-1])


def mla_mixer(h_lat, h_ctx, w_down, qnorm, w_uq, kvnorm, w_ukv, wo, ctx_out):
    L = h_lat.shape[1]
    rows = L // GRID_W
    r_idx, c_idx = jnp.meshgrid(jnp.arange(rows), jnp.arange(GRID_W), indexing='ij')
    half = MLA_ROPE // 2
    inv_freq = ROPE_THETA ** (-jnp.arange(0, half, 2, dtype=F32) / half)
    ang_row = r_idx.reshape(-1).astype(F32)[:, None] * inv_freq[None, :]
    ang_col = c_idx.reshape(-1).astype(F32)[:, None] * inv_freq[None, :]
    ql, kl, vl = mla_qkv(h_lat, (ang_row, ang_col), w_down, qnorm, w_uq, kvnorm, w_ukv)
    qc, kc, vc = mla_qkv(h_ctx, None, w_down, qnorm, w_uq, kvnorm, w_ukv)
    k_all = jnp.concatenate([kc, kl], axis=1)
    v_all = jnp.concatenate([vc, vl], axis=1)
    y_lat = block_attention(ql, k_all, v_all) @ wo
    y_ctx = block_attention(qc, kc, vc) @ wo if ctx_out else None
    return y_lat, y_ctx


def setup_inputs(seed: int = 0) -> dict:
    key = jax.random.key(seed)
    ks = iter(jax.random.split(key, 64))
    D = D_MODEL

    def nrm(shape, scale):
        return jax.random.normal(next(ks), shape, F32) * scale

    def gain(shape):
        return 1.0 + nrm(shape, 0.05)

    gla_in_cols = 2 * GLA_DK + 2 * GLA_DV + 2 * GLA_GATE_RANK
    fw = HY_FILTER_WIDTH
    return {
        'x': nrm((BATCH, SEQ, D), 1.0),
        'c': nrm((BATCH, D), 1.0),
        'ctx': nrm((BATCH, CTX_LEN, D), 1.0),
        'c_ctx': nrm((D,), 1.0),
        'ada_w': nrm((DEPTH, D, 6 * D), 0.5 * D ** -0.5),
        'ada_b': nrm((DEPTH, 6 * D), 0.01),
        'norm1_g': gain((DEPTH, D)),
        'norm2_g': gain((DEPTH, D)),
        'mlp_w1': nrm((DEPTH, D, MLP_HIDDEN), D ** -0.5),
        'mlp_w2': nrm((DEPTH, MLP_HIDDEN, D), MLP_HIDDEN ** -0.5),
        'final_g': gain((D,)),
        'hy_w_in': nrm((N_HYENA, D, (HY_ORDER + 1) * D), D ** -0.5),
        'hy_b_in': nrm((N_HYENA, (HY_ORDER + 1) * D), 0.01),
        'hy_conv_w': nrm((N_HYENA, HY_SHORT, (HY_ORDER + 1) * D), HY_SHORT ** -0.5),
        'hy_conv_b': nrm((N_HYENA, (HY_ORDER + 1) * D), 0.01),
        'hy_f_w1': nrm((N_HYENA, HY_EMB, fw), HY_EMB ** -0.5),
        'hy_f_b1': nrm((N_HYENA, fw), 0.1),
        'hy_f_w2': nrm((N_HYENA, fw, fw), fw ** -0.5),
        'hy_f_b2': nrm((N_HYENA, fw), 0.1),
        'hy_f_w3': nrm((N_HYENA, fw, fw), fw ** -0.5),
        'hy_f_b3': nrm((N_HYENA, fw), 0.1),
        'hy_f_w4': nrm((N_HYENA, fw, HY_ORDER * 2 * D), fw ** -0.5),
        'hy_freq': 1.0 + nrm((N_HYENA, fw), 0.1),
        'hy_bias': nrm((N_HYENA, HY_ORDER, D), 0.5),
        'hy_w_out': nrm((N_HYENA, D, D), D ** -0.5),
        'hy_b_out': nrm((N_HYENA, D), 0.01),
        'gla_w_in': nrm((N_GLA, D, gla_in_cols), D ** -0.5),
        'gla_gk_w2': nrm((N_GLA, 2, GLA_GATE_RANK, GLA_DK), GLA_GATE_RANK ** -0.5),
        'gla_gk_b': nrm((N_GLA, 2, GLA_DK), 0.1),
        'gla_onorm': gain((N_GLA, GLA_HV)),
        'gla_wo': nrm((N_GLA, GLA_DV, D), GLA_DV ** -0.5),
        'mla_w_down': nrm((N_MLA, D, MLA_Q_RANK + MLA_KV_RANK + MLA_ROPE), D ** -0.5),
        'mla_qnorm': gain((N_MLA, MLA_Q_RANK)),
        'mla_w_uq': nrm((N_MLA, MLA_Q_RANK, MLA_HEADS * (MLA_NOPE + MLA_ROPE)), MLA_Q_RANK ** -0.5),
        'mla_kvnorm': gain((N_MLA, MLA_KV_RANK)),
        'mla_w_ukv': nrm((N_MLA, MLA_KV_RANK, MLA_HEADS * (MLA_NOPE + MLA_V)), MLA_KV_RANK ** -0.5),
        'mla_wo': nrm((N_MLA, MLA_HEADS * MLA_V, D), (MLA_HEADS * MLA_V) ** -0.5),
    }


def reference(x, c, ctx, c_ctx, ada_w, ada_b, norm1_g, norm2_g, mlp_w1, mlp_w2, final_g,
              hy_w_in, hy_b_in, hy_conv_w, hy_conv_b, hy_f_w1, hy_f_b1, hy_f_w2, hy_f_b2,
              hy_f_w3, hy_f_b3, hy_f_w4, hy_freq, hy_bias, hy_w_out, hy_b_out,
              gla_w_in, gla_gk_w2, gla_gk_b, gla_onorm, gla_wo,
              mla_w_down, mla_qnorm, mla_w_uq, mla_kvnorm, mla_w_ukv, mla_wo):
    x_lat = x
    x_ctx = ctx
    silu_c = jax.nn.silu(c)
    silu_cc = jax.nn.silu(c_ctx)
    for i in range(DEPTH):
        kind = i % N_MIXERS
        j = i // N_MIXERS
        ctx_read = kind != 0
        ctx_live = any(l % N_MIXERS != 0 for l in range(i + 1, DEPTH))
        mod_l = jnp.split((silu_c @ ada_w[i] + ada_b[i])[:, None, :], 6, axis=-1)
        h_lat = modulate(rmsnorm(x_lat, norm1_g[i]), mod_l[0], mod_l[1])
        if ctx_read or ctx_live:
            mod_c = jnp.split(silu_cc @ ada_w[i] + ada_b[i], 6, axis=-1)
            h_ctx = modulate(rmsnorm(x_ctx, norm1_g[i]), mod_c[0], mod_c[1])
        if kind == 0:
            hp = (hy_w_in[j], hy_b_in[j], hy_conv_w[j], hy_conv_b[j], hy_f_w1[j], hy_f_b1[j],
                  hy_f_w2[j], hy_f_b2[j], hy_f_w3[j], hy_f_b3[j], hy_f_w4[j], hy_freq[j],
                  hy_bias[j], hy_w_out[j], hy_b_out[j])
            y_lat = hyena_mixer(h_lat, *hp)
            y_ctx = hyena_mixer(h_ctx, *hp) if ctx_live else None
        elif kind == 1:
            y_lat, y_ctx = gla_mixer(h_lat, h_ctx, gla_w_in[j], gla_gk_w2[j], gla_gk_b[j],
                                     gla_onorm[j], gla_wo[j], ctx_live)
        else:
            y_lat, y_ctx = mla_mixer(h_lat, h_ctx, mla_w_down[j], mla_qnorm[j], mla_w_uq[j],
                                     mla_kvnorm[j], mla_w_ukv[j], mla_wo[j], ctx_live)
        x_lat = x_lat + mod_l[2] * y_lat
        x_lat = x_lat + mod_l[5] * sq_relu_mlp(
            modulate(rmsnorm(x_lat, norm2_g[i]), mod_l[3], mod_l[4]), mlp_w1[i], mlp_w2[i])
        if ctx_live:
            x_ctx = x_ctx + mod_c[2] * y_ctx
            x_ctx = x_ctx + mod_c[5] * sq_relu_mlp(
                modulate(rmsnorm(x_ctx, norm2_g[i]), mod_c[3], mod_c[4]), mlp_w1[i], mlp_w2[i])
    return rmsnorm(x_lat, final_g)
```

```python
import math
import os
from contextlib import ExitStack
import numpy as np
import concourse.bass as bass
import concourse.mybir as mybir
from concourse.bass_utils import run_bass_kernel_spmd

F32 = mybir.dt.float32
BF16 = mybir.dt.bfloat16
AF = mybir.ActivationFunctionType
ALU = mybir.AluOpType
AX = mybir.AxisListType


class Tk:
    __slots__ = ("name", "t", "last_w", "readers", "dsem", "dcnt")

    def __init__(self, name, t):
        self.name = name
        self.t = t
        self.last_w = None
        self.readers = {}
        self.dsem = None
        self.dcnt = 0

    def __getitem__(self, idx):
        return self.t[idx]


class B:
    EPOCH = 28000

    def __init__(self):
        self.nc = bass.Bass("TRN2", target_bir_lowering=False)
        nc = self.nc
        self.eng = {"pe": nc.tensor, "dve": nc.vector, "act": nc.scalar, "pool": nc.gpsimd, "sp": nc.sync}
        self.esem = {}
        self.ecnt = {}
        self.known = {e: {} for e in self.eng}
        self.nsem = 0
        self.allsems = []
        self.sempool = []
        self.phase_tks = [[]]
        for e in self.eng:
            self.esem[e] = self.newsem("e_" + e)
            self.ecnt[e] = 0
        self.ninstr = 0
        self.out_objs = []
        self.uid = 0
        self.stk = ExitStack()
        self.stks = []
        self.dma_owners = {}
        self.pending_old = []
        self.barsem = self.newsem("bar")
        self.barcnt = 0

    def newsem(self, name):
        self.nsem += 1
        sm = self.nc.alloc_semaphore(name + "_%d" % self.nsem)
        self.allsems.append(sm)
        return sm

    def sb(self, name, shape, dt):
        self.uid += 1
        t = self.stk.enter_context(self.nc.sbuf_tensor("%s_%d" % (name, self.uid), list(shape), dt))
        tk = Tk(name, t)
        self.phase_tks[-1].append(tk)
        return tk

    def begin_phase(self):
        self.stks.append(self.stk)
        self.stk = ExitStack()
        self.phase_tks.append([])

    def end_phase(self):
        self.minrem = min(getattr(self, "minrem", 1 << 30), self.nc.sbuf_bytes_remaining)
        if PRINT_MEM:
            print("phase end: sbuf remaining", self.nc.sbuf_bytes_remaining)
        self.barrier()
        for tk in self.phase_tks.pop():
            if tk.dsem is not None:
                if tk.dcnt < self.EPOCH - 4000:
                    self.sempool.append((tk.dsem, tk.dcnt))
                tk.dsem = None
        self.stk.close()
        self.stk = self.stks.pop()

    def barrier(self):
        for e in self.eng:
            if self.ecnt[e] > 0:
                self._need("sp", (self.esem[e], self.ecnt[e]))
        for o in self.dma_owners.values():
            self._need("sp", (o.dsem, o.dcnt))
        self.dma_owners = {}
        for dep in self.pending_old:
            self._need("sp", dep)
        self.pending_old = []
        if self.barcnt >= self.EPOCH:
            self.barsem = self.newsem("bar")
            self.barcnt = 0
        self.eng["sp"].sem_inc(self.barsem, 1)
        self.barcnt += 1
        for e in self.eng:
            if e != "sp":
                self.eng[e].wait_ge(self.barsem, self.barcnt)
            for e2 in self.eng:
                self.known[e][id(self.esem[e2])] = self.ecnt[e2]

    def ps(self, name, shape=(128, 512), dt=F32):
        return Tk(name, self.nc.alloc_psum_tensor(name, list(shape), dt))

    def dram(self, name, shape, dt, kind="Internal"):
        return Tk(name, self.nc.dram_tensor(name, list(shape), dt, kind=kind))

    def _need(self, e, dep):
        if dep is None:
            return
        sem, val = dep
        k = self.known[e]
        if k.get(id(sem), 0) >= val:
            return
        self.eng[e].wait_ge(sem, val)
        k[id(sem)] = val

    def _deps(self, e, reads, writes, pe_acc=False):
        for r in reads:
            self._need(e, r.last_w)
        for w in writes:
            if not (pe_acc and w.last_w is not None and w.last_w[0] is self.esem["pe"]):
                self._need(e, w.last_w)
            for dep in w.readers.values():
                self._need(e, dep)

    def _mark(self, dep, reads, writes):
        for r in reads:
            r.readers[id(dep[0])] = dep
        for w in writes:
            w.last_w = dep
            w.readers = {}

    def op(self, e, ins_fn, reads=(), writes=(), pe_acc=False):
        self._deps(e, reads, writes, pe_acc)
        if self.ecnt[e] >= self.EPOCH:
            self.esem[e] = self.newsem("e_" + e)
            self.ecnt[e] = 0
        ins = ins_fn(self.eng[e])
        self.ecnt[e] += 1
        ins.then_inc(self.esem[e], 1)
        dep = (self.esem[e], self.ecnt[e])
        self._mark(dep, reads, writes)
        self.ninstr += 1

    def dma(self, q, out_ap, in_ap, reads=(), writes=(), owner=None, **kw):
        if owner is None:
            owner = writes[0]
        self._deps(q, reads, writes)
        if owner.dsem is None or owner.dcnt >= self.EPOCH:
            if owner.dsem is not None:
                self.pending_old.append((owner.dsem, owner.dcnt))
            if self.sempool:
                owner.dsem, owner.dcnt = self.sempool.pop()
            else:
                owner.dsem = self.newsem("d_" + owner.name)
                owner.dcnt = 0
        ins = self.eng[q].dma_start(out=out_ap, in_=in_ap, **kw)
        owner.dcnt += 16
        ins.then_inc(owner.dsem, 16)
        dep = (owner.dsem, owner.dcnt)
        self.dma_owners[id(owner)] = owner
        self._mark(dep, reads, writes)
        self.ninstr += 1

    def finish(self):
        for o in self.out_objs:
            self._need("sp", o.last_w)


class Cfg:
    D = 1024
    DC = 8
    HID = 4096
    HC = 32
    DEPTH = 4
    EPS = 1e-6

    def __init__(self, SEQ=8192, CTX=256, TB=256):
        self.SEQ = SEQ
        self.CTX = CTX
        self.NTOK = SEQ + CTX
        self.TB = TB


ENG3 = ("dve", "act", "pool")
PRINT_MEM = False
GLA_BARRIER = int(os.environ.get('GLA_BARRIER', '1'))


class Net(B):
    def __init__(self, cfg, test=None):
        super().__init__()
        self.cfg = cfg
        self.test = test
        self.inputs = {}
        self.rr = 0
        nc = self.nc
        self.pb = [self.ps("pb%d" % i) for i in range(8)]
        self.pbi = 0

    def inp(self, name, shape, dt=F32):
        t = self.dram(name, shape, dt, kind="ExternalInput")
        self.inputs[name] = t
        return t

    def ccol(self, val):
        return self.ccols[round(float(val), 9)]

    def make_consts(self, vals):
        for v in vals:
            key = round(float(v), 9)
            t = self.sb("cc", [128, 1], F32)
            self.op("pool", lambda g: g.memset(t.t[:], float(v)), [], [t])
            self.ccols[key] = t

    def bank(self):
        b = self.pb[self.pbi % 6]
        self.pbi += 1
        return b

    def copy(self, e, out_t, out_ap, in_t, in_ap):
        if e == "act":
            self.op("act", lambda g: g.activation(out=out_ap, in_=in_ap, func=AF.Copy), [in_t], [out_t])
        else:
            self.op(e, lambda g: g.tensor_copy(out=out_ap, in_=in_ap), [in_t], [out_t])

    def load_cast(self, w2d, K, N, dst, stg, qs=("sp", "act")):
        S = stg[0].t.shape[-1]
        wv = w2d.rearrange("(k p) n -> p k n", p=128)
        KC = K // 128
        if N >= S:
            assert N % S == 0
            pieces = [(k, 1, n0, S) for k in range(KC) for n0 in range(0, N, S)]
        else:
            g = S // N
            assert KC % g == 0
            pieces = [(k, g, 0, N) for k in range(0, KC, g)]
        for (k, g, n0, n) in pieces:
            st = stg[self.rr % len(stg)]
            q = qs[self.rr % len(qs)]
            e = ENG3[self.rr % 3]
            self.rr += 1
            sv = st.t[:, 0:g * n].rearrange("p (g n) -> p g n", g=g)
            self.dma(q, sv, wv[:, k:k + g, n0:n0 + n], reads=[], writes=[st])
            self.copy(e, dst, dst.t[:, k:k + g, n0:n0 + n], st, sv)

    def build_ada(self):
        cfg = self.cfg
        cc = self.inp("cc", [128, 8, 2])
        ada_w = self.inp("ada_w", [cfg.DEPTH, 1024, 6144])
        ada_b = self.inp("ada_b", [128, cfg.DEPTH, 48])
        n1g = self.inp("n1g", [128, cfg.DEPTH, 8])
        n2g = self.inp("n2g", [128, cfg.DEPTH, 8])
        scs = [self.sb("sc%d" % l, [128, 48, 2], F32) for l in range(cfg.DEPTH)]
        self.begin_phase()
        cs = self.sb("cs", [128, 8, 2], F32)
        cs2 = self.sb("cs2", [128, 8, 2], F32)
        abt = self.sb("abt", [128, cfg.DEPTH, 48], F32)
        g1t = self.sb("g1t", [128, cfg.DEPTH, 8], F32)
        g2t = self.sb("g2t", [128, cfg.DEPTH, 8], F32)
        self.dma("sp", cs.t[:], cc.t.ap(), writes=[cs])
        self.dma("sp", abt.t[:], ada_b.t.ap(), writes=[abt])
        self.dma("sp", g1t.t[:], n1g.t.ap(), writes=[g1t])
        self.dma("sp", g2t.t[:], n2g.t.ap(), writes=[g2t])
        self.op("act", lambda g: g.activation(out=cs2.t[:], in_=cs.t[:], func=AF.Silu), [cs], [cs2])
        self.sc = []
        wst = [self.sb("adaw%d" % i, [128, 8, 512], F32) for i in range(2)]
        modt = self.sb("modt", [128, 48, 2], F32)
        for l in range(cfg.DEPTH):
            pbk = self.pb[6]
            wv = ada_w.t.ap()[l].rearrange("(k p) n -> p k n", p=128)
            for piece in range(12):
                st = wst[piece % 2]
                self.dma("sp" if piece % 2 == 0 else "act", st.t[:], wv[:, :, piece * 512:(piece + 1) * 512], writes=[st])
                for cch in range(4):
                    j = piece * 4 + cch
                    for k in range(8):
                        self.op("pe", lambda g, k=k, cch=cch, j=j, st=st: g.matmul(
                            pbk.t[:, 2 * j:2 * j + 2], st.t[:, k, cch * 128:(cch + 1) * 128], cs2.t[:, k, :],
                            start=(k == 0), stop=(k == 7)), [st, cs2], [pbk], pe_acc=True)
            pv = pbk.t[:, 0:96].rearrange("p (j c) -> p j c", c=2)
            for col in range(2):
                self.op("dve", lambda g, col=col: g.tensor_tensor(out=modt.t[:, :, col], in0=pv[:, :, col], in1=abt.t[:, l, :], op=ALU.add),
                        [pbk, abt], [modt])
            sc = scs[l]
            for col in range(2):
                self.op("dve", lambda g, col=col: g.scalar_tensor_tensor(out=sc.t[:, 0:8, col], in0=modt.t[:, 8:16, col], scalar=1.0,
                        in1=g1t.t[:, l, :], op0=ALU.add, op1=ALU.mult), [modt, g1t], [sc])
                self.op("dve", lambda g, col=col: g.tensor_copy(out=sc.t[:, 8:16, col], in_=modt.t[:, 0:8, col]), [modt], [sc])
                self.op("dve", lambda g, col=col: g.tensor_copy(out=sc.t[:, 16:24, col], in_=modt.t[:, 16:24, col]), [modt], [sc])
                self.op("dve", lambda g, col=col: g.scalar_tensor_tensor(out=sc.t[:, 24:32, col], in0=modt.t[:, 32:40, col], scalar=1.0,
                        in1=g2t.t[:, l, :], op0=ALU.add, op1=ALU.mult), [modt, g2t], [sc])
                self.op("dve", lambda g, col=col: g.tensor_copy(out=sc.t[:, 32:40, col], in_=modt.t[:, 24:32, col]), [modt], [sc])
                self.op("dve", lambda g, col=col: g.tensor_copy(out=sc.t[:, 40:48, col], in_=modt.t[:, 40:48, col]), [modt], [sc])
            self.sc.append(sc)
        self.end_phase()

    def rstd_of(self, xt, n, sq, rstd):
        cfg = self.cfg
        pbk = self.pb[7]
        for dc in range(8):
            self.op("act", lambda g, dc=dc: g.activation(out=sq.t[:, dc, :n], in_=xt.t[:, dc, :n], func=AF.Square), [xt], [sq])
        for dc in range(8):
            self.op("pe", lambda g, dc=dc: g.matmul(pbk.t[:, :n], self.ones.t[:, :], sq.t[:, dc, :n], start=(dc == 0), stop=(dc == 7)),
                    [self.ones, sq], [pbk], pe_acc=True)
        self.op("dve", lambda g: g.tensor_scalar(out=rstd.t[:, :n], in0=pbk.t[:, :n], scalar1=1.0 / cfg.D, scalar2=cfg.EPS,
                                                  op0=ALU.mult, op1=ALU.add), [pbk], [rstd])
        self.op("act", lambda g: g.activation(out=rstd.t[:, :n], in_=rstd.t[:, :n], func=AF.Sqrt), [rstd], [rstd])
        self.op("dve", lambda g: g.reciprocal(out=rstd.t[:, :n], in_=rstd.t[:, :n]), [rstd], [rstd])

    def norm_mod(self, xt, n, rstd, tmp, ht, sc, ia, ib, col):
        for dc in range(8):
            self.op("dve", lambda g, dc=dc: g.tensor_tensor(out=tmp.t[:, dc, :n], in0=xt.t[:, dc, :n], in1=rstd.t[:, :n], op=ALU.mult),
                    [xt, rstd], [tmp])
            if ib is None:
                self.op("act", lambda g, dc=dc: g.activation(out=ht.t[:, dc, :n], in_=tmp.t[:, dc, :n], func=AF.Copy,
                                                             scale=sc.t[:, ia + dc, col:col + 1]), [tmp, sc], [ht])
            else:
                self.op("act", lambda g, dc=dc: g.activation(out=ht.t[:, dc, :n], in_=tmp.t[:, dc, :n], func=AF.Identity,
                                                             scale=sc.t[:, ia + dc, col:col + 1], bias=sc.t[:, ib + dc, col:col + 1]),
                        [tmp, sc], [ht])

    def blocks(self, with_ctx):
        cfg = self.cfg
        bl = [(t0, cfg.TB, 0) for t0 in range(0, cfg.SEQ, cfg.TB)]
        if with_ctx:
            bl += [(cfg.SEQ + t0, min(cfg.TB, cfg.CTX - t0), 1) for t0 in range(0, cfg.CTX, cfg.TB)]
        return bl

    def setup_common(self):
        cfg = self.cfg
        TB = cfg.TB
        self.ones = self.sb("ones", [128, 128], BF16)
        self.op("pool", lambda g: g.memset(self.ones.t[:], 1.0), [], [self.ones])
        self.xT = self.inp("xT", [1024, cfg.NTOK])
        self.X = [self.dram("X%d" % i, [1024, cfg.NTOK], F32) for i in range(2)]
        hk = {"tok": "ExternalOutput", "mix": "ExternalInput"}.get(self.test, "Internal")
        mk = {"tok": "ExternalInput", "mix": "ExternalOutput"}.get(self.test, "Internal")
        self.H = self.dram("Hbuf", [1024, cfg.NTOK], BF16, kind=hk)
        self.M = self.dram("Mbuf", [1024, cfg.NTOK], BF16, kind=mk)
        self.OUT = self.dram("outT", [1024, cfg.SEQ], F32, kind="ExternalOutput")
        self.out_objs.append(self.OUT)
        fing = self.inp("fing", [128, 8, 1])
        self.fg = self.sb("fg", [128, 8, 1], F32)
        self.dma("sp", self.fg.t[:], fing.t.ap(), writes=[self.fg])
        identin = self.inp("ident", [128, 128])
        self.ident = self.sb("ident", [128, 128], F32)
        self.dma("sp", self.ident.t[:], identin.t.ap(), writes=[self.ident])
        self.ccols = {}
        self.make_consts([1.0, math.log(128.0 ** -0.5), -math.pi, 0.0])
        self.zero8 = self.sb("zero8", [128, 8], F32)
        self.op("pool", lambda g: g.memset(self.zero8.t[:], 0.0), [], [self.zero8])

    def phase_A0(self):
        self.begin_phase()
        self._phase_A0()
        self.end_phase()

    def _phase_A0(self):
        cfg = self.cfg
        TB = cfg.TB
        xts = [self.sb("a_x%d" % i, [128, 8, TB], F32) for i in range(2)]
        sq = self.sb("a_sq", [128, 8, TB], BF16)
        tmp = self.sb("a_tmp", [128, 8, TB], F32)
        rstd = self.sb("a_rstd", [128, TB], F32)
        hts = [self.sb("a_h%d" % i, [128, 8, TB], BF16) for i in range(2)]
        sc = self.sc[0]
        xv = self.xT.t.ap().rearrange("(k p) t -> p k t", p=128)
        hv = self.H.t.ap().rearrange("(k p) t -> p k t", p=128)
        for bi, (t0, n, col) in enumerate(self.blocks(True)):
            xt = xts[bi % 2]
            ht = hts[bi % 2]
            self.dma("sp", xt.t[:, :, :n], xv[:, :, t0:t0 + n], writes=[xt])
            self.rstd_of(xt, n, sq, rstd)
            self.norm_mod(xt, n, rstd, tmp, ht, sc, 0, 8, col)
            self.dma("act", hv[:, :, t0:t0 + n], ht.t[:, :, :n], reads=[ht], writes=[self.H], owner=self.H)

    def phase_C(self, l, wo_ap, bo_t, w1_ap, w2_ap, x_src, x_dst, ctx_live, next_ctx, last):
        self.begin_phase()
        cfg = self.cfg
        TB = cfg.TB
        stg = [self.sb("c_stg%d" % i, [128, 512], F32) for i in range(2)]
        wo = self.sb("c_wo", [128, 8, 1024], BF16)
        w1 = self.sb("c_w1", [128, 8, 4096], BF16)
        w2 = self.sb("c_w2", [128, 32, 1024], BF16)
        self.load_cast(wo_ap, 1024, 1024, wo, stg)
        self.load_cast(w1_ap, 1024, 4096, w1, stg)
        self.load_cast(w2_ap, 4096, 1024, w2, stg)
        xts = [self.sb("c_x%d" % i, [128, 8, TB], F32) for i in range(2)]
        mts = [self.sb("c_m%d" % i, [128, 8, TB], BF16) for i in range(1)]
        sq = self.sb("c_sq", [128, 8, TB], BF16)
        tmp = self.sb("c_tmp", [128, 8, TB], F32)
        rstd = self.sb("c_rstd", [128, TB], F32)
        h2 = self.sb("c_h2", [128, 8, TB], BF16)
        h1 = self.sb("c_h1", [128, 32, TB], BF16)
        rl = [self.sb("c_rl%d" % i, [128, TB], F32) for i in range(2)]
        hn = h2
        on = tmp
        sc = self.sc[l]
        xv = x_src.t.ap().rearrange("(k p) t -> p k t", p=128)
        xo = x_dst.t.ap().rearrange("(k p) t -> p k t", p=128)
        mv = self.M.t.ap().rearrange("(k p) t -> p k t", p=128)
        hv = self.H.t.ap().rearrange("(k p) t -> p k t", p=128)
        ov = self.OUT.t.ap().rearrange("(k p) t -> p k t", p=128)
        for bi, (t0, n, col) in enumerate(self.blocks(ctx_live)):
            xt = xts[bi % 2]
            mt = mts[0]
            self.dma("sp", xt.t[:, :, :n], xv[:, :, t0:t0 + n], reads=[x_src], writes=[xt])
            self.dma("act", mt.t[:, :, :n], mv[:, :, t0:t0 + n], reads=[self.M], writes=[mt])
            for dc in range(8):
                pbk = self.bank()
                for k in range(8):
                    self.op("pe", lambda g, k=k, dc=dc, pbk=pbk: g.matmul(pbk.t[:, :n], wo.t[:, k, dc * 128:(dc + 1) * 128], mt.t[:, k, :n],
                                                                         start=(k == 0), stop=(k == 7)), [wo, mt], [pbk], pe_acc=True)
                self.op("dve", lambda g, dc=dc, pbk=pbk: g.tensor_scalar(out=tmp.t[:, dc, :n], in0=pbk.t[:, :n], scalar1=bo_t.t[:, dc:dc + 1],
                        scalar2=sc.t[:, 16 + dc, col:col + 1], op0=ALU.add, op1=ALU.mult), [pbk, bo_t, sc], [tmp])
                self.op("pool", lambda g, dc=dc: g.tensor_tensor(out=xt.t[:, dc, :n], in0=xt.t[:, dc, :n], in1=tmp.t[:, dc, :n], op=ALU.add),
                        [xt, tmp], [xt])
            self.rstd_of(xt, n, sq, rstd)
            self.norm_mod(xt, n, rstd, tmp, h2, sc, 24, 32, col)
            for j in range(32):
                pbk = self.bank()
                for k in range(8):
                    self.op("pe", lambda g, k=k, j=j, pbk=pbk: g.matmul(pbk.t[:, :n], w1.t[:, k, j * 128:(j + 1) * 128], h2.t[:, k, :n],
                                                                        start=(k == 0), stop=(k == 7)), [w1, h2], [pbk], pe_acc=True)
                r = rl[j % 2]
                self.op("act", lambda g, pbk=pbk, r=r: g.activation(out=r.t[:, :n], in_=pbk.t[:, :n], func=AF.Relu), [pbk], [r])
                e = "dve" if j % 2 == 0 else "pool"
                self.op(e, lambda g, j=j, r=r: g.tensor_tensor(out=h1.t[:, j, :n], in0=r.t[:, :n], in1=r.t[:, :n], op=ALU.mult), [r], [h1])
            for dc in range(8):
                pbk = self.bank()
                for j in range(32):
                    self.op("pe", lambda g, j=j, dc=dc, pbk=pbk: g.matmul(pbk.t[:, :n], w2.t[:, j, dc * 128:(dc + 1) * 128], h1.t[:, j, :n],
                                                                         start=(j == 0), stop=(j == 31)), [w2, h1], [pbk], pe_acc=True)
                self.op("dve", lambda g, dc=dc, pbk=pbk: g.scalar_tensor_tensor(out=xt.t[:, dc, :n], in0=pbk.t[:, :n],
                        scalar=sc.t[:, 40 + dc, col:col + 1], in1=xt.t[:, dc, :n], op0=ALU.mult, op1=ALU.add), [pbk, sc, xt], [xt])
            if not last:
                self.dma("sp", xo[:, :, t0:t0 + n], xt.t[:, :, :n], reads=[xt], writes=[x_dst], owner=x_dst)
                if col == 0 or next_ctx:
                    self.rstd_of(xt, n, sq, rstd)
                    self.norm_mod(xt, n, rstd, tmp, hn, self.sc[l + 1], 0, 8, col)
                    self.dma("act", hv[:, :, t0:t0 + n], hn.t[:, :, :n], reads=[hn], writes=[self.H], owner=self.H)
            elif col == 0:
                self.rstd_of(xt, n, sq, rstd)
                for dc in range(8):
                    self.op("dve", lambda g, dc=dc: g.tensor_tensor(out=tmp.t[:, dc, :n], in0=xt.t[:, dc, :n], in1=rstd.t[:, :n], op=ALU.mult),
                            [xt, rstd], [tmp])
                    self.op("act", lambda g, dc=dc: g.activation(out=on.t[:, dc, :n], in_=tmp.t[:, dc, :n], func=AF.Copy,
                                                                 scale=self.fg.t[:, dc, 0:1]), [tmp, self.fg], [on])
                self.dma("sp", ov[:, :, t0:t0 + n], on.t[:, :, :n], reads=[on], writes=[self.OUT], owner=self.OUT)
        self.end_phase()

    def proj(self, pbk, w, c0, msz, act, kcs, asl, n):
        for i, k in enumerate(kcs):
            self.op("pe", lambda g, k=k, i=i: g.matmul(pbk.t[:msz, :n], w.t[:, k, c0:c0 + msz], act.t[:, k, asl],
                                                       start=(i == 0), stop=(i == len(kcs) - 1)), [w, act], [pbk], pe_acc=True)

    def rstd_small(self, src, nch, n, sq, rstd, dim):
        pbk = self.pb[7]
        for c in range(nch):
            self.op("act", lambda g, c=c: g.activation(out=sq.t[:, c, :n], in_=src.t[:, c, :n], func=AF.Square), [src], [sq])
        for c in range(nch):
            self.op("pe", lambda g, c=c: g.matmul(pbk.t[:, :n], self.ones.t[:, :], sq.t[:, c, :n], start=(c == 0), stop=(c == nch - 1)),
                    [self.ones, sq], [pbk], pe_acc=True)
        self.op("dve", lambda g: g.tensor_scalar(out=rstd.t[:, :n], in0=pbk.t[:, :n], scalar1=1.0 / dim, scalar2=self.cfg.EPS,
                                                  op0=ALU.mult, op1=ALU.add), [pbk], [rstd])
        self.op("act", lambda g: g.activation(out=rstd.t[:, :n], in_=rstd.t[:, :n], func=AF.Sqrt), [rstd], [rstd])
        self.op("dve", lambda g: g.reciprocal(out=rstd.t[:, :n], in_=rstd.t[:, :n]), [rstd], [rstd])

    def make_rope_perm(self, wp, w, nk, c0):
        for grp, src, sign in ((0, 1, -1.0), (1, 0, 1.0), (2, 3, -1.0), (3, 2, 1.0)):
            self.op("act", lambda g, grp=grp, src=src, sign=sign: g.activation(
                out=wp.t[:, 0:nk, grp * 16:(grp + 1) * 16], in_=w.t[:, 0:nk, c0 + src * 16:c0 + (src + 1) * 16],
                func=AF.Copy, scale=sign), [w], [wp])

    def phase_mla(self, w_down, qnorm, w_uq, kvnorm, w_ukv, cos_t, sin_t):
        cfg = self.cfg
        SEQ, NTOK = cfg.SEQ, cfg.NTOK
        NH = 8
        QB = 512
        self.begin_phase()
        wd = self.sb("m_wd", [128, 8, 704], BF16)
        wdp = self.sb("m_wdp", [128, 8, 64], BF16)
        wuq = self.sb("m_wuq", [128, 3, 1536], BF16)
        wuqp = self.sb("m_wuqp", [128, 3, 8, 64], BF16)
        wukv = self.sb("m_wukv", [128, 2, 2048], BF16)
        qn_g = self.sb("m_qng", [128, 3], F32)
        kvn_g = self.sb("m_kvng", [128, 2], F32)
        cqn_d = self.dram("m_cqn_d", [128, 3, SEQ], BF16)
        ckvn = self.sb("m_ckvn", [128, 2, NTOK], BF16)
        krT = self.sb("m_krT", [64, NTOK], BF16)
        self.begin_phase()
        stg = [self.sb("m_stg%d" % i, [128, 1024], F32) for i in range(2)]
        self.dma("sp", qn_g.t[:], qnorm.t.ap(), writes=[qn_g])
        self.dma("sp", kvn_g.t[:], kvnorm.t.ap(), writes=[kvn_g])
        wv = w_down.t.ap().rearrange("(k p) n -> p k n", p=128)
        for k in range(8):
            st = stg[k % 2]
            self.dma("sp" if k % 2 == 0 else "act", st.t[:, 0:704], wv[:, k, :], writes=[st])
            self.copy(ENG3[k % 3], wd, wd.t[:, k, :], st, st.t[:, 0:704])
        self.load_cast(w_uq.t.ap(), 384, 1536, wuq, [self.sb("m_stg2%d" % i, [128, 1536], F32) for i in range(2)])
        self.load_cast(w_ukv.t.ap(), 256, 2048, wukv, [self.sb("m_stg3%d" % i, [128, 2048], F32) for i in range(2)])
        self.make_rope_perm(wdp, wd, 8, 640)
        for hd in range(NH):
            for grp, src, sign in ((0, 1, -1.0), (1, 0, 1.0), (2, 3, -1.0), (3, 2, 1.0)):
                c0 = hd * 192 + 128
                self.op("act", lambda g, grp=grp, src=src, sign=sign, c0=c0, hd=hd: g.activation(
                    out=wuqp.t[:, :, hd, grp * 16:(grp + 1) * 16], in_=wuq.t[:, :, c0 + src * 16:c0 + (src + 1) * 16],
                    func=AF.Copy, scale=sign), [wuq], [wuqp])
        self.end_phase()
        self.begin_phase()
        hts = [self.sb("m_h%d" % i, [128, 8, QB], BF16) for i in range(2)]
        craw = self.sb("m_craw", [128, 5, QB], F32)
        sq = self.sb("m_sq", [128, 3, QB], BF16)
        rstd = self.sb("m_rstd", [128, QB], F32)
        tmp = self.sb("m_tmp", [128, 3, QB], F32)
        cqb = [self.sb("m_cqb%d" % i, [128, 3, QB], BF16) for i in range(2)]
        cs_t = [self.sb("m_cos%d" % i, [64, QB], F32) for i in range(2)]
        sn_t = [self.sb("m_sin%d" % i, [64, QB], F32) for i in range(2)]
        kr_a = self.sb("m_kra", [64, QB], F32)
        kr_b = self.sb("m_krb", [64, QB], F32)
        hv = self.H.t.ap().rearrange("(k p) t -> p k t", p=128)
        tblocks = [(t0, min(QB, NTOK - t0)) for t0 in range(0, NTOK, QB)]
        for bi, (t0, n) in enumerate(tblocks):
            ht = hts[bi % 2]
            self.dma("sp", ht.t[:, :, :n], hv[:, :, t0:t0 + n], reads=[self.H], writes=[ht])
            cs, sn = cs_t[bi % 2], sn_t[bi % 2]
            self.dma("act", cs.t[:, :n], cos_t.t.ap()[:, t0:t0 + n], writes=[cs])
            self.dma("act", sn.t[:, :n], sin_t.t.ap()[:, t0:t0 + n], writes=[sn])
            for oc in range(5):
                pbk = self.bank()
                self.proj(pbk, wd, oc * 128, 128, ht, range(8), slice(0, n), n)
                self.op("act", lambda g, oc=oc, pbk=pbk: g.activation(out=craw.t[:, oc, :n], in_=pbk.t[:, :n], func=AF.Copy), [pbk], [craw])
            pbk = self.bank()
            self.proj(pbk, wd, 640, 64, ht, range(8), slice(0, n), n)
            pb2 = self.bank()
            for i, k in enumerate(range(8)):
                self.op("pe", lambda g, k=k, i=i: g.matmul(pb2.t[:64, :n], wdp.t[:, k, :], ht.t[:, k, 0:n], start=(i == 0), stop=(i == 7)),
                        [wdp, ht], [pb2], pe_acc=True)
            self.op("dve", lambda g: g.tensor_tensor(out=kr_a.t[:, :n], in0=pbk.t[:64, :n], in1=cs.t[:, :n], op=ALU.mult), [pbk, cs], [kr_a])
            self.op("dve", lambda g: g.tensor_tensor(out=kr_b.t[:, :n], in0=pb2.t[:64, :n], in1=sn.t[:, :n], op=ALU.mult), [pb2, sn], [kr_b])
            self.op("pool", lambda g: g.tensor_tensor(out=krT.t[:, t0:t0 + n], in0=kr_a.t[:, :n], in1=kr_b.t[:, :n], op=ALU.add), [kr_a, kr_b], [krT])
            if t0 < SEQ:
                self.rstd_small(craw, 3, n, sq, rstd, 384.0)
                for c in range(3):
                    self.op("dve", lambda g, c=c: g.tensor_tensor(out=tmp.t[:, c, :n], in0=craw.t[:, c, :n], in1=rstd.t[:, :n], op=ALU.mult), [craw, rstd], [tmp])
                    self.op("act", lambda g, c=c: g.activation(out=cqb[bi % 2].t[:, c, :n], in_=tmp.t[:, c, :n], func=AF.Copy, scale=qn_g.t[:, c:c + 1]), [tmp, qn_g], [cqb[bi % 2]])
                self.dma("sp", cqn_d.t.ap()[:, :, t0:t0 + n], cqb[bi % 2].t[:, :, :n], reads=[cqb[bi % 2]], writes=[cqn_d], owner=cqn_d)
            pb7 = self.pb[7]
            for c in range(2):
                self.op("act", lambda g, c=c: g.activation(out=sq.t[:, c, :n], in_=craw.t[:, 3 + c, :n], func=AF.Square), [craw], [sq])
            for c in range(2):
                self.op("pe", lambda g, c=c: g.matmul(pb7.t[:, :n], self.ones.t[:, :], sq.t[:, c, :n], start=(c == 0), stop=(c == 1)), [self.ones, sq], [pb7], pe_acc=True)
            self.op("dve", lambda g: g.tensor_scalar(out=rstd.t[:, :n], in0=pb7.t[:, :n], scalar1=1.0 / 256.0, scalar2=cfg.EPS, op0=ALU.mult, op1=ALU.add), [pb7], [rstd])
            self.op("act", lambda g: g.activation(out=rstd.t[:, :n], in_=rstd.t[:, :n], func=AF.Sqrt), [rstd], [rstd])
            self.op("dve", lambda g: g.reciprocal(out=rstd.t[:, :n], in_=rstd.t[:, :n]), [rstd], [rstd])
            for c in range(2):
                self.op("dve", lambda g, c=c: g.tensor_tensor(out=tmp.t[:, c, :n], in0=craw.t[:, 3 + c, :n], in1=rstd.t[:, :n], op=ALU.mult), [craw, rstd], [tmp])
                self.op("act", lambda g, c=c: g.activation(out=ckvn.t[:, c, t0:t0 + n], in_=tmp.t[:, c, :n], func=AF.Copy, scale=kvn_g.t[:, c:c + 1]), [tmp, kvn_g], [ckvn])
        self.end_phase()
        NKB = NTOK // 128
        cql = [self.sb("m_cql%d" % i, [128, 3, QB], BF16) for i in range(2)]
        qn = self.sb("m_qn", [128, SEQ], BF16)
        qr = self.sb("m_qr", [64, SEQ], BF16)
        kn = self.sb("m_kn", [128, NTOK], BF16)
        vt = self.sb("m_v", [128, NKB, 128], BF16)
        sqh = self.sb("m_sqh", [128, 2, QB], BF16)
        cs_t = [self.sb("m_cos%d" % i, [64, QB], F32) for i in range(2)]
        sn_t = [self.sb("m_sin%d" % i, [64, QB], F32) for i in range(2)]
        ra = self.sb("m_ra", [64, QB], F32)
        rb = self.sb("m_rb", [64, QB], F32)
        bmax = self.sb("m_bmax", [128, 1], F32)
        qmax = self.sb("m_qmax", [128, 1], F32)
        kmax = self.sb("m_kmax", [128, 1], F32)
        negm = self.sb("m_negm", [128, 1], F32)
        pts = [self.sb("m_pt%d" % i, [128, QB], BF16) for i in range(4)]
        rec = self.sb("m_rec", [128, QB], F32)
        ao = [self.sb("m_ao%d" % i, [128, QB], BF16) for i in range(2)]
        scale = (128 + 64) ** -0.5
        sb_s = [self.pb[0], self.pb[1], self.pb[2]]
        sb_o = [self.pb[3], self.pb[4]]
        sb_l = [self.pb[5], self.pb[6]]
        pb7 = self.pb[7]
        qblocks = [(t0, min(QB, SEQ - t0)) for t0 in range(0, SEQ, QB)]
        for hd in range(NH):
            self.op("pool", lambda g: g.memset(qmax.t[:], 0.0), [], [qmax])
            self.op("pool", lambda g: g.memset(kmax.t[:], 0.0), [], [kmax])
            for bi, (t0, n) in enumerate(qblocks):
                cs, sn = cs_t[bi % 2], sn_t[bi % 2]
                self.dma("act", cs.t[:, :n], cos_t.t.ap()[:, t0:t0 + n], writes=[cs])
                self.dma("act", sn.t[:, :n], sin_t.t.ap()[:, t0:t0 + n], writes=[sn])
                cqn = cql[bi % 2]
                self.dma("sp", cqn.t[:, :, :n], cqn_d.t.ap()[:, :, t0:t0 + n], reads=[cqn_d], writes=[cqn])
                pbk = self.bank()
                self.proj(pbk, wuq, hd * 192, 128, cqn, range(3), slice(0, n), n)
                self.op("act", lambda g, pbk=pbk: g.activation(out=qn.t[:, t0:t0 + n], in_=pbk.t[:, :n], func=AF.Copy), [pbk], [qn])
                pbk = self.bank()
                self.proj(pbk, wuq, hd * 192 + 128, 64, cqn, range(3), slice(0, n), n)
                pb2 = self.bank()
                for k in range(3):
                    self.op("pe", lambda g, k=k: g.matmul(pb2.t[:64, :n], wuqp.t[:, k, hd, :], cqn.t[:, k, 0:n], start=(k == 0), stop=(k == 2)),
                            [wuqp, cqn], [pb2], pe_acc=True)
                self.op("dve", lambda g, pbk=pbk: g.tensor_tensor(out=ra.t[:, :n], in0=pbk.t[:64, :n], in1=cs.t[:, :n], op=ALU.mult), [pbk, cs], [ra])
                self.op("dve", lambda g: g.tensor_tensor(out=rb.t[:, :n], in0=pb2.t[:64, :n], in1=sn.t[:, :n], op=ALU.mult), [pb2, sn], [rb])
                self.op("pool", lambda g: g.tensor_tensor(out=qr.t[:, t0:t0 + n], in0=ra.t[:, :n], in1=rb.t[:, :n], op=ALU.add), [ra, rb], [qr])
                self.op("act", lambda g: g.activation(out=sqh.t[:, 0, :n], in_=qn.t[:, t0:t0 + n], func=AF.Square), [qn], [sqh])
                self.op("act", lambda g: g.activation(out=sqh.t[:64, 1, :n], in_=qr.t[:, t0:t0 + n], func=AF.Square), [qr], [sqh])
                self.op("pe", lambda g: g.matmul(pb7.t[:, :n], self.ones.t[:, :], sqh.t[:, 0, :n], start=True, stop=False), [self.ones, sqh], [pb7], pe_acc=True)
                self.op("pe", lambda g: g.matmul(pb7.t[:, :n], self.ones.t[:64, :], sqh.t[:64, 1, :n], start=False, stop=True), [self.ones, sqh], [pb7], pe_acc=True)
                self.op("dve", lambda g: g.tensor_reduce(out=bmax.t[:], in_=pb7.t[:, :n], axis=AX.X, op=ALU.max), [pb7], [bmax])
                self.op("dve", lambda g: g.tensor_tensor(out=qmax.t[:], in0=qmax.t[:], in1=bmax.t[:], op=ALU.max), [qmax, bmax], [qmax])
            for bi, (t0, n) in enumerate(tblocks):
                pbk = self.bank()
                self.proj(pbk, wukv, hd * 256, 128, ckvn, range(2), slice(t0, t0 + n), n)
                self.op("act", lambda g, pbk=pbk: g.activation(out=kn.t[:, t0:t0 + n], in_=pbk.t[:, :n], func=AF.Copy), [pbk], [kn])
                self.op("act", lambda g: g.activation(out=sqh.t[:, 0, :n], in_=kn.t[:, t0:t0 + n], func=AF.Square), [kn], [sqh])
                self.op("act", lambda g: g.activation(out=sqh.t[:64, 1, :n], in_=krT.t[:, t0:t0 + n], func=AF.Square), [krT], [sqh])
                self.op("pe", lambda g: g.matmul(pb7.t[:, :n], self.ones.t[:, :], sqh.t[:, 0, :n], start=True, stop=False), [self.ones, sqh], [pb7], pe_acc=True)
                self.op("pe", lambda g: g.matmul(pb7.t[:, :n], self.ones.t[:64, :], sqh.t[:64, 1, :n], start=False, stop=True), [self.ones, sqh], [pb7], pe_acc=True)
                self.op("dve", lambda g: g.tensor_reduce(out=bmax.t[:], in_=pb7.t[:, :n], axis=AX.X, op=ALU.max), [pb7], [bmax])
                self.op("dve", lambda g: g.tensor_tensor(out=kmax.t[:], in0=kmax.t[:], in1=bmax.t[:], op=ALU.max), [kmax, bmax], [kmax])
                pbk = self.bank()
                nkb = n // 128
                for j in range(nkb):
                    kb = t0 // 128 + j
                    for k in range(2):
                        self.op("pe", lambda g, k=k, j=j, kb=kb: g.matmul(pbk.t[:, j * 128:(j + 1) * 128], ckvn.t[:, k, kb * 128:(kb + 1) * 128],
                                wukv.t[:, k, hd * 256 + 128:hd * 256 + 256], start=(k == 0), stop=(k == 1)), [ckvn, wukv], [pbk], pe_acc=True)
                self.op("dve", lambda g, pbk=pbk, nkb=nkb: g.tensor_copy(out=vt.t[:, t0 // 128:t0 // 128 + nkb, :],
                        in_=pbk.t[:, :nkb * 128].rearrange("p (j d) -> p j d", d=128)), [pbk], [vt])
            self.op("dve", lambda g: g.tensor_tensor(out=negm.t[:], in0=qmax.t[:], in1=kmax.t[:], op=ALU.mult), [qmax, kmax], [negm])
            self.op("act", lambda g: g.activation(out=negm.t[:], in_=negm.t[:], func=AF.Sqrt), [negm], [negm])
            self.op("dve", lambda g: g.tensor_scalar(out=negm.t[:], in0=negm.t[:], scalar1=-1.02 * scale, scalar2=None, op0=ALU.mult), [negm], [negm])
            mv = self.M.t.ap()
            it = 0
            for qi, (q0, nq) in enumerate(qblocks):
                po = sb_o[qi % 2]
                pl = sb_l[qi % 2]
                for kb in range(NKB):
                    psb = sb_s[it % 3]
                    pt = pts[it % 4]
                    it += 1
                    self.op("pe", lambda g, kb=kb, psb=psb: g.matmul(psb.t[:, :nq], kn.t[:, kb * 128:(kb + 1) * 128], qn.t[:, q0:q0 + nq], start=True, stop=False),
                            [kn, qn], [psb], pe_acc=False)
                    self.op("pe", lambda g, kb=kb, psb=psb: g.matmul(psb.t[:, :nq], krT.t[:, kb * 128:(kb + 1) * 128], qr.t[:, q0:q0 + nq], start=False, stop=True),
                            [krT, qr], [psb], pe_acc=True)
                    self.op("act", lambda g, psb=psb, pt=pt: g.activation(out=pt.t[:, :nq], in_=psb.t[:, :nq], func=AF.Exp, scale=scale, bias=negm.t[:, 0:1]),
                            [psb, negm], [pt])
                    self.op("pe", lambda g, kb=kb, pt=pt: g.matmul(po.t[:, :nq], vt.t[:, kb, :], pt.t[:, :nq], start=(kb == 0), stop=(kb == NKB - 1)),
                            [vt, pt], [po], pe_acc=(kb > 0))
                    self.op("pe", lambda g, kb=kb, pt=pt: g.matmul(pl.t[:, :nq], self.ones.t[:, :], pt.t[:, :nq], start=(kb == 0), stop=(kb == NKB - 1)),
                            [self.ones, pt], [pl], pe_acc=(kb > 0))
                self.op("dve", lambda g: g.reciprocal(out=rec.t[:, :nq], in_=pl.t[:, :nq]), [pl], [rec])
                a = ao[qi % 2]
                self.op("dve", lambda g, a=a: g.tensor_tensor(out=a.t[:, :nq], in0=po.t[:, :nq], in1=rec.t[:, :nq], op=ALU.mult), [po, rec], [a])
                self.dma("sp", mv[hd * 128:(hd + 1) * 128, q0:q0 + nq], a.t[:, :nq], reads=[a], writes=[self.M], owner=self.M)
        self.end_phase()


def rope_tables(SEQ, NTOK):
    t = np.arange(SEQ)
    row = (t // 64).astype(np.float32)
    colp = (t % 64).astype(np.float32)
    inv_freq = (10000.0 ** (-np.arange(0, 32, 2, dtype=np.float32) / 32)).astype(np.float32)
    ar = row[None, :] * inv_freq[:, None]
    ac = colp[None, :] * inv_freq[:, None]
    ang = np.concatenate([ar, ar, ac, ac], axis=0)
    cos = np.ones((64, NTOK), np.float32)
    sin = np.zeros((64, NTOK), np.float32)
    cos[:, :SEQ] = np.cos(ang)
    sin[:, :SEQ] = np.sin(ang)
    return cos, sin


def _gla_phase(self, w_in, gk_w2, gk_b, onorm, masks, segmask_in):
    cfg = self.cfg
    SEQ, NTOK = cfg.SEQ, cfg.NTOK
    NCH = NTOK // 128
    self.begin_phase()
    stg = [self.sb("g_stg%d" % i, [128, 3104], F32) for i in range(2)]
    win = self.sb("g_win", [128, 8, 3104], BF16)
    wv = w_in.t.ap().rearrange("(k p) n -> p k n", p=128)
    for k in range(8):
        st = stg[k % 2]
        self.dma("sp" if k % 2 == 0 else "act", st.t[:, :], wv[:, k, :], writes=[st])
        self.copy(ENG3[k % 3], win, win.t[:, k, :], st, st.t[:, :])
    w2s = self.sb("g_w2s", [16, 2, 512], F32)
    w2 = self.sb("g_w2", [16, 2, 512], BF16)
    self.dma("sp", w2s.t[:], gk_w2.t.ap().rearrange("z r d -> r z d"), writes=[w2s])
    self.copy("dve", w2, w2.t[:], w2s, w2s.t[:])
    gb = self.sb("g_gb", [128, 2, 4], F32)
    self.dma("sp", gb.t[:], gk_b.t.ap(), writes=[gb])
    ngb = self.sb("g_ngb", [128, 2, 4], F32)
    self.op("dve", lambda g: g.tensor_scalar(out=ngb.t[:], in0=gb.t[:], scalar1=-1.0, scalar2=None, op0=ALU.mult), [gb], [ngb])
    ong = self.sb("g_ong", [128, 2], F32)
    self.dma("sp", ong.t[:], onorm.t.ap(), writes=[ong])
    msk = self.sb("g_msk", [128, 2, 4, 128], F32)
    self.dma("sp", msk.t[:], masks.t.ap(), writes=[msk])
    seg = self.sb("g_seg", [128, 512], F32)
    self.dma("sp", seg.t[:], segmask_in.t.ap(), writes=[seg])
    ident = self.ident
    S = self.sb("g_S", [128, 4, 256], F32)
    Sbf = [self.sb("g_Sbf%d" % i, [128, 4, 256], BF16) for i in range(2)]
    OF = self.dram("g_OF", [128, 8, NTOK], F32)

    def T2(name, shape, dt):
        return [self.sb("g_%s%d" % (name, i), shape, dt) for i in range(2)]
    hT = T2("hT", [128, 8, 128], BF16)
    r_sb = T2("r", [16, 128], BF16)
    e1 = T2("e1", [128, 4, 128], F32)
    ls = T2("ls", [128, 4, 128], F32)
    cs = T2("cs", [128, 4, 128], F32)
    tmp = T2("tmp", [128, 4, 128], F32)
    Ep = T2("Ep", [128, 4, 128], F32)
    Em = T2("Em", [128, 4, 128], F32)
    dec = T2("dec", [128, 4], F32)
    qt = T2("qt", [128, 4, 128], BF16)
    kt32 = T2("kt32", [128, 4, 128], F32)
    ktb = T2("ktb", [128, 4, 128], BF16)
    attm = T2("attm", [128, 4, 128], BF16)
    vsb = T2("v", [128, 1024], BF16)
    ktok = T2("ktok", [128, 4, 128], BF16)
    osb = T2("o", [128, 8, 128], F32)
    ofin = T2("ofin", [128, 8, 128], F32)
    sq = T2("sq", [128, 8, 128], BF16)
    rstd = T2("rstd", [128, 4, 128], F32)
    ogs = T2("ogs", [128, 8, 128], F32)
    fin = T2("fin", [128, 8, 128], BF16)
    ssum = T2("ssum", [128, 4, 256], F32)
    hv = self.H.t.ap().rearrange("(k p) t -> p k t", p=128)
    mv = self.M.t.ap().rearrange("(k p) t -> p k t", p=128)
    qscale_ln = math.log(128.0 ** -0.5)
    lat = list(range(0, SEQ // 128))
    ctxc = list(range(SEQ // 128, NCH))
    orders = [ctxc + lat, list(range(NCH - 1, -1, -1))]
    pbs = self.pb
    it = 0
    CUT = int(os.environ.get("GLA_CUT", "0"))
    for d in range(2):
        if CUT in (5, 6) and d == 1:
            break
        self.op("pool", lambda g: g.memset(S.t[:], 0.0), [], [S])
        self.op("pool", lambda g: g.memset(Sbf[it % 2].t[:], 0.0), [], [Sbf[it % 2]])
        for c in orders[d]:
            i = it % 2
            t0 = c * 128
            h_t = hT[i]
            self.dma("sp", h_t.t[:], hv[:, :, t0:t0 + 128], reads=[self.H], writes=[h_t])
            bq, bk, br, bg, bv0, bv1 = pbs[0], pbs[1], pbs[2], pbs[3], pbs[4], pbs[5]
            for hd in range(4):
                for k in range(8):
                    self.op("pe", lambda g: g.matmul(bq.t[:, hd * 128:(hd + 1) * 128], win.t[:, k, hd * 128:(hd + 1) * 128], h_t.t[:, k, :],
                                                      start=(k == 0), stop=(k == 7)), [win, h_t], [bq], pe_acc=True)
            for hd in range(4):
                for k in range(8):
                    self.op("pe", lambda g: g.matmul(bk.t[:, hd * 128:(hd + 1) * 128], win.t[:, k, 512 + hd * 128:512 + (hd + 1) * 128], h_t.t[:, k, :],
                                                      start=(k == 0), stop=(k == 7)), [win, h_t], [bk], pe_acc=True)
            for k in range(8):
                self.op("pe", lambda g: g.matmul(br.t[:16, :128], win.t[:, k, 3072 + d * 16:3072 + (d + 1) * 16], h_t.t[:, k, :],
                                                  start=(k == 0), stop=(k == 7)), [win, h_t], [br], pe_acc=True)
            self.op("act", lambda g: g.activation(out=r_sb[i].t[:, :], in_=br.t[:16, :128], func=AF.Copy), [br], [r_sb[i]])
            for hd in range(4):
                self.op("pe", lambda g: g.matmul(bg.t[:, hd * 128:(hd + 1) * 128], w2.t[:, d, hd * 128:(hd + 1) * 128], r_sb[i].t[:, :],
                                                  start=True, stop=True), [w2, r_sb[i]], [bg], pe_acc=True)
            for half in range(2):
                bv = (bv0, bv1)[half]
                for k in range(8):
                    self.op("pe", lambda g: g.matmul(bv.t[:, :], h_t.t[:, k, :], win.t[:, k, 1024 + half * 512:1024 + (half + 1) * 512],
                                                      start=(k == 0), stop=(k == 7)), [win, h_t], [bv], pe_acc=True)
                self.op("act", lambda g: g.activation(out=vsb[i].t[:, half * 512:(half + 1) * 512], in_=bv.t[:, :], func=AF.Copy), [bv], [vsb[i]])
            if CUT == 1:
                self.end_phase(); return
            for hd in range(4):
                self.op("act", lambda g: g.activation(out=e1[i].t[:, hd, :], in_=bg.t[:, hd * 128:(hd + 1) * 128], func=AF.Exp, scale=-1.0,
                                                      bias=ngb.t[:, d, hd:hd + 1]), [bg, ngb], [e1[i]])
            one_c = self.ccol(1.0)
            self.op("act", lambda g: g.activation(out=ls[i].t[:], in_=e1[i].t[:], func=AF.Ln, bias=one_c.t[:, 0:1]), [e1[i], one_c], [ls[i]])
            lsf = ls[i].t[:].rearrange("p h t -> p (h t)")
            csf = cs[i].t[:].rearrange("p h t -> p (h t)")
            self.op("dve", lambda g: g.tensor_tensor_scan(out=csf, data0=seg.t[:, :], data1=lsf, initial=0.0, op0=ALU.mult, op1=ALU.add),
                    [seg, ls[i]], [cs[i]])
            if d == 0:
                cum = cs[i]
                last = 127
            else:
                self.op("dve", lambda g: g.tensor_tensor(out=tmp[i].t[:], in0=ls[i].t[:], in1=cs[i].t[:], op=ALU.subtract), [ls[i], cs[i]], [tmp[i]])
                for hd in range(4):
                    self.op("dve", lambda g: g.tensor_scalar(out=tmp[i].t[:, hd, :], in0=tmp[i].t[:, hd, :], scalar1=cs[i].t[:, hd, 127:128],
                                                             scalar2=None, op0=ALU.add), [tmp[i], cs[i]], [tmp[i]])
                cum = tmp[i]
                last = 0
            qs_c = self.ccol(qscale_ln)
            self.op("act", lambda g: g.activation(out=Ep[i].t[:], in_=cum.t[:], func=AF.Exp, scale=-1.0 / 16, bias=qs_c.t[:, 0:1]), [cum, qs_c], [Ep[i]])
            self.op("act", lambda g: g.activation(out=Em[i].t[:], in_=cum.t[:], func=AF.Exp, scale=1.0 / 16), [cum], [Em[i]])
            self.op("act", lambda g: g.activation(out=dec[i].t[:, :], in_=cum.t[:, :, last], func=AF.Exp, scale=-1.0 / 16), [cum], [dec[i]])
            if CUT == 2:
                self.end_phase(); return
            bqv = bq.t[:, :].rearrange("p (h t) -> p h t", h=4)
            bkv = bk.t[:, :].rearrange("p (h t) -> p h t", h=4)
            self.op("dve", lambda g: g.tensor_tensor(out=qt[i].t[:], in0=bqv, in1=Ep[i].t[:], op=ALU.mult), [bq, Ep[i]], [qt[i]])
            self.op("dve", lambda g: g.tensor_tensor(out=kt32[i].t[:], in0=bkv, in1=Em[i].t[:], op=ALU.mult), [bk, Em[i]], [kt32[i]])
            self.op("act", lambda g: g.activation(out=ktb[i].t[:], in_=kt32[i].t[:], func=AF.Copy), [kt32[i]], [ktb[i]])
            ba, bt = pbs[6], pbs[7]
            for hd in range(4):
                self.op("pe", lambda g: g.matmul(ba.t[:, hd * 128:(hd + 1) * 128], ktb[i].t[:, hd, :], qt[i].t[:, hd, :], start=True, stop=True),
                        [ktb[i], qt[i]], [ba], pe_acc=True)
            for hd in range(4):
                self.op("pe", lambda g: g.transpose(out=bt.t[:, hd * 128:(hd + 1) * 128], in_=kt32[i].t[:, hd, :], identity=ident.t[:, :]),
                        [kt32[i], ident], [bt], pe_acc=True)
            bav = ba.t[:, :].rearrange("p (h t) -> p h t", h=4)
            self.op("dve", lambda g: g.tensor_tensor(out=attm[i].t[:], in0=bav, in1=msk.t[:, d, :, :], op=ALU.mult), [ba, msk], [attm[i]])
            self.op("act", lambda g: g.activation(out=ktok[i].t[:].rearrange("p h t -> p (h t)"), in_=bt.t[:, :], func=AF.Copy), [bt], [ktok[i]])
            if CUT == 3:
                self.end_phase(); return
            sb_cur = Sbf[it % 2]
            sb_nxt = Sbf[(it + 1) % 2]
            bo = (pbs[0], pbs[1])
            for hd in range(4):
                for j in range(2):
                    b_o = bo[hd // 2]
                    col = ((hd % 2) * 2 + j) * 128
                    self.op("pe", lambda g: g.matmul(b_o.t[:, col:col + 128], vsb[i].t[:, hd * 256 + j * 128:hd * 256 + (j + 1) * 128], attm[i].t[:, hd, :],
                                                      start=True, stop=False), [vsb[i], attm[i]], [b_o], pe_acc=True)
                    self.op("pe", lambda g: g.matmul(b_o.t[:, col:col + 128], sb_cur.t[:, hd, j * 128:(j + 1) * 128], qt[i].t[:, hd, :],
                                                      start=False, stop=True), [sb_cur, qt[i]], [b_o], pe_acc=True)
            for half in range(2):
                self.op("act" if half == 0 else "dve", (lambda g: g.activation(out=osb[i].t[:, half * 4:(half + 1) * 4, :].rearrange("p j t -> p (j t)"),
                        in_=bo[half].t[:, :], func=AF.Copy)) if half == 0 else
                        (lambda g: g.tensor_copy(out=osb[i].t[:, half * 4:(half + 1) * 4, :].rearrange("p j t -> p (j t)"), in_=bo[half].t[:, :])),
                        [bo[half]], [osb[i]])
            bs = (pbs[2], pbs[3])
            for hd in range(4):
                b_s = bs[hd // 2]
                col = (hd % 2) * 256
                self.op("pe", lambda g: g.matmul(b_s.t[:, col:col + 256], ktok[i].t[:, hd, :], vsb[i].t[:, hd * 256:(hd + 1) * 256], start=True, stop=True),
                        [ktok[i], vsb[i]], [b_s], pe_acc=True)
            for half in range(2):
                self.op("dve", lambda g: g.tensor_tensor(out=ssum[i].t[:, half * 2:(half + 1) * 2, :].rearrange("p h v -> p (h v)"),
                        in0=bs[half].t[:, :], in1=S.t[:, half * 2:(half + 1) * 2, :].rearrange("p h v -> p (h v)"), op=ALU.add), [bs[half], S], [ssum[i]])
            for hd in range(4):
                self.op("dve", lambda g: g.tensor_scalar(out=S.t[:, hd, :], in0=ssum[i].t[:, hd, :], scalar1=dec[i].t[:, hd:hd + 1], scalar2=None,
                                                         op0=ALU.mult), [ssum[i], dec[i]], [S])
            self.op("act", lambda g: g.activation(out=sb_nxt.t[:], in_=S.t[:], func=AF.Copy), [S], [sb_nxt])
            if CUT == 4:
                self.end_phase(); return
            ofv = OF.t.ap()
            if d == 0:
                if CUT != 6:
                    self.dma("act", ofv[:, :, t0:t0 + 128], osb[i].t[:], reads=[osb[i]], writes=[OF], owner=OF)
                if CUT == 7:
                    self.end_phase(); return
            else:
                self.dma("act", ofin[i].t[:], ofv[:, :, t0:t0 + 128], reads=[OF], writes=[ofin[i]])
                self.op("pool", lambda g: g.tensor_tensor(out=osb[i].t[:], in0=osb[i].t[:], in1=ofin[i].t[:], op=ALU.add), [osb[i], ofin[i]], [osb[i]])
                self.op("act", lambda g: g.activation(out=sq[i].t[:], in_=osb[i].t[:], func=AF.Square), [osb[i]], [sq[i]])
                b_n = pbs[4]
                for hd in range(4):
                    for j in range(2):
                        self.op("pe", lambda g: g.matmul(b_n.t[:, hd * 128:(hd + 1) * 128], self.ones.t[:, :], sq[i].t[:, hd * 2 + j, :],
                                                          start=(j == 0), stop=(j == 1)), [self.ones, sq[i]], [b_n], pe_acc=True)
                rf = rstd[i].t[:].rearrange("p h t -> p (h t)")
                self.op("dve", lambda g: g.tensor_scalar(out=rf, in0=b_n.t[:, :], scalar1=1.0 / 256, scalar2=cfg.EPS, op0=ALU.mult, op1=ALU.add), [b_n], [rstd[i]])
                self.op("act", lambda g: g.activation(out=rf, in_=rf, func=AF.Sqrt), [rstd[i]], [rstd[i]])
                self.op("dve", lambda g: g.reciprocal(out=rf, in_=rf), [rstd[i]], [rstd[i]])
                for half in range(2):
                    b_g = (pbs[5], pbs[6])[half]
                    for jj in range(4):
                        ch = half * 4 + jj
                        for k in range(8):
                            self.op("pe", lambda g: g.matmul(b_g.t[:, jj * 128:(jj + 1) * 128], win.t[:, k, 2048 + ch * 128:2048 + (ch + 1) * 128], h_t.t[:, k, :],
                                                              start=(k == 0), stop=(k == 7)), [win, h_t], [b_g], pe_acc=True)
                    self.op("act", lambda g: g.activation(out=ogs[i].t[:, half * 4:(half + 1) * 4, :].rearrange("p j t -> p (j t)"), in_=b_g.t[:, :], func=AF.Silu),
                            [b_g], [ogs[i]])
                for hd in range(4):
                    for j in range(2):
                        ch = hd * 2 + j
                        self.op("dve", lambda g: g.scalar_tensor_tensor(out=osb[i].t[:, ch, :], in0=osb[i].t[:, ch, :], scalar=ong.t[:, j:j + 1],
                                in1=rstd[i].t[:, hd, :], op0=ALU.mult, op1=ALU.mult), [osb[i], ong, rstd[i]], [osb[i]])
                self.op("pool", lambda g: g.tensor_tensor(out=fin[i].t[:], in0=osb[i].t[:], in1=ogs[i].t[:], op=ALU.mult), [osb[i], ogs[i]], [fin[i]])
                self.dma("sp", mv[:, :, t0:t0 + 128], fin[i].t[:], reads=[fin[i]], writes=[self.M], owner=self.M)
            it += 1
            if GLA_BARRIER:
                self.barrier()
    self.end_phase()


Net.phase_gla = _gla_phase


def gla_consts():
    s = np.arange(128)[:, None]
    c = np.arange(128)[None, :]
    mf = (s <= c).astype(np.float32)
    mb = (s >= c).astype(np.float32)
    masks = np.stack([np.broadcast_to(mf[:, None, :], (128, 4, 128)), np.broadcast_to(mb[:, None, :], (128, 4, 128))], axis=1)
    seg = np.ones((128, 512), np.float32)
    seg[:, ::128] = 0.0
    return np.ascontiguousarray(masks), seg, np.eye(128, dtype=np.float32)


def hyena_consts(L):
    N = 2 * L
    N1 = N // 128
    P1 = N1 // 2
    t = np.linspace(0.0, 1.0, L, dtype=np.float32)
    bands = 16
    w = (2.0 * math.pi * np.arange(L, dtype=np.float32) / L).astype(np.float32)
    f = np.linspace(1e-4, bands - 1, bands, dtype=np.float32)
    z = np.concatenate([t[:, None], np.cos(f[None, :] * w[:, None]), -np.sin(f[None, :] * w[:, None])], axis=-1).astype(np.float32)
    idx = np.concatenate([np.arange(L), [0], np.arange(L - 1, 0, -1)])
    z2T = np.ascontiguousarray(z[idx].T)
    t2 = t[idx].copy()
    t2[L] = 0.0
    t2row = np.ascontiguousarray(np.broadcast_to(t2[None, :], (128, N))).astype(np.float32)
    n1 = np.arange(N1)[:, None].astype(np.float64)
    k1 = np.arange(N1)[None, :].astype(np.float64)
    th = 2 * math.pi * n1 * k1 / N1
    Fa = np.concatenate([np.cos(th), -np.sin(th)], axis=1)
    n2 = np.arange(128, dtype=np.float64)
    k2 = np.arange(128, dtype=np.float64)
    Ec = np.zeros((N1, 128, 3, 128), np.float32)
    Ei = np.zeros((N1, 128, 3, 128), np.float32)
    for kk in range(N1):
        thc = 2 * math.pi * n2[:, None] * (kk + N1 * k2[None, :]) / N
        c, s = np.cos(thc), np.sin(thc)
        Ec[kk, :, 0], Ec[kk, :, 1], Ec[kk, :, 2] = c, -s, s
        Ei[kk, :, 0], Ei[kk, :, 1], Ei[kk, :, 2] = c.T, s.T, -s.T
    thg = 2 * math.pi * np.arange(N1)[:, None] * np.arange(P1)[None, :] / N1
    G = np.stack([np.cos(thg) / N, -np.sin(thg) / N], axis=1)
    import ml_dtypes
    bf = ml_dtypes.bfloat16
    return dict(z2T=z2T, t2row=t2row, Fa=Fa.astype(np.float32), Ec=Ec.astype(bf), Ei=Ei.astype(bf), G=G.astype(np.float32))


def hyena_negdelta():
    max_decay = math.log(1e-2) / 0.3
    min_decay = math.log(1e-2) / 1.5
    deltas = np.linspace(min_decay, max_decay, 1024, dtype=np.float32)
    nd = -np.abs(deltas)
    return np.ascontiguousarray(nd.reshape(8, 128).T)


def _hy_inproj(self, w_in, b_in, conv_w, conv_b, Z, with_ctx):
    cfg = self.cfg
    self.begin_phase()
    stg = [self.sb("hi_stg%d" % i, [128, 3072], F32) for i in range(2)]
    win = self.sb("hi_win", [128, 8, 3072], BF16)
    wv = w_in.rearrange("(k p) n -> p k n", p=128)
    for k in range(8):
        st = stg[k % 2]
        self.dma("sp" if k % 2 == 0 else "act", st.t[:, :], wv[:, k, :], writes=[st])
        self.copy(ENG3[k % 3], win, win.t[:, k, :], st, st.t[:, :])
    bi = self.sb("hi_bi", [128, 24], F32)
    cb = self.sb("hi_cb", [128, 24], F32)
    cw = self.sb("hi_cw", [128, 3, 24], F32)
    self.dma("sp", bi.t[:], b_in, writes=[bi])
    self.dma("sp", cb.t[:], conv_b, writes=[cb])
    self.dma("sp", cw.t[:], conv_w, writes=[cw])
    TBH = 510
    hts = [self.sb("hi_h%d" % i, [128, 8, 512], BF16) for i in range(2)]
    zp = [self.sb("hi_zp%d" % i, [128, 512], F32) for i in range(3)]
    zo = [self.sb("hi_zo%d" % i, [128, 24, TBH], F32) for i in range(2)]
    hv = self.H.t.ap().rearrange("(k p) t -> p k t", p=128)
    zv = Z.t.ap().rearrange("(k p) t -> p k t", p=128)
    seqs = [(0, cfg.SEQ)] + ([(cfg.SEQ, cfg.CTX)] if with_ctx else [])
    it = 0
    for (s0, sl) in seqs:
        for a in range(0, sl, TBH):
            b = min(a + TBH, sl)
            n = b - a
            lo = max(a - 1, 0)
            hi = min(b + 1, sl)
            off = lo - (a - 1)
            ncol = hi - lo
            ht = hts[it % 2]
            zt = zo[it % 2]
            it += 1
            self.dma("sp", ht.t[:, :, off:off + ncol], hv[:, :, s0 + lo:s0 + hi], reads=[self.H], writes=[ht])
            for ch in range(24):
                pbk = self.bank()
                for k in range(8):
                    self.op("pe", lambda g: g.matmul(pbk.t[:, off:off + ncol], win.t[:, k, ch * 128:(ch + 1) * 128], ht.t[:, k, off:off + ncol],
                                                      start=(k == 0), stop=(k == 7)), [win, ht], [pbk], pe_acc=True)
                z = zp[ch % 3]
                if off == 1:
                    self.op("pool", lambda g: g.memset(z.t[:, 0:1], 0.0), [], [z])
                if hi == sl and (b + 1) > sl:
                    self.op("pool", lambda g: g.memset(z.t[:, n + 1:n + 2], 0.0), [], [z])
                self.op("act", lambda g: g.activation(out=z.t[:, off:off + ncol], in_=pbk.t[:, off:off + ncol], func=AF.Identity,
                                                      bias=bi.t[:, ch:ch + 1]), [pbk, bi], [z])
                self.op("dve", lambda g: g.tensor_scalar(out=zt.t[:, ch, :n], in0=z.t[:, 0:n], scalar1=cw.t[:, 0, ch:ch + 1],
                        scalar2=cb.t[:, ch:ch + 1], op0=ALU.mult, op1=ALU.add), [z, cw, cb], [zt])
                self.op("dve", lambda g: g.scalar_tensor_tensor(out=zt.t[:, ch, :n], in0=z.t[:, 1:n + 1], scalar=cw.t[:, 1, ch:ch + 1],
                        in1=zt.t[:, ch, :n], op0=ALU.mult, op1=ALU.add), [z, cw, zt], [zt])
                self.op("dve", lambda g: g.scalar_tensor_tensor(out=zt.t[:, ch, :n], in0=z.t[:, 2:n + 2], scalar=cw.t[:, 2, ch:ch + 1],
                        in1=zt.t[:, ch, :n], op0=ALU.mult, op1=ALU.add), [z, cw, zt], [zt])
            self.dma("act", zv[:, :, s0 + a:s0 + b], zt.t[:, :, :n], reads=[zt], writes=[Z], owner=Z)
    self.end_phase()


def _hy_filter(self, L, hc, fw, Hf, skip):
    N = 2 * L
    BS = min(512, L)
    NB = N // BS
    self.begin_phase()
    w1 = self.sb("hf_w1", [33, 64], F32)
    w2 = self.sb("hf_w2", [64, 64], F32)
    w3 = self.sb("hf_w3", [64, 64], F32)
    w4 = self.sb("hf_w4", [64, 4096], F32)
    fr = self.sb("hf_fr", [64, 1], F32)
    bs = self.sb("hf_bs", [64, 3], F32)
    fb = self.sb("hf_fb", [64, 3], F32)
    self.dma("sp", w1.t[:], fw["w1"], writes=[w1])
    self.dma("sp", w2.t[:], fw["w2"], writes=[w2])
    self.dma("sp", w3.t[:], fw["w3"], writes=[w3])
    self.dma("sp", w4.t[:], fw["w4"], writes=[w4])
    self.dma("sp", fr.t[:], fw["freq"], writes=[fr])
    for i, nm in enumerate(("b1", "b2", "b3")):
        self.dma("sp", bs.t[:, i:i + 1], fw[nm], writes=[bs])
    self.op("dve", lambda g: g.tensor_scalar(out=fb.t[:], in0=bs.t[:], scalar1=fr.t[:, 0:1], scalar2=16 * math.pi, op0=ALU.mult, op1=ALU.add), [bs, fr], [fb])
    nd = self.sb("hf_nd", [128, 8], F32)
    self.dma("sp", nd.t[:], hc["negdelta"].t.ap(), writes=[nd])
    sk = self.sb("hf_sk", [128, 2, 8], F32)
    self.dma("sp", sk.t[:], skip, writes=[sk])
    qq = self.sb("hf_qq", [64, 512], F32)
    ki = self.sb("hf_ki", [64, 512], mybir.dt.int32)
    h3T = self.sb("hf_h3T", [64, N], F32)
    zt = [self.sb("hf_z%d" % i, [33, 512], F32) for i in range(2)]
    aa = [self.sb("hf_a%d" % i, [64, 512], F32) for i in range(2)]
    ss = [self.sb("hf_s%d" % i, [64, 512], F32) for i in range(2)]
    ws = (w1, w2, w3)
    for blk in range(NB):
        z = zt[blk % 2]
        self.dma("sp", z.t[:, :BS], hc["z2T"].t.ap()[:, blk * BS:(blk + 1) * BS], writes=[z])
        src = z
        for li in range(3):
            pbk = self.bank()
            kk = 33 if li == 0 else 64
            self.op("pe", lambda g: g.matmul(pbk.t[:64, :BS], ws[li].t[:kk, :], src.t[:kk, :BS], start=True, stop=True), [ws[li], src], [pbk], pe_acc=True)
            a = aa[li % 2]
            self.op("act", lambda g: g.activation(out=a.t[:, :BS], in_=pbk.t[:64, :BS], func=AF.Identity, scale=fr.t[:, 0:1], bias=fb.t[:, li:li + 1]),
                    [pbk, fr, fb], [a])
            self.op("dve", lambda g: g.tensor_scalar(out=qq.t[:, :BS], in0=a.t[:, :BS], scalar1=1.0 / (2 * math.pi), scalar2=0.5, op0=ALU.mult, op1=ALU.add), [a], [qq])
            self.op("dve", lambda g: g.tensor_copy(out=ki.t[:, :BS], in_=qq.t[:, :BS]), [qq], [ki])
            self.op("dve", lambda g: g.tensor_copy(out=qq.t[:, :BS], in_=ki.t[:, :BS]), [ki], [qq])
            self.op("dve", lambda g: g.scalar_tensor_tensor(out=a.t[:, :BS], in0=qq.t[:, :BS], scalar=-2 * math.pi, in1=a.t[:, :BS], op0=ALU.mult, op1=ALU.add), [qq, a], [a])
            self.op("dve", lambda g: g.tensor_scalar(out=qq.t[:, :BS], in0=a.t[:, :BS], scalar1=-math.pi, scalar2=2 * math.pi, op0=ALU.is_lt, op1=ALU.mult), [a], [qq])
            self.op("pool", lambda g: g.tensor_tensor(out=a.t[:, :BS], in0=a.t[:, :BS], in1=qq.t[:, :BS], op=ALU.add), [a, qq], [a])
            self.op("dve", lambda g: g.tensor_scalar(out=qq.t[:, :BS], in0=a.t[:, :BS], scalar1=math.pi, scalar2=-2 * math.pi, op0=ALU.is_gt, op1=ALU.mult), [a], [qq])
            self.op("pool", lambda g: g.tensor_tensor(out=a.t[:, :BS], in0=a.t[:, :BS], in1=qq.t[:, :BS], op=ALU.add), [a, qq], [a])
            if li < 2:
                dst = ss[li % 2]
                self.op("act", lambda g: g.activation(out=dst.t[:, :BS], in_=a.t[:, :BS], func=AF.Sin), [a], [dst])
                src = dst
            else:
                self.op("act", lambda g: g.activation(out=h3T.t[:, blk * BS:(blk + 1) * BS], in_=a.t[:, :BS], func=AF.Sin), [a], [h3T])
    hraw = self.sb("hf_hraw", [128, N], F32)
    t2 = [self.sb("hf_t2%d" % i, [128, 512], F32) for i in range(2)]
    dc = [self.sb("hf_dc%d" % i, [128, 512], F32) for i in range(2)]
    sab = self.sb("hf_sab", [128, 1], F32)
    hfv = Hf.t.ap()
    it = 0
    for o in range(2):
        for cc in range(8):
            for blk in range(NB):
                d = 0 if blk < NB // 2 else 1
                col0 = o * 2048 + d * 1024 + cc * 128
                pbk = self.bank()
                self.op("pe", lambda g: g.matmul(pbk.t[:, :BS], w4.t[:, col0:col0 + 128], h3T.t[:, blk * BS:(blk + 1) * BS], start=True, stop=True),
                        [w4, h3T], [pbk], pe_acc=True)
                tt = t2[it % 2]
                dd = dc[it % 2]
                it += 1
                self.dma("act", tt.t[:, :BS], hc["t2row"].t.ap()[:, blk * BS:(blk + 1) * BS], writes=[tt])
                self.op("act", lambda g: g.activation(out=dd.t[:, :BS], in_=tt.t[:, :BS], func=AF.Exp, scale=nd.t[:, cc:cc + 1]), [tt, nd], [dd])
                self.op("dve", lambda g: g.tensor_tensor(out=hraw.t[:, blk * BS:(blk + 1) * BS], in0=pbk.t[:, :BS], in1=dd.t[:, :BS], op=ALU.mult),
                        [pbk, dd], [hraw])
            self.op("pool", lambda g: g.memset(hraw.t[:, L:L + 1], 0.0), [], [hraw])
            self.op("dve", lambda g: g.tensor_reduce(out=sab.t[:], in_=hraw.t[:], axis=AX.X, op=ALU.add, apply_absolute_value=True), [hraw], [sab])
            self.op("dve", lambda g: g.reciprocal(out=sab.t[:], in_=sab.t[:]), [sab], [sab])
            self.op("dve", lambda g: g.tensor_scalar(out=hraw.t[:], in0=hraw.t[:], scalar1=sab.t[:, 0:1], scalar2=None, op0=ALU.mult), [hraw, sab], [hraw])
            self.op("dve", lambda g: g.tensor_scalar(out=hraw.t[:, 0:1], in0=hraw.t[:, 0:1], scalar1=sk.t[:, o, cc:cc + 1], scalar2=None, op0=ALU.add),
                    [hraw, sk], [hraw])
            self.dma("sp", hfv[o, cc * 128:(cc + 1) * 128, :], hraw.t[:], reads=[hraw], writes=[Hf], owner=Hf)
    self.end_phase()


Net.hy_inproj = _hy_inproj
Net.hy_filter = _hy_filter


def _hy_fft_tables(self, hc, N1):
    P1 = N1 // 2
    fas = self.sb("hx_fas", [N1, 2 * N1], F32)
    fa = self.sb("hx_fa", [N1, 2 * N1], BF16)
    gs = self.sb("hx_gs", [N1, 2, P1], F32)
    gt = self.sb("hx_g", [N1, 2, P1], BF16)
    self.dma("sp", fas.t[:], hc["Fa"].t.ap(), writes=[fas])
    self.dma("sp", gs.t[:], hc["G"].t.ap(), writes=[gs])
    self.copy("dve", fa, fa.t[:], fas, fas.t[:])
    self.copy("dve", gt, gt.t[:], gs, gs.t[:])
    return fa, gt


def _hy_stageA(self, ub, nrow, fa, A_sb, N1, CG):
    W = 2 * N1
    per = max(1, 512 // W)
    for c0 in range(0, CG, per):
        pbk = self.pb[c0 // per % 2]
        for j in range(per):
            c = c0 + j
            self.op("pe", lambda g: g.matmul(pbk.t[:, j * W:(j + 1) * W], ub.t[:nrow, c, :], fa.t[:nrow, :], start=True, stop=True),
                    [ub, fa], [pbk], pe_acc=True)
        self.op("act", lambda g: g.activation(out=A_sb.t[:, c0:c0 + per, :], in_=pbk.t[:, :per * W].rearrange("p (j w) -> p j w", w=W), func=AF.Copy),
                [pbk], [A_sb])


def _hy_stageC(self, A_sb, ect, k1_0, KB, N1, CG, bre, bim):
    for kk in range(KB):
        k1 = k1_0 + kk
        a_re = A_sb.t[:, :, k1]
        a_im = A_sb.t[:, :, N1 + k1]
        osl = slice(kk * CG, (kk + 1) * CG)
        self.op("pe", lambda g: g.matmul(bre.t[:, osl], ect.t[:, kk, 0, :], a_re, start=True, stop=False), [ect, A_sb], [bre], pe_acc=True)
        self.op("pe", lambda g: g.matmul(bre.t[:, osl], ect.t[:, kk, 2, :], a_im, start=False, stop=True), [ect, A_sb], [bre], pe_acc=True)
        self.op("pe", lambda g: g.matmul(bim.t[:, osl], ect.t[:, kk, 1, :], a_re, start=True, stop=False), [ect, A_sb], [bim], pe_acc=True)
        self.op("pe", lambda g: g.matmul(bim.t[:, osl], ect.t[:, kk, 0, :], a_im, start=False, stop=True), [ect, A_sb], [bim], pe_acc=True)


def _hy_spectrum(self, L, hc, Hf, Hspec):
    N = 2 * L
    N1 = N // 128
    CG = 64
    KB = min(4, N1)
    self.begin_phase()
    fa, gt = _hy_fft_tables(self, hc, N1)
    uf = [self.sb("hs_uf%d" % i, [N1, CG, 128], F32) for i in range(1)]
    ub = self.sb("hs_ub", [N1, CG, 128], BF16)
    A_sb = self.sb("hs_A", [128, CG, 2 * N1], BF16)
    ects = [self.sb("hs_ec%d" % i, [128, KB, 3, 128], BF16) for i in range(2)]
    hso = [self.sb("hs_o%d" % i, [128, KB, 2, CG], F32) for i in range(2)]
    bre, bim = self.pb[2], self.pb[3]
    ecv = hc["Ec"].t.ap().rearrange("k p j n -> p k j n")
    it = 0
    for o in range(2):
        for gi in range(1024 // CG):
            c0 = gi * CG
            u = uf[0]
            self.dma("sp", u.t[:], Hf.t.ap()[o, c0:c0 + CG, :].rearrange("c (a b) -> a c b", b=128), reads=[Hf], writes=[u])
            self.copy("dve", ub, ub.t[:], u, u.t[:])
            _hy_stageA(self, ub, N1, fa, A_sb, N1, CG)
            for k1_0 in range(0, N1, KB):
                ect = ects[it % 2]
                ho = hso[it % 2]
                it += 1
                self.dma("act", ect.t[:], ecv[:, k1_0:k1_0 + KB], writes=[ect])
                _hy_stageC(self, A_sb, ect, k1_0, KB, N1, CG, bre, bim)
                self.op("act", lambda g: g.activation(out=ho.t[:, :, 0, :], in_=bre.t[:, :KB * CG].rearrange("p (k c) -> p k c", c=CG), func=AF.Copy), [bre], [ho])
                self.op("dve", lambda g: g.tensor_copy(out=ho.t[:, :, 1, :], in_=bim.t[:, :KB * CG].rearrange("p (k c) -> p k c", c=CG)), [bim], [ho])
                self.dma("sp", Hspec.t.ap()[o, gi, k1_0:k1_0 + KB].rearrange("k p x -> p k x"), ho.t[:].rearrange("p k a c -> p k (a c)"),
                         reads=[ho], writes=[Hspec], owner=Hspec)
    self.end_phase()


def _hy_conv(self, L, tok0, hc, Z, Hspec):
    N = 2 * L
    N1 = N // 128
    P1 = N1 // 2
    CG = 64
    KB = min(4, N1)
    self.begin_phase()
    fa, gt = _hy_fft_tables(self, hc, N1)
    U1 = self.sb("hv_u", [P1, CG, 128], F32)
    ub = self.sb("hv_ub", [P1, CG, 128], BF16)
    A_sb = self.sb("hv_A", [128, CG, 2 * N1], BF16)
    D_sb = self.sb("hv_D", [128, CG, 2, N1], F32)
    ects = [self.sb("hv_ec%d" % i, [128, KB, 3, 128], BF16) for i in range(2)]
    eits = [self.sb("hv_ei%d" % i, [128, KB, 3, 128], BF16) for i in range(2)]
    hsp = [self.sb("hv_h%d" % i, [128, KB, 2, CG], F32) for i in range(2)]
    tm = [self.sb("hv_t%d" % i, [128, KB, CG], F32) for i in range(4)]
    zz = [self.sb("hv_z%d" % i, [128, 2, KB, CG], BF16) for i in range(2)]
    dts = [self.sb("hv_dt%d" % i, [N1, 2, 4, 128], BF16) for i in range(2)]
    gts = [self.sb("hv_g%d" % i, [P1, 4, 128], F32) for i in range(2)]
    yo = [self.sb("hv_y%d" % i, [P1, 4, 128], BF16) for i in range(2)]
    bre, bim, dre, dim_, tre, tim = self.pb[2], self.pb[3], self.pb[4], self.pb[5], self.pb[6], self.pb[7]
    ecv = hc["Ec"].t.ap().rearrange("k p j n -> p k j n")
    eiv = hc["Ei"].t.ap().rearrange("k p j n -> p k j n")
    zv = Z.t.ap()
    mv = self.M.t.ap()
    it = 0
    it2 = 0
    for gi in range(1024 // CG):
        c0 = gi * CG
        self.dma("sp", U1.t[:], zv[c0:c0 + CG, tok0:tok0 + L].rearrange("c (a b) -> a c b", b=128), reads=[Z], writes=[U1])
        for o in range(2):
            self.copy("dve", ub, ub.t[:], U1, U1.t[:])
            _hy_stageA(self, ub, P1, fa, A_sb, N1, CG)
            for k1_0 in range(0, N1, KB):
                ect, eit, hs = ects[it % 2], eits[it % 2], hsp[it % 2]
                z = zz[it % 2]
                it += 1
                self.dma("act", ect.t[:], ecv[:, k1_0:k1_0 + KB], writes=[ect])
                self.dma("act", eit.t[:], eiv[:, k1_0:k1_0 + KB], writes=[eit])
                self.dma("sp", hs.t[:].rearrange("p k a c -> p k (a c)"), Hspec.t.ap()[o, gi, k1_0:k1_0 + KB].rearrange("k p x -> p k x"),
                         reads=[Hspec], writes=[hs])
                _hy_stageC(self, A_sb, ect, k1_0, KB, N1, CG, bre, bim)
                xre = bre.t[:, :KB * CG].rearrange("p (k c) -> p k c", c=CG)
                xim = bim.t[:, :KB * CG].rearrange("p (k c) -> p k c", c=CG)
                hre, him = hs.t[:, :, 0, :], hs.t[:, :, 1, :]
                self.op("dve", lambda g: g.tensor_tensor(out=tm[0].t[:], in0=xre, in1=hre, op=ALU.mult), [bre, hs], [tm[0]])
                self.op("dve", lambda g: g.tensor_tensor(out=tm[1].t[:], in0=xim, in1=him, op=ALU.mult), [bim, hs], [tm[1]])
                self.op("pool", lambda g: g.tensor_tensor(out=z.t[:, 0], in0=tm[0].t[:], in1=tm[1].t[:], op=ALU.subtract), [tm[0], tm[1]], [z])
                self.op("dve", lambda g: g.tensor_tensor(out=tm[2].t[:], in0=xre, in1=him, op=ALU.mult), [bre, hs], [tm[2]])
                self.op("dve", lambda g: g.tensor_tensor(out=tm[3].t[:], in0=xim, in1=hre, op=ALU.mult), [bim, hs], [tm[3]])
                self.op("pool", lambda g: g.tensor_tensor(out=z.t[:, 1], in0=tm[2].t[:], in1=tm[3].t[:], op=ALU.add), [tm[2], tm[3]], [z])
                for kk in range(KB):
                    osl = slice(kk * CG, (kk + 1) * CG)
                    self.op("pe", lambda g: g.matmul(dre.t[:, osl], eit.t[:, kk, 0, :], z.t[:, 0, kk, :], start=True, stop=False), [eit, z], [dre], pe_acc=True)
                    self.op("pe", lambda g: g.matmul(dre.t[:, osl], eit.t[:, kk, 2, :], z.t[:, 1, kk, :], start=False, stop=True), [eit, z], [dre], pe_acc=True)
                    self.op("pe", lambda g: g.matmul(dim_.t[:, osl], eit.t[:, kk, 1, :], z.t[:, 0, kk, :], start=True, stop=False), [eit, z], [dim_], pe_acc=True)
                    self.op("pe", lambda g: g.matmul(dim_.t[:, osl], eit.t[:, kk, 0, :], z.t[:, 1, kk, :], start=False, stop=True), [eit, z], [dim_], pe_acc=True)
                self.op("act", lambda g: g.activation(out=D_sb.t[:, :, 0, k1_0:k1_0 + KB], in_=dre.t[:, :KB * CG].rearrange("p (k c) -> p c k", c=CG), func=AF.Copy),
                        [dre], [D_sb])
                self.op("act", lambda g: g.activation(out=D_sb.t[:, :, 1, k1_0:k1_0 + KB], in_=dim_.t[:, :KB * CG].rearrange("p (k c) -> p c k", c=CG), func=AF.Copy),
                        [dim_], [D_sb])
            for s0 in range(0, CG, 4):
                dt = dts[it2 % 2]
                gtile = gts[it2 % 2]
                y_o = yo[it2 % 2]
                it2 += 1
                for j in range(4):
                    self.op("pe", lambda g: g.transpose(out=tre.t[:N1, j * 128:(j + 1) * 128], in_=D_sb.t[:, s0 + j, 0, :], identity=self.ident.t[:, :]),
                            [D_sb, self.ident], [tre], pe_acc=True)
                    self.op("pe", lambda g: g.transpose(out=tim.t[:N1, j * 128:(j + 1) * 128], in_=D_sb.t[:, s0 + j, 1, :], identity=self.ident.t[:, :]),
                            [D_sb, self.ident], [tim], pe_acc=True)
                self.op("act", lambda g: g.activation(out=dt.t[:, 0].rearrange("p j n -> p (j n)"), in_=tre.t[:N1, :], func=AF.Copy), [tre], [dt])
                self.op("dve", lambda g: g.tensor_copy(out=dt.t[:, 1].rearrange("p j n -> p (j n)"), in_=tim.t[:N1, :]), [tim], [dt])
                pby = self.pb[s0 // 4 % 2]
                self.op("pe", lambda g: g.matmul(pby.t[:P1, :], gt.t[:, 0, :], dt.t[:, 0].rearrange("p j n -> p (j n)"), start=True, stop=False), [gt, dt], [pby], pe_acc=True)
                self.op("pe", lambda g: g.matmul(pby.t[:P1, :], gt.t[:, 1, :], dt.t[:, 1].rearrange("p j n -> p (j n)"), start=False, stop=True), [gt, dt], [pby], pe_acc=True)
                cg0 = (1 + o) * 1024 + c0 + s0
                self.dma("sp", gtile.t[:], zv[cg0:cg0 + 4, tok0:tok0 + L].rearrange("c (a b) -> a c b", b=128), reads=[Z], writes=[gtile])
                if o == 0:
                    self.op("dve", lambda g: g.tensor_tensor(out=U1.t[:, s0:s0 + 4, :].rearrange("p j n -> p (j n)"), in0=pby.t[:P1, :],
                            in1=gtile.t[:].rearrange("p j n -> p (j n)"), op=ALU.mult), [pby, gtile], [U1])
                else:
                    self.op("dve", lambda g: g.tensor_tensor(out=y_o.t[:].rearrange("p j n -> p (j n)"), in0=pby.t[:P1, :],
                            in1=gtile.t[:].rearrange("p j n -> p (j n)"), op=ALU.mult), [pby, gtile], [y_o])
                    self.dma("act", mv[c0 + s0:c0 + s0 + 4, tok0:tok0 + L].rearrange("c (a b) -> a c b", b=128), y_o.t[:], reads=[y_o], writes=[self.M], owner=self.M)
    self.end_phase()


Net.hy_spectrum = _hy_spectrum
Net.hy_conv = _hy_conv


_FWN = {"w1": [33, 64], "b1": [64, 1], "w2": [64, 64], "b2": [64, 1], "w3": [64, 64], "b3": [64, 1], "w4": [64, 4096], "freq": [64, 1]}


def build_full(cfg):
    import ml_dtypes
    net = Net(cfg)
    net.setup_common()
    net.build_ada()
    SEQ, CTX, NTOK = cfg.SEQ, cfg.CTX, cfg.NTOK
    consts = {}
    mlp_w1 = net.inp("mlp_w1", [4, 1024, 4096]).t.ap()
    mlp_w2 = net.inp("mlp_w2", [4, 4096, 1024]).t.ap()
    hy_w_in = net.inp("hy_w_in", [2, 1024, 3072]).t.ap()
    hy_b_in = net.inp("hy_b_in", [2, 128, 24]).t.ap()
    hy_cw = net.inp("hy_cw", [2, 128, 3, 24]).t.ap()
    hy_cb = net.inp("hy_cb", [2, 128, 24]).t.ap()
    hy_f = {k: net.inp("hy_f_" + k, [2] + v).t.ap() for k, v in _FWN.items()}
    hy_skip = net.inp("hy_skip", [2, 128, 2, 8]).t.ap()
    hy_w_out = net.inp("hy_w_out", [2, 1024, 1024]).t.ap()
    hy_b_out = net.inp("hy_b_out", [128, 2, 8])
    nd = net.inp("negdelta", [128, 8])
    consts["negdelta"] = hyena_negdelta()

    def hconsts(L, tag):
        hcn = hyena_consts(L)
        tk = {}
        for k, v in hcn.items():
            nm = "hc%s_%s" % (tag, k)
            tk[k] = net.inp(nm, list(v.shape), BF16 if v.dtype == ml_dtypes.bfloat16 else F32)
            consts[nm] = v
        tk["negdelta"] = nd
        return tk
    tkL = hconsts(SEQ, "L")
    tkC = hconsts(CTX, "C")
    gla_w_in = net.inp("gla_w_in", [1024, 3104])
    gla_gk_w2 = net.inp("gla_gk_w2", [2, 16, 512])
    gla_gk_b = net.inp("gla_gk_b", [128, 2, 4])
    gla_onorm = net.inp("gla_onorm", [128, 2])
    gla_wo = net.inp("gla_wo", [1024, 1024]).t.ap()
    masks = net.inp("masks", [128, 2, 4, 128])
    seg = net.inp("seg", [128, 512])
    mk, sg, _ = gla_consts()
    consts["masks"], consts["seg"], consts["ident"] = mk, sg, np.eye(128, dtype=np.float32)
    mla_w_down = net.inp("mla_w_down", [1024, 704])
    mla_qnorm = net.inp("mla_qnorm", [128, 3])
    mla_w_uq = net.inp("mla_w_uq", [384, 1536])
    mla_kvnorm = net.inp("mla_kvnorm", [128, 2])
    mla_w_ukv = net.inp("mla_w_ukv", [256, 2048])
    mla_wo = net.inp("mla_wo", [1024, 1024]).t.ap()
    cos_t = net.inp("cos_t", [64, NTOK])
    sin_t = net.inp("sin_t", [64, NTOK])
    consts["cos_t"], consts["sin_t"] = rope_tables(SEQ, NTOK)
    bo = net.sb("hy_bo", [128, 2, 8], F32)
    net.dma("sp", bo.t[:], hy_b_out.t.ap(), writes=[bo])
    bo_j = []
    for j in range(2):
        t = net.sb("hy_bo%d" % j, [128, 8], F32)
        net.op("dve", lambda g: g.tensor_copy(out=t.t[:], in_=bo.t[:, j, :]), [bo], [t])
        bo_j.append(t)
    Z = net.dram("Z", [3072, NTOK], F32)
    HfL = net.dram("HfL", [2, 1024, 2 * SEQ], F32)
    HsL = net.dram("HsL", [2, 16, 2 * SEQ // 128, 128, 128], F32)
    HfC = net.dram("HfC", [2, 1024, 2 * CTX], F32)
    HsC = net.dram("HsC", [2, 16, 2 * CTX // 128, 128, 128], F32)

    def hyena(j, with_ctx):
        net.hy_inproj(hy_w_in[j], hy_b_in[j], hy_cw[j], hy_cb[j], Z, with_ctx)
        fw = {k: v[j] for k, v in hy_f.items()}
        net.hy_filter(SEQ, tkL, fw, HfL, hy_skip[j])
        net.hy_spectrum(SEQ, tkL, HfL, HsL)
        net.hy_conv(SEQ, 0, tkL, Z, HsL)
        if with_ctx:
            net.hy_filter(CTX, tkC, fw, HfC, hy_skip[j])
            net.hy_spectrum(CTX, tkC, HfC, HsC)
            net.hy_conv(CTX, SEQ, tkC, Z, HsC)

    X = net.X
    net.phase_A0()
    hyena(0, True)
    net.phase_C(0, hy_w_out[0], bo_j[0], mlp_w1[0], mlp_w2[0], net.xT, X[0], True, True, False)
    net.phase_gla(gla_w_in, gla_gk_w2, gla_gk_b, gla_onorm, masks, seg)
    net.phase_C(1, gla_wo, net.zero8, mlp_w1[1], mlp_w2[1], X[0], X[1], True, True, False)
    net.phase_mla(mla_w_down, mla_qnorm, mla_w_uq, mla_kvnorm, mla_w_ukv, cos_t, sin_t)
    net.phase_C(2, mla_wo, net.zero8, mlp_w1[2], mlp_w2[2], X[1], X[0], False, False, False)
    hyena(1, False)
    net.phase_C(3, hy_w_out[1], bo_j[1], mlp_w1[3], mlp_w2[3], X[0], X[1], False, False, True)
    net.finish()
    return net, consts


def _pc(v, nch):
    return np.ascontiguousarray(np.asarray(v, np.float32).reshape(nch, 128).T)


def make_shared(inp, consts):
    f = lambda a: np.ascontiguousarray(np.asarray(a, np.float32))
    sh = dict(consts)
    sh["ada_w"] = f(inp["ada_w"])
    sh["ada_b"] = np.ascontiguousarray(np.stack([_pc(inp["ada_b"][l], 48) for l in range(4)], axis=1))
    sh["n1g"] = np.ascontiguousarray(np.stack([_pc(inp["norm1_g"][l], 8) for l in range(4)], axis=1))
    sh["n2g"] = np.ascontiguousarray(np.stack([_pc(inp["norm2_g"][l], 8) for l in range(4)], axis=1))
    sh["fing"] = _pc(inp["final_g"], 8)[:, :, None].copy()
    sh["mlp_w1"] = f(inp["mlp_w1"])
    sh["mlp_w2"] = f(inp["mlp_w2"])
    sh["hy_w_in"] = f(inp["hy_w_in"])
    sh["hy_b_in"] = np.ascontiguousarray(np.stack([_pc(inp["hy_b_in"][j], 24) for j in range(2)]))
    sh["hy_cb"] = np.ascontiguousarray(np.stack([_pc(inp["hy_conv_b"][j], 24) for j in range(2)]))
    sh["hy_cw"] = np.ascontiguousarray(np.stack([np.stack([_pc(inp["hy_conv_w"][j][t], 24) for t in range(3)], axis=1) for j in range(2)]))
    sh["hy_f_w1"] = f(inp["hy_f_w1"])
    sh["hy_f_w2"] = f(inp["hy_f_w2"])
    sh["hy_f_w3"] = f(inp["hy_f_w3"])
    sh["hy_f_w4"] = f(inp["hy_f_w4"])
    sh["hy_f_b1"] = f(inp["hy_f_b1"])[:, :, None].copy()
    sh["hy_f_b2"] = f(inp["hy_f_b2"])[:, :, None].copy()
    sh["hy_f_b3"] = f(inp["hy_f_b3"])[:, :, None].copy()
    sh["hy_f_freq"] = f(inp["hy_freq"])[:, :, None].copy()
    sh["hy_skip"] = np.ascontiguousarray(np.stack([np.stack([_pc(inp["hy_bias"][j][o], 8) for o in range(2)], axis=1) for j in range(2)]))
    sh["hy_w_out"] = f(inp["hy_w_out"])
    sh["hy_b_out"] = np.ascontiguousarray(np.stack([_pc(inp["hy_b_out"][j], 8) for j in range(2)], axis=1))
    sh["gla_w_in"] = f(inp["gla_w_in"][0])
    sh["gla_gk_w2"] = f(inp["gla_gk_w2"][0])
    sh["gla_gk_b"] = np.ascontiguousarray(np.asarray(inp["gla_gk_b"][0], np.float32).reshape(2, 4, 128).transpose(2, 0, 1))
    sh["gla_onorm"] = _pc(inp["gla_onorm"][0], 2)
    sh["gla_wo"] = f(inp["gla_wo"][0])
    sh["mla_w_down"] = f(inp["mla_w_down"][0])
    sh["mla_qnorm"] = _pc(inp["mla_qnorm"][0], 3)
    sh["mla_w_uq"] = f(inp["mla_w_uq"][0])
    sh["mla_kvnorm"] = _pc(inp["mla_kvnorm"][0], 2)
    sh["mla_w_ukv"] = f(inp["mla_w_ukv"][0])
    sh["mla_wo"] = f(inp["mla_wo"][0])
    return sh


def make_core(inp, b):
    x = np.asarray(inp["x"][b], np.float32)
    cx = np.asarray(inp["ctx"][b], np.float32)
    xT = np.ascontiguousarray(np.concatenate([x, cx], axis=0).T)
    cc = np.ascontiguousarray(np.stack([_pc(inp["c"][b], 8), _pc(inp["c_ctx"], 8)], axis=-1))
    return {"xT": xT, "cc": cc}


_CACHE = {}
NCORES = int(os.environ.get('NCORES', '4'))


def kernel(**inputs):
    SEQ = inputs["x"].shape[1]
    CTX = inputs["ctx"].shape[1]
    B = inputs["x"].shape[0]
    key = (SEQ, CTX)
    if key not in _CACHE:
        _CACHE[key] = build_full(Cfg(SEQ=SEQ, CTX=CTX, TB=256))
    net, consts = _CACHE[key]
    shared = make_shared(inputs, consts)
    n = NCORES
    in_maps = []
    for core in range(n):
        m = dict(shared)
        m.update(make_core(inputs, core % B))
        in_maps.append(m)
    res = run_bass_kernel_spmd(net.nc, in_maps, core_ids=list(range(n)))
    out = np.stack([np.ascontiguousarray(res.results[b]["outT"].T) for b in range(B)], axis=0)
    return out.astype(np.float32)
```

```python
import math
import os
from contextlib import ExitStack
import numpy as np
import concourse.bass as bass
import concourse.mybir as mybir
from concourse.bass_utils import run_bass_kernel_spmd

F32 = mybir.dt.float32
BF16 = mybir.dt.bfloat16
AF = mybir.ActivationFunctionType
ALU = mybir.AluOpType
AX = mybir.AxisListType


class Tk:
    __slots__ = ("name", "t", "last_w", "readers", "dsem", "dcnt")

    def __init__(self, name, t):
        self.name = name
        self.t = t
        self.last_w = None
        self.readers = {}
        self.dsem = None
        self.dcnt = 0

    def __getitem__(self, idx):
        return self.t[idx]


class B:
    EPOCH = 28000

    def __init__(self):
        self.nc = bass.Bass("TRN2", target_bir_lowering=False)
        nc = self.nc
        self.eng = {"pe": nc.tensor, "dve": nc.vector, "act": nc.scalar, "pool": nc.gpsimd, "sp": nc.sync}
        self.esem = {}
        self.ecnt = {}
        self.known = {e: {} for e in self.eng}
        self.nsem = 0
        self.allsems = []
        self.sempool = []
        self.phase_tks = [[]]
        for e in self.eng:
            self.esem[e] = self.newsem("e_" + e)
            self.ecnt[e] = 0
        self.ninstr = 0
        self.out_objs = []
        self.uid = 0
        self.stk = ExitStack()
        self.stks = []
        self.dma_owners = {}
        self.pending_old = []
        self.barsem = self.newsem("bar")
        self.barcnt = 0

    def newsem(self, name):
        self.nsem += 1
        sm = self.nc.alloc_semaphore(name + "_%d" % self.nsem)
        self.allsems.append(sm)
        return sm

    def sb(self, name, shape, dt):
        self.uid += 1
        t = self.stk.enter_context(self.nc.sbuf_tensor("%s_%d" % (name, self.uid), list(shape), dt))
        tk = Tk(name, t)
        self.phase_tks[-1].append(tk)
        return tk

    def begin_phase(self):
        self.stks.append(self.stk)
        self.stk = ExitStack()
        self.phase_tks.append([])

    def end_phase(self):
        self.minrem = min(getattr(self, "minrem", 1 << 30), self.nc.sbuf_bytes_remaining)
        if PRINT_MEM:
            print("phase end: sbuf remaining", self.nc.sbuf_bytes_remaining)
        self.barrier()
        for tk in self.phase_tks.pop():
            if tk.dsem is not None:
                if tk.dcnt < self.EPOCH - 4000:
                    self.sempool.append((tk.dsem, tk.dcnt))
                tk.dsem = None
        self.stk.close()
        self.stk = self.stks.pop()

    def barrier(self):
        for e in self.eng:
            if self.ecnt[e] > 0:
                self._need("sp", (self.esem[e], self.ecnt[e]))
        for o in self.dma_owners.values():
            self._need("sp", (o.dsem, o.dcnt))
        self.dma_owners = {}
        for dep in self.pending_old:
            self._need("sp", dep)
        self.pending_old = []
        if self.barcnt >= self.EPOCH:
            self.barsem = self.newsem("bar")
            self.barcnt = 0
        self.eng["sp"].sem_inc(self.barsem, 1)
        self.barcnt += 1
        for e in self.eng:
            if e != "sp":
                self.eng[e].wait_ge(self.barsem, self.barcnt)
            for e2 in self.eng:
                self.known[e][id(self.esem[e2])] = self.ecnt[e2]

    def ps(self, name, shape=(128, 512), dt=F32):
        return Tk(name, self.nc.alloc_psum_tensor(name, list(shape), dt))

    def dram(self, name, shape, dt, kind="Internal"):
        return Tk(name, self.nc.dram_tensor(name, list(shape), dt, kind=kind))

    def _need(self, e, dep):
        if dep is None:
            return
        sem, val = dep
        k = self.known[e]
        if k.get(id(sem), 0) >= val:
            return
        self.eng[e].wait_ge(sem, val)
        k[id(sem)] = val

    def _deps(self, e, reads, writes, pe_acc=False):
        for r in reads:
            self._need(e, r.last_w)
        for w in writes:
            if not (pe_acc and w.last_w is not None and w.last_w[0] is self.esem["pe"]):
                self._need(e, w.last_w)
            for dep in w.readers.values():
                self._need(e, dep)

    def _mark(self, dep, reads, writes):
        for r in reads:
            r.readers[id(dep[0])] = dep
        for w in writes:
            w.last_w = dep
            w.readers = {}

    def op(self, e, ins_fn, reads=(), writes=(), pe_acc=False):
        self._deps(e, reads, writes, pe_acc)
        if self.ecnt[e] >= self.EPOCH:
            self.esem[e] = self.newsem("e_" + e)
            self.ecnt[e] = 0
        ins = ins_fn(self.eng[e])
        self.ecnt[e] += 1
        ins.then_inc(self.esem[e], 1)
        dep = (self.esem[e], self.ecnt[e])
        self._mark(dep, reads, writes)
        self.ninstr += 1

    def dma(self, q, out_ap, in_ap, reads=(), writes=(), owner=None, **kw):
        if owner is None:
            owner = writes[0]
        self._deps(q, reads, writes)
        if owner.dsem is None or owner.dcnt >= self.EPOCH:
            if owner.dsem is not None:
                self.pending_old.append((owner.dsem, owner.dcnt))
            if self.sempool:
                owner.dsem, owner.dcnt = self.sempool.pop()
            else:
                owner.dsem = self.newsem("d_" + owner.name)
                owner.dcnt = 0
        ins = self.eng[q].dma_start(out=out_ap, in_=in_ap, **kw)
        owner.dcnt += 16
        ins.then_inc(owner.dsem, 16)
        dep = (owner.dsem, owner.dcnt)
        self.dma_owners[id(owner)] = owner
        self._mark(dep, reads, writes)
        self.ninstr += 1

    def finish(self):
        for o in self.out_objs:
            self._need("sp", o.last_w)


class Cfg:
    D = 1024
    DC = 8
    HID = 4096
    HC = 32
    DEPTH = 4
    EPS = 1e-6

    def __init__(self, SEQ=8192, CTX=256, TB=256):
        self.SEQ = SEQ
        self.CTX = CTX
        self.NTOK = SEQ + CTX
        self.TB = TB


ENG3 = ("dve", "act", "pool")
PRINT_MEM = False
GLA_BARRIER = int(os.environ.get('GLA_BARRIER', '1'))


class Net(B):
    def __init__(self, cfg, test=None):
        super().__init__()
        self.cfg = cfg
        self.test = test
        self.inputs = {}
        self.rr = 0
        nc = self.nc
        self.pb = [self.ps("pb%d" % i) for i in range(8)]
        self.pbi = 0

    def inp(self, name, shape, dt=F32):
        t = self.dram(name, shape, dt, kind="ExternalInput")
        self.inputs[name] = t
        return t

    def ccol(self, val):
        return self.ccols[round(float(val), 9)]

    def make_consts(self, vals):
        for v in vals:
            key = round(float(v), 9)
            t = self.sb("cc", [128, 1], F32)
            self.op("pool", lambda g: g.memset(t.t[:], float(v)), [], [t])
            self.ccols[key] = t

    def bank(self):
        b = self.pb[self.pbi % 6]
        self.pbi += 1
        return b

    def copy(self, e, out_t, out_ap, in_t, in_ap):
        if e == "act":
            self.op("act", lambda g: g.activation(out=out_ap, in_=in_ap, func=AF.Copy), [in_t], [out_t])
        else:
            self.op(e, lambda g: g.tensor_copy(out=out_ap, in_=in_ap), [in_t], [out_t])

    def load_cast(self, w2d, K, N, dst, stg, qs=("sp", "act")):
        S = stg[0].t.shape[-1]
        wv = w2d.rearrange("(k p) n -> p k n", p=128)
        KC = K // 128
        if N >= S:
            assert N % S == 0
            pieces = [(k, 1, n0, S) for k in range(KC) for n0 in range(0, N, S)]
        else:
            g = S // N
            assert KC % g == 0
            pieces = [(k, g, 0, N) for k in range(0, KC, g)]
        for (k, g, n0, n) in pieces:
            st = stg[self.rr % len(stg)]
            q = qs[self.rr % len(qs)]
            e = ENG3[self.rr % 3]
            self.rr += 1
            sv = st.t[:, 0:g * n].rearrange("p (g n) -> p g n", g=g)
            self.dma(q, sv, wv[:, k:k + g, n0:n0 + n], reads=[], writes=[st])
            self.copy(e, dst, dst.t[:, k:k + g, n0:n0 + n], st, sv)

    def build_ada(self):
        cfg = self.cfg
        cc = self.inp("cc", [128, 8, 2])
        ada_w = self.inp("ada_w", [cfg.DEPTH, 1024, 6144])
        ada_b = self.inp("ada_b", [128, cfg.DEPTH, 48])
        n1g = self.inp("n1g", [128, cfg.DEPTH, 8])
        n2g = self.inp("n2g", [128, cfg.DEPTH, 8])
        scs = [self.sb("sc%d" % l, [128, 48, 2], F32) for l in range(cfg.DEPTH)]
        self.begin_phase()
        cs = self.sb("cs", [128, 8, 2], F32)
        cs2 = self.sb("cs2", [128, 8, 2], F32)
        abt = self.sb("abt", [128, cfg.DEPTH, 48], F32)
        g1t = self.sb("g1t", [128, cfg.DEPTH, 8], F32)
        g2t = self.sb("g2t", [128, cfg.DEPTH, 8], F32)
        self.dma("sp", cs.t[:], cc.t.ap(), writes=[cs])
        self.dma("sp", abt.t[:], ada_b.t.ap(), writes=[abt])
        self.dma("sp", g1t.t[:], n1g.t.ap(), writes=[g1t])
        self.dma("sp", g2t.t[:], n2g.t.ap(), writes=[g2t])
        self.op("act", lambda g: g.activation(out=cs2.t[:], in_=cs.t[:], func=AF.Silu), [cs], [cs2])
        self.sc = []
        wst = [self.sb("adaw%d" % i, [128, 8, 512], F32) for i in range(2)]
        modt = self.sb("modt", [128, 48, 2], F32)
        for l in range(cfg.DEPTH):
            pbk = self.pb[6]
            wv = ada_w.t.ap()[l].rearrange("(k p) n -> p k n", p=128)
            for piece in range(12):
                st = wst[piece % 2]
                self.dma("sp" if piece % 2 == 0 else "act", st.t[:], wv[:, :, piece * 512:(piece + 1) * 512], writes=[st])
                for cch in range(4):
                    j = piece * 4 + cch
                    for k in range(8):
                        self.op("pe", lambda g, k=k, cch=cch, j=j, st=st: g.matmul(
                            pbk.t[:, 2 * j:2 * j + 2], st.t[:, k, cch * 128:(cch + 1) * 128], cs2.t[:, k, :],
                            start=(k == 0), stop=(k == 7)), [st, cs2], [pbk], pe_acc=True)
            pv = pbk.t[:, 0:96].rearrange("p (j c) -> p j c", c=2)
            for col in range(2):
                self.op("dve", lambda g, col=col: g.tensor_tensor(out=modt.t[:, :, col], in0=pv[:, :, col], in1=abt.t[:, l, :], op=ALU.add),
                        [pbk, abt], [modt])
            sc = scs[l]
            for col in range(2):
                self.op("dve", lambda g, col=col: g.scalar_tensor_tensor(out=sc.t[:, 0:8, col], in0=modt.t[:, 8:16, col], scalar=1.0,
                        in1=g1t.t[:, l, :], op0=ALU.add, op1=ALU.mult), [modt, g1t], [sc])
                self.op("dve", lambda g, col=col: g.tensor_copy(out=sc.t[:, 8:16, col], in_=modt.t[:, 0:8, col]), [modt], [sc])
                self.op("dve", lambda g, col=col: g.tensor_copy(out=sc.t[:, 16:24, col], in_=modt.t[:, 16:24, col]), [modt], [sc])
                self.op("dve", lambda g, col=col: g.scalar_tensor_tensor(out=sc.t[:, 24:32, col], in0=modt.t[:, 32:40, col], scalar=1.0,
                        in1=g2t.t[:, l, :], op0=ALU.add, op1=ALU.mult), [modt, g2t], [sc])
                self.op("dve", lambda g, col=col: g.tensor_copy(out=sc.t[:, 32:40, col], in_=modt.t[:, 24:32, col]), [modt], [sc])
                self.op("dve", lambda g, col=col: g.tensor_copy(out=sc.t[:, 40:48, col], in_=modt.t[:, 40:48, col]), [modt], [sc])
            self.sc.append(sc)
        self.end_phase()

    def rstd_of(self, xt, n, sq, rstd):
        cfg = self.cfg
        pbk = self.pb[7]
        for dc in range(8):
            self.op("act", lambda g, dc=dc: g.activation(out=sq.t[:, dc, :n], in_=xt.t[:, dc, :n], func=AF.Square), [xt], [sq])
        for dc in range(8):
            self.op("pe", lambda g, dc=dc: g.matmul(pbk.t[:, :n], self.ones.t[:, :], sq.t[:, dc, :n], start=(dc == 0), stop=(dc == 7)),
                    [self.ones, sq], [pbk], pe_acc=True)
        self.op("dve", lambda g: g.tensor_scalar(out=rstd.t[:, :n], in0=pbk.t[:, :n], scalar1=1.0 / cfg.D, scalar2=cfg.EPS,
                                                  op0=ALU.mult, op1=ALU.add), [pbk], [rstd])
        self.op("act", lambda g: g.activation(out=rstd.t[:, :n], in_=rstd.t[:, :n], func=AF.Sqrt), [rstd], [rstd])
        self.op("dve", lambda g: g.reciprocal(out=rstd.t[:, :n], in_=rstd.t[:, :n]), [rstd], [rstd])

    def norm_mod(self, xt, n, rstd, tmp, ht, sc, ia, ib, col):
        for dc in range(8):
            self.op("dve", lambda g, dc=dc: g.tensor_tensor(out=tmp.t[:, dc, :n], in0=xt.t[:, dc, :n], in1=rstd.t[:, :n], op=ALU.mult),
                    [xt, rstd], [tmp])
            if ib is None:
                self.op("act", lambda g, dc=dc: g.activation(out=ht.t[:, dc, :n], in_=tmp.t[:, dc, :n], func=AF.Copy,
                                                             scale=sc.t[:, ia + dc, col:col + 1]), [tmp, sc], [ht])
            else:
                self.op("act", lambda g, dc=dc: g.activation(out=ht.t[:, dc, :n], in_=tmp.t[:, dc, :n], func=AF.Identity,
                                                             scale=sc.t[:, ia + dc, col:col + 1], bias=sc.t[:, ib + dc, col:col + 1]),
                        [tmp, sc], [ht])

    def blocks(self, with_ctx):
        cfg = self.cfg
        bl = [(t0, cfg.TB, 0) for t0 in range(0, cfg.SEQ, cfg.TB)]
        if with_ctx:
            bl += [(cfg.SEQ + t0, min(cfg.TB, cfg.CTX - t0), 1) for t0 in range(0, cfg.CTX, cfg.TB)]
        return bl

    def setup_common(self):
        cfg = self.cfg
        TB = cfg.TB
        self.ones = self.sb("ones", [128, 128], BF16)
        self.op("pool", lambda g: g.memset(self.ones.t[:], 1.0), [], [self.ones])
        self.xT = self.inp("xT", [1024, cfg.NTOK])
        self.X = [self.dram("X%d" % i, [1024, cfg.NTOK], F32) for i in range(2)]
        hk = {"tok": "ExternalOutput", "mix": "ExternalInput"}.get(self.test, "Internal")
        mk = {"tok": "ExternalInput", "mix": "ExternalOutput"}.get(self.test, "Internal")
        self.H = self.dram("Hbuf", [1024, cfg.NTOK], BF16, kind=hk)
        self.M = self.dram("Mbuf", [1024, cfg.NTOK], BF16, kind=mk)
        self.OUT = self.dram("outT", [1024, cfg.SEQ], F32, kind="ExternalOutput")
        self.out_objs.append(self.OUT)
        fing = self.inp("fing", [128, 8, 1])
        self.fg = self.sb("fg", [128, 8, 1], F32)
        self.dma("sp", self.fg.t[:], fing.t.ap(), writes=[self.fg])
        identin = self.inp("ident", [128, 128])
        self.ident = self.sb("ident", [128, 128], F32)
        self.dma("sp", self.ident.t[:], identin.t.ap(), writes=[self.ident])
        self.ccols = {}
        self.make_consts([1.0, math.log(128.0 ** -0.5), -math.pi, 0.0])
        self.zero8 = self.sb("zero8", [128, 8], F32)
        self.op("pool", lambda g: g.memset(self.zero8.t[:], 0.0), [], [self.zero8])

    def phase_A0(self):
        self.begin_phase()
        self._phase_A0()
        self.end_phase()

    def _phase_A0(self):
        cfg = self.cfg
        TB = cfg.TB
        xts = [self.sb("a_x%d" % i, [128, 8, TB], F32) for i in range(2)]
        sq = self.sb("a_sq", [128, 8, TB], BF16)
        tmp = self.sb("a_tmp", [128, 8, TB], F32)
        rstd = self.sb("a_rstd", [128, TB], F32)
        hts = [self.sb("a_h%d" % i, [128, 8, TB], BF16) for i in range(2)]
        sc = self.sc[0]
        xv = self.xT.t.ap().rearrange("(k p) t -> p k t", p=128)
        hv = self.H.t.ap().rearrange("(k p) t -> p k t", p=128)
        for bi, (t0, n, col) in enumerate(self.blocks(True)):
            xt = xts[bi % 2]
            ht = hts[bi % 2]
            self.dma("sp", xt.t[:, :, :n], xv[:, :, t0:t0 + n], writes=[xt])
            self.rstd_of(xt, n, sq, rstd)
            self.norm_mod(xt, n, rstd, tmp, ht, sc, 0, 8, col)
            self.dma("act", hv[:, :, t0:t0 + n], ht.t[:, :, :n], reads=[ht], writes=[self.H], owner=self.H)

    def phase_C(self, l, wo_ap, bo_t, w1_ap, w2_ap, x_src, x_dst, ctx_live, next_ctx, last):
        self.begin_phase()
        cfg = self.cfg
        TB = cfg.TB
        stg = [self.sb("c_stg%d" % i, [128, 512], F32) for i in range(2)]
        wo = self.sb("c_wo", [128, 8, 1024], BF16)
        w1 = self.sb("c_w1", [128, 8, 4096], BF16)
        w2 = self.sb("c_w2", [128, 32, 1024], BF16)
        self.load_cast(wo_ap, 1024, 1024, wo, stg)
        self.load_cast(w1_ap, 1024, 4096, w1, stg)
        self.load_cast(w2_ap, 4096, 1024, w2, stg)
        xts = [self.sb("c_x%d" % i, [128, 8, TB], F32) for i in range(2)]
        mts = [self.sb("c_m%d" % i, [128, 8, TB], BF16) for i in range(1)]
        sq = self.sb("c_sq", [128, 8, TB], BF16)
        tmp = self.sb("c_tmp", [128, 8, TB], F32)
        rstd = self.sb("c_rstd", [128, TB], F32)
        h2 = self.sb("c_h2", [128, 8, TB], BF16)
        h1 = self.sb("c_h1", [128, 32, TB], BF16)
        rl = [self.sb("c_rl%d" % i, [128, TB], F32) for i in range(2)]
        hn = h2
        on = tmp
        sc = self.sc[l]
        xv = x_src.t.ap().rearrange("(k p) t -> p k t", p=128)
        xo = x_dst.t.ap().rearrange("(k p) t -> p k t", p=128)
        mv = self.M.t.ap().rearrange("(k p) t -> p k t", p=128)
        hv = self.H.t.ap().rearrange("(k p) t -> p k t", p=128)
        ov = self.OUT.t.ap().rearrange("(k p) t -> p k t", p=128)
        for bi, (t0, n, col) in enumerate(self.blocks(ctx_live)):
            xt = xts[bi % 2]
            mt = mts[0]
            self.dma("sp", xt.t[:, :, :n], xv[:, :, t0:t0 + n], reads=[x_src], writes=[xt])
            self.dma("act", mt.t[:, :, :n], mv[:, :, t0:t0 + n], reads=[self.M], writes=[mt])
            for dc in range(8):
                pbk = self.bank()
                for k in range(8):
                    self.op("pe", lambda g, k=k, dc=dc, pbk=pbk: g.matmul(pbk.t[:, :n], wo.t[:, k, dc * 128:(dc + 1) * 128], mt.t[:, k, :n],
                                                                         start=(k == 0), stop=(k == 7)), [wo, mt], [pbk], pe_acc=True)
                self.op("dve", lambda g, dc=dc, pbk=pbk: g.tensor_scalar(out=tmp.t[:, dc, :n], in0=pbk.t[:, :n], scalar1=bo_t.t[:, dc:dc + 1],
                        scalar2=sc.t[:, 16 + dc, col:col + 1], op0=ALU.add, op1=ALU.mult), [pbk, bo_t, sc], [tmp])
                self.op("pool", lambda g, dc=dc: g.tensor_tensor(out=xt.t[:, dc, :n], in0=xt.t[:, dc, :n], in1=tmp.t[:, dc, :n], op=ALU.add),
                        [xt, tmp], [xt])
            self.rstd_of(xt, n, sq, rstd)
            self.norm_mod(xt, n, rstd, tmp, h2, sc, 24, 32, col)
            for j in range(32):
                pbk = self.bank()
                for k in range(8):
                    self.op("pe", lambda g, k=k, j=j, pbk=pbk: g.matmul(pbk.t[:, :n], w1.t[:, k, j * 128:(j + 1) * 128], h2.t[:, k, :n],
                                                                        start=(k == 0), stop=(k == 7)), [w1, h2], [pbk], pe_acc=True)
                r = rl[j % 2]
                self.op("act", lambda g, pbk=pbk, r=r: g.activation(out=r.t[:, :n], in_=pbk.t[:, :n], func=AF.Relu), [pbk], [r])
                e = "dve" if j % 2 == 0 else "pool"
                self.op(e, lambda g, j=j, r=r: g.tensor_tensor(out=h1.t[:, j, :n], in0=r.t[:, :n], in1=r.t[:, :n], op=ALU.mult), [r], [h1])
            for dc in range(8):
                pbk = self.bank()
                for j in range(32):
                    self.op("pe", lambda g, j=j, dc=dc, pbk=pbk: g.matmul(pbk.t[:, :n], w2.t[:, j, dc * 128:(dc + 1) * 128], h1.t[:, j, :n],
                                                                         start=(j == 0), stop=(j == 31)), [w2, h1], [pbk], pe_acc=True)
                self.op("dve", lambda g, dc=dc, pbk=pbk: g.scalar_tensor_tensor(out=xt.t[:, dc, :n], in0=pbk.t[:, :n],
                        scalar=sc.t[:, 40 + dc, col:col + 1], in1=xt.t[:, dc, :n], op0=ALU.mult, op1=ALU.add), [pbk, sc, xt], [xt])
            if not last:
                self.dma("sp", xo[:, :, t0:t0 + n], xt.t[:, :, :n], reads=[xt], writes=[x_dst], owner=x_dst)
                if col == 0 or next_ctx:
                    self.rstd_of(xt, n, sq, rstd)
                    self.norm_mod(xt, n, rstd, tmp, hn, self.sc[l + 1], 0, 8, col)
                    self.dma("act", hv[:, :, t0:t0 + n], hn.t[:, :, :n], reads=[hn], writes=[self.H], owner=self.H)
            elif col == 0:
                self.rstd_of(xt, n, sq, rstd)
                for dc in range(8):
                    self.op("dve", lambda g, dc=dc: g.tensor_tensor(out=tmp.t[:, dc, :n], in0=xt.t[:, dc, :n], in1=rstd.t[:, :n], op=ALU.mult),
                            [xt, rstd], [tmp])
                    self.op("act", lambda g, dc=dc: g.activation(out=on.t[:, dc, :n], in_=tmp.t[:, dc, :n], func=AF.Copy,
                                                                 scale=self.fg.t[:, dc, 0:1]), [tmp, self.fg], [on])
                self.dma("sp", ov[:, :, t0:t0 + n], on.t[:, :, :n], reads=[on], writes=[self.OUT], owner=self.OUT)
        self.end_phase()

    def proj(self, pbk, w, c0, msz, act, kcs, asl, n):
        for i, k in enumerate(kcs):
            self.op("pe", lambda g, k=k, i=i: g.matmul(pbk.t[:msz, :n], w.t[:, k, c0:c0 + msz], act.t[:, k, asl],
                                                       start=(i == 0), stop=(i == len(kcs) - 1)), [w, act], [pbk], pe_acc=True)

    def rstd_small(self, src, nch, n, sq, rstd, dim):
        pbk = self.pb[7]
        for c in range(nch):
            self.op("act", lambda g, c=c: g.activation(out=sq.t[:, c, :n], in_=src.t[:, c, :n], func=AF.Square), [src], [sq])
        for c in range(nch):
            self.op("pe", lambda g, c=c: g.matmul(pbk.t[:, :n], self.ones.t[:, :], sq.t[:, c, :n], start=(c == 0), stop=(c == nch - 1)),
                    [self.ones, sq], [pbk], pe_acc=True)
        self.op("dve", lambda g: g.tensor_scalar(out=rstd.t[:, :n], in0=pbk.t[:, :n], scalar1=1.0 / dim, scalar2=self.cfg.EPS,
                                                  op0=ALU.mult, op1=ALU.add), [pbk], [rstd])
        self.op("act", lambda g: g.activation(out=rstd.t[:, :n], in_=rstd.t[:, :n], func=AF.Sqrt), [rstd], [rstd])
        self.op("dve", lambda g: g.reciprocal(out=rstd.t[:, :n], in_=rstd.t[:, :n]), [rstd], [rstd])

    def make_rope_perm(self, wp, w, nk, c0):
        for grp, src, sign in ((0, 1, -1.0), (1, 0, 1.0), (2, 3, -1.0), (3, 2, 1.0)):
            self.op("act", lambda g, grp=grp, src=src, sign=sign: g.activation(
                out=wp.t[:, 0:nk, grp * 16:(grp + 1) * 16], in_=w.t[:, 0:nk, c0 + src * 16:c0 + (src + 1) * 16],
                func=AF.Copy, scale=sign), [w], [wp])

    def phase_mla(self, w_down, qnorm, w_uq, kvnorm, w_ukv, cos_t, sin_t):
        cfg = self.cfg
        SEQ, NTOK = cfg.SEQ, cfg.NTOK
        NH = 8
        QB = 512
        self.begin_phase()
        wd = self.sb("m_wd", [128, 8, 704], BF16)
        wdp = self.sb("m_wdp", [128, 8, 64], BF16)
        wuq = self.sb("m_wuq", [128, 3, 1536], BF16)
        wuqp = self.sb("m_wuqp", [128, 3, 8, 64], BF16)
        wukv = self.sb("m_wukv", [128, 2, 2048], BF16)
        qn_g = self.sb("m_qng", [128, 3], F32)
        kvn_g = self.sb("m_kvng", [128, 2], F32)
        cqn_d = self.dram("m_cqn_d", [128, 3, SEQ], BF16)
        ckvn = self.sb("m_ckvn", [128, 2, NTOK], BF16)
        krT = self.sb("m_krT", [64, NTOK], BF16)
        self.begin_phase()
        stg = [self.sb("m_stg%d" % i, [128, 1024], F32) for i in range(2)]
        self.dma("sp", qn_g.t[:], qnorm.t.ap(), writes=[qn_g])
        self.dma("sp", kvn_g.t[:], kvnorm.t.ap(), writes=[kvn_g])
        wv = w_down.t.ap().rearrange("(k p) n -> p k n", p=128)
        for k in range(8):
            st = stg[k % 2]
            self.dma("sp" if k % 2 == 0 else "act", st.t[:, 0:704], wv[:, k, :], writes=[st])
            self.copy(ENG3[k % 3], wd, wd.t[:, k, :], st, st.t[:, 0:704])
        self.load_cast(w_uq.t.ap(), 384, 1536, wuq, [self.sb("m_stg2%d" % i, [128, 1536], F32) for i in range(2)])
        self.load_cast(w_ukv.t.ap(), 256, 2048, wukv, [self.sb("m_stg3%d" % i, [128, 2048], F32) for i in range(2)])
        self.make_rope_perm(wdp, wd, 8, 640)
        for hd in range(NH):
            for grp, src, sign in ((0, 1, -1.0), (1, 0, 1.0), (2, 3, -1.0), (3, 2, 1.0)):
                c0 = hd * 192 + 128
                self.op("act", lambda g, grp=grp, src=src, sign=sign, c0=c0, hd=hd: g.activation(
                    out=wuqp.t[:, :, hd, grp * 16:(grp + 1) * 16], in_=wuq.t[:, :, c0 + src * 16:c0 + (src + 1) * 16],
                    func=AF.Copy, scale=sign), [wuq], [wuqp])
        self.end_phase()
        self.begin_phase()
        hts = [self.sb("m_h%d" % i, [128, 8, QB], BF16) for i in range(2)]
        craw = self.sb("m_craw", [128, 5, QB], F32)
        sq = self.sb("m_sq", [128, 3, QB], BF16)
        rstd = self.sb("m_rstd", [128, QB], F32)
        tmp = self.sb("m_tmp", [128, 3, QB], F32)
        cqb = [self.sb("m_cqb%d" % i, [128, 3, QB], BF16) for i in range(2)]
        cs_t = [self.sb("m_cos%d" % i, [64, QB], F32) for i in range(2)]
        sn_t = [self.sb("m_sin%d" % i, [64, QB], F32) for i in range(2)]
        kr_a = self.sb("m_kra", [64, QB], F32)
        kr_b = self.sb("m_krb", [64, QB], F32)
        hv = self.H.t.ap().rearrange("(k p) t -> p k t", p=128)
        tblocks = [(t0, min(QB, NTOK - t0)) for t0 in range(0, NTOK, QB)]
        for bi, (t0, n) in enumerate(tblocks):
            ht = hts[bi % 2]
            self.dma("sp", ht.t[:, :, :n], hv[:, :, t0:t0 + n], reads=[self.H], writes=[ht])
            cs, sn = cs_t[bi % 2], sn_t[bi % 2]
            self.dma("act", cs.t[:, :n], cos_t.t.ap()[:, t0:t0 + n], writes=[cs])
            self.dma("act", sn.t[:, :n], sin_t.t.ap()[:, t0:t0 + n], writes=[sn])
            for oc in range(5):
                pbk = self.bank()
                self.proj(pbk, wd, oc * 128, 128, ht, range(8), slice(0, n), n)
                self.op("act", lambda g, oc=oc, pbk=pbk: g.activation(out=craw.t[:, oc, :n], in_=pbk.t[:, :n], func=AF.Copy), [pbk], [craw])
            pbk = self.bank()
            self.proj(pbk, wd, 640, 64, ht, range(8), slice(0, n), n)
            pb2 = self.bank()
            for i, k in enumerate(range(8)):
                self.op("pe", lambda g, k=k, i=i: g.matmul(pb2.t[:64, :n], wdp.t[:, k, :], ht.t[:, k, 0:n], start=(i == 0), stop=(i == 7)),
                        [wdp, ht], [pb2], pe_acc=True)
            self.op("dve", lambda g: g.tensor_tensor(out=kr_a.t[:, :n], in0=pbk.t[:64, :n], in1=cs.t[:, :n], op=ALU.mult), [pbk, cs], [kr_a])
            self.op("dve", lambda g: g.tensor_tensor(out=kr_b.t[:, :n], in0=pb2.t[:64, :n], in1=sn.t[:, :n], op=ALU.mult), [pb2, sn], [kr_b])
            self.op("pool", lambda g: g.tensor_tensor(out=krT.t[:, t0:t0 + n], in0=kr_a.t[:, :n], in1=kr_b.t[:, :n], op=ALU.add), [kr_a, kr_b], [krT])
            if t0 < SEQ:
                self.rstd_small(craw, 3, n, sq, rstd, 384.0)
                for c in range(3):
                    self.op("dve", lambda g, c=c: g.tensor_tensor(out=tmp.t[:, c, :n], in0=craw.t[:, c, :n], in1=rstd.t[:, :n], op=ALU.mult), [craw, rstd], [tmp])
                    self.op("act", lambda g, c=c: g.activation(out=cqb[bi % 2].t[:, c, :n], in_=tmp.t[:, c, :n], func=AF.Copy, scale=qn_g.t[:, c:c + 1]), [tmp, qn_g], [cqb[bi % 2]])
                self.dma("sp", cqn_d.t.ap()[:, :, t0:t0 + n], cqb[bi % 2].t[:, :, :n], reads=[cqb[bi % 2]], writes=[cqn_d], owner=cqn_d)
            pb7 = self.pb[7]
            for c in range(2):
                self.op("act", lambda g, c=c: g.activation(out=sq.t[:, c, :n], in_=craw.t[:, 3 + c, :n], func=AF.Square), [craw], [sq])
            for c in range(2):
                self.op("pe", lambda g, c=c: g.matmul(pb7.t[:, :n], self.ones.t[:, :], sq.t[:, c, :n], start=(c == 0), stop=(c == 1)), [self.ones, sq], [pb7], pe_acc=True)
            self.op("dve", lambda g: g.tensor_scalar(out=rstd.t[:, :n], in0=pb7.t[:, :n], scalar1=1.0 / 256.0, scalar2=cfg.EPS, op0=ALU.mult, op1=ALU.add), [pb7], [rstd])
            self.op("act", lambda g: g.activation(out=rstd.t[:, :n], in_=rstd.t[:, :n], func=AF.Sqrt), [rstd], [rstd])
            self.op("dve", lambda g: g.reciprocal(out=rstd.t[:, :n], in_=rstd.t[:, :n]), [rstd], [rstd])
            for c in range(2):
                self.op("dve", lambda g, c=c: g.tensor_tensor(out=tmp.t[:, c, :n], in0=craw.t[:, 3 + c, :n], in1=rstd.t[:, :n], op=ALU.mult), [craw, rstd], [tmp])
                self.op("act", lambda g, c=c: g.activation(out=ckvn.t[:, c, t0:t0 + n], in_=tmp.t[:, c, :n], func=AF.Copy, scale=kvn_g.t[:, c:c + 1]), [tmp, kvn_g], [ckvn])
        self.end_phase()
        NKB = NTOK // 128
        cql = [self.sb("m_cql%d" % i, [128, 3, QB], BF16) for i in range(2)]
        qn = self.sb("m_qn", [128, SEQ], BF16)
        qr = self.sb("m_qr", [64, SEQ], BF16)
        kn = self.sb("m_kn", [128, NTOK], BF16)
        vt = self.sb("m_v", [128, NKB, 128], BF16)
        sqh = self.sb("m_sqh", [128, 2, QB], BF16)
        cs_t = [self.sb("m_cos%d" % i, [64, QB], F32) for i in range(2)]
        sn_t = [self.sb("m_sin%d" % i, [64, QB], F32) for i in range(2)]
        ra = self.sb("m_ra", [64, QB], F32)
        rb = self.sb("m_rb", [64, QB], F32)
        bmax = self.sb("m_bmax", [128, 1], F32)
        qmax = self.sb("m_qmax", [128, 1], F32)
        kmax = self.sb("m_kmax", [128, 1], F32)
        negm = self.sb("m_negm", [128, 1], F32)
        pts = [self.sb("m_pt%d" % i, [128, QB], BF16) for i in range(4)]
        rec = self.sb("m_rec", [128, QB], F32)
        ao = [self.sb("m_ao%d" % i, [128, QB], BF16) for i in range(2)]
        scale = (128 + 64) ** -0.5
        sb_s = [self.pb[0], self.pb[1], self.pb[2]]
        sb_o = [self.pb[3], self.pb[4]]
        sb_l = [self.pb[5], self.pb[6]]
        pb7 = self.pb[7]
        qblocks = [(t0, min(QB, SEQ - t0)) for t0 in range(0, SEQ, QB)]
        for hd in range(NH):
            self.op("pool", lambda g: g.memset(qmax.t[:], 0.0), [], [qmax])
            self.op("pool", lambda g: g.memset(kmax.t[:], 0.0), [], [kmax])
            for bi, (t0, n) in enumerate(qblocks):
                cs, sn = cs_t[bi % 2], sn_t[bi % 2]
                self.dma("act", cs.t[:, :n], cos_t.t.ap()[:, t0:t0 + n], writes=[cs])
                self.dma("act", sn.t[:, :n], sin_t.t.ap()[:, t0:t0 + n], writes=[sn])
                cqn = cql[bi % 2]
                self.dma("sp", cqn.t[:, :, :n], cqn_d.t.ap()[:, :, t0:t0 + n], reads=[cqn_d], writes=[cqn])
                pbk = self.bank()
                self.proj(pbk, wuq, hd * 192, 128, cqn, range(3), slice(0, n), n)
                self.op("act", lambda g, pbk=pbk: g.activation(out=qn.t[:, t0:t0 + n], in_=pbk.t[:, :n], func=AF.Copy), [pbk], [qn])
                pbk = self.bank()
                self.proj(pbk, wuq, hd * 192 + 128, 64, cqn, range(3), slice(0, n), n)
                pb2 = self.bank()
                for k in range(3):
                    self.op("pe", lambda g, k=k: g.matmul(pb2.t[:64, :n], wuqp.t[:, k, hd, :], cqn.t[:, k, 0:n], start=(k == 0), stop=(k == 2)),
                            [wuqp, cqn], [pb2], pe_acc=True)
                self.op("dve", lambda g, pbk=pbk: g.tensor_tensor(out=ra.t[:, :n], in0=pbk.t[:64, :n], in1=cs.t[:, :n], op=ALU.mult), [pbk, cs], [ra])
                self.op("dve", lambda g: g.tensor_tensor(out=rb.t[:, :n], in0=pb2.t[:64, :n], in1=sn.t[:, :n], op=ALU.mult), [pb2, sn], [rb])
                self.op("pool", lambda g: g.tensor_tensor(out=qr.t[:, t0:t0 + n], in0=ra.t[:, :n], in1=rb.t[:, :n], op=ALU.add), [ra, rb], [qr])
                self.op("act", lambda g: g.activation(out=sqh.t[:, 0, :n], in_=qn.t[:, t0:t0 + n], func=AF.Square), [qn], [sqh])
                self.op("act", lambda g: g.activation(out=sqh.t[:64, 1, :n], in_=qr.t[:, t0:t0 + n], func=AF.Square), [qr], [sqh])
                self.op("pe", lambda g: g.matmul(pb7.t[:, :n], self.ones.t[:, :], sqh.t[:, 0, :n], start=True, stop=False), [self.ones, sqh], [pb7], pe_acc=True)
                self.op("pe", lambda g: g.matmul(pb7.t[:, :n], self.ones.t[:64, :], sqh.t[:64, 1, :n], start=False, stop=True), [self.ones, sqh], [pb7], pe_acc=True)
                self.op("dve", lambda g: g.tensor_reduce(out=bmax.t[:], in_=pb7.t[:, :n], axis=AX.X, op=ALU.max), [pb7], [bmax])
                self.op("dve", lambda g: g.tensor_tensor(out=qmax.t[:], in0=qmax.t[:], in1=bmax.t[:], op=ALU.max), [qmax, bmax], [qmax])
            for bi, (t0, n) in enumerate(tblocks):
                pbk = self.bank()
                self.proj(pbk, wukv, hd * 256, 128, ckvn, range(2), slice(t0, t0 + n), n)
                self.op("act", lambda g, pbk=pbk: g.activation(out=kn.t[:, t0:t0 + n], in_=pbk.t[:, :n], func=AF.Copy), [pbk], [kn])
                self.op("act", lambda g: g.activation(out=sqh.t[:, 0, :n], in_=kn.t[:, t0:t0 + n], func=AF.Square), [kn], [sqh])
                self.op("act", lambda g: g.activation(out=sqh.t[:64, 1, :n], in_=krT.t[:, t0:t0 + n], func=AF.Square), [krT], [sqh])
                self.op("pe", lambda g: g.matmul(pb7.t[:, :n], self.ones.t[:, :], sqh.t[:, 0, :n], start=True, stop=False), [self.ones, sqh], [pb7], pe_acc=True)
                self.op("pe", lambda g: g.matmul(pb7.t[:, :n], self.ones.t[:64, :], sqh.t[:64, 1, :n], start=False, stop=True), [self.ones, sqh], [pb7], pe_acc=True)
                self.op("dve", lambda g: g.tensor_reduce(out=bmax.t[:], in_=pb7.t[:, :n], axis=AX.X, op=ALU.max), [pb7], [bmax])
                self.op("dve", lambda g: g.tensor_tensor(out=kmax.t[:], in0=kmax.t[:], in1=bmax.t[:], op=ALU.max), [kmax, bmax], [kmax])
                pbk = self.bank()
                nkb = n // 128
                for j in range(nkb):
                    kb = t0 // 128 + j
                    for k in range(2):
                        self.op("pe", lambda g, k=k, j=j, kb=kb: g.matmul(pbk.t[:, j * 128:(j + 1) * 128], ckvn.t[:, k, kb * 128:(kb + 1) * 128],
                                wukv.t[:, k, hd * 256 + 128:hd * 256 + 256], start=(k == 0), stop=(k == 1)), [ckvn, wukv], [pbk], pe_acc=True)
                self.op("dve", lambda g, pbk=pbk, nkb=nkb: g.tensor_copy(out=vt.t[:, t0 // 128:t0 // 128 + nkb, :],
                        in_=pbk.t[:, :nkb * 128].rearrange("p (j d) -> p j d", d=128)), [pbk], [vt])
            self.op("dve", lambda g: g.tensor_tensor(out=negm.t[:], in0=qmax.t[:], in1=kmax.t[:], op=ALU.mult), [qmax, kmax], [negm])
            self.op("act", lambda g: g.activation(out=negm.t[:], in_=negm.t[:], func=AF.Sqrt), [negm], [negm])
            self.op("dve", lambda g: g.tensor_scalar(out=negm.t[:], in0=negm.t[:], scalar1=-1.02 * scale, scalar2=None, op0=ALU.mult), [negm], [negm])
            mv = self.M.t.ap()
            it = 0
            for qi, (q0, nq) in enumerate(qblocks):
                po = sb_o[qi % 2]
                pl = sb_l[qi % 2]
                for kb in range(NKB):
                    psb = sb_s[it % 3]
                    pt = pts[it % 4]
                    it += 1
                    self.op("pe", lambda g, kb=kb, psb=psb: g.matmul(psb.t[:, :nq], kn.t[:, kb * 128:(kb + 1) * 128], qn.t[:, q0:q0 + nq], start=True, stop=False),
                            [kn, qn], [psb], pe_acc=False)
                    self.op("pe", lambda g, kb=kb, psb=psb: g.matmul(psb.t[:, :nq], krT.t[:, kb * 128:(kb + 1) * 128], qr.t[:, q0:q0 + nq], start=False, stop=True),
                            [krT, qr], [psb], pe_acc=True)
                    self.op("act", lambda g, psb=psb, pt=pt: g.activation(out=pt.t[:, :nq], in_=psb.t[:, :nq], func=AF.Exp, scale=scale, bias=negm.t[:, 0:1]),
                            [psb, negm], [pt])
                    self.op("pe", lambda g, kb=kb, pt=pt: g.matmul(po.t[:, :nq], vt.t[:, kb, :], pt.t[:, :nq], start=(kb == 0), stop=(kb == NKB - 1)),
                            [vt, pt], [po], pe_acc=(kb > 0))
                    self.op("pe", lambda g, kb=kb, pt=pt: g.matmul(pl.t[:, :nq], self.ones.t[:, :], pt.t[:, :nq], start=(kb == 0), stop=(kb == NKB - 1)),
                            [self.ones, pt], [pl], pe_acc=(kb > 0))
                self.op("dve", lambda g: g.reciprocal(out=rec.t[:, :nq], in_=pl.t[:, :nq]), [pl], [rec])
                a = ao[qi % 2]
                self.op("dve", lambda g, a=a: g.tensor_tensor(out=a.t[:, :nq], in0=po.t[:, :nq], in1=rec.t[:, :nq], op=ALU.mult), [po, rec], [a])
                self.dma("sp", mv[hd * 128:(hd + 1) * 128, q0:q0 + nq], a.t[:, :nq], reads=[a], writes=[self.M], owner=self.M)
        self.end_phase()


def rope_tables(SEQ, NTOK):
    t = np.arange(SEQ)
    row = (t // 64).astype(np.float32)
    colp = (t % 64).astype(np.float32)
    inv_freq = (10000.0 ** (-np.arange(0, 32, 2, dtype=np.float32) / 32)).astype(np.float32)
    ar = row[None, :] * inv_freq[:, None]
    ac = colp[None, :] * inv_freq[:, None]
    ang = np.concatenate([ar, ar, ac, ac], axis=0)
    cos = np.ones((64, NTOK), np.float32)
    sin = np.zeros((64, NTOK), np.float32)
    cos[:, :SEQ] = np.cos(ang)
    sin[:, :SEQ] = np.sin(ang)
    return cos, sin


def _gla_phase(self, w_in, gk_w2, gk_b, onorm, masks, segmask_in):
    cfg = self.cfg
    SEQ, NTOK = cfg.SEQ, cfg.NTOK
    NCH = NTOK // 128
    self.begin_phase()
    stg = [self.sb("g_stg%d" % i, [128, 3104], F32) for i in range(2)]
    win = self.sb("g_win", [128, 8, 3104], BF16)
    wv = w_in.t.ap().rearrange("(k p) n -> p k n", p=128)
    for k in range(8):
        st = stg[k % 2]
        self.dma("sp" if k % 2 == 0 else "act", st.t[:, :], wv[:, k, :], writes=[st])
        self.copy(ENG3[k % 3], win, win.t[:, k, :], st, st.t[:, :])
    w2s = self.sb("g_w2s", [16, 2, 512], F32)
    w2 = self.sb("g_w2", [16, 2, 512], BF16)
    self.dma("sp", w2s.t[:], gk_w2.t.ap().rearrange("z r d -> r z d"), writes=[w2s])
    self.copy("dve", w2, w2.t[:], w2s, w2s.t[:])
    gb = self.sb("g_gb", [128, 2, 4], F32)
    self.dma("sp", gb.t[:], gk_b.t.ap(), writes=[gb])
    ngb = self.sb("g_ngb", [128, 2, 4], F32)
    self.op("dve", lambda g: g.tensor_scalar(out=ngb.t[:], in0=gb.t[:], scalar1=-1.0, scalar2=None, op0=ALU.mult), [gb], [ngb])
    ong = self.sb("g_ong", [128, 2], F32)
    self.dma("sp", ong.t[:], onorm.t.ap(), writes=[ong])
    msk = self.sb("g_msk", [128, 2, 4, 128], F32)
    self.dma("sp", msk.t[:], masks.t.ap(), writes=[msk])
    seg = self.sb("g_seg", [128, 512], F32)
    self.dma("sp", seg.t[:], segmask_in.t.ap(), writes=[seg])
    ident = self.ident
    S = self.sb("g_S", [128, 4, 256], F32)
    Sbf = [self.sb("g_Sbf%d" % i, [128, 4, 256], BF16) for i in range(2)]
    OF = self.dram("g_OF", [128, 8, NTOK], F32)

    def T2(name, shape, dt):
        return [self.sb("g_%s%d" % (name, i), shape, dt) for i in range(2)]
    hT = T2("hT", [128, 8, 128], BF16)
    r_sb = T2("r", [16, 128], BF16)
    e1 = T2("e1", [128, 4, 128], F32)
    ls = T2("ls", [128, 4, 128], F32)
    cs = T2("cs", [128, 4, 128], F32)
    tmp = T2("tmp", [128, 4, 128], F32)
    Ep = T2("Ep", [128, 4, 128], F32)
    Em = T2("Em", [128, 4, 128], F32)
    dec = T2("dec", [128, 4], F32)
    qt = T2("qt", [128, 4, 128], BF16)
    kt32 = T2("kt32", [128, 4, 128], F32)
    ktb = T2("ktb", [128, 4, 128], BF16)
    attm = T2("attm", [128, 4, 128], BF16)
    vsb = T2("v", [128, 1024], BF16)
    ktok = T2("ktok", [128, 4, 128], BF16)
    osb = T2("o", [128, 8, 128], F32)
    ofin = T2("ofin", [128, 8, 128], F32)
    sq = T2("sq", [128, 8, 128], BF16)
    rstd = T2("rstd", [128, 4, 128], F32)
    ogs = T2("ogs", [128, 8, 128], F32)
    fin = T2("fin", [128, 8, 128], BF16)
    ssum = T2("ssum", [128, 4, 256], F32)
    hv = self.H.t.ap().rearrange("(k p) t -> p k t", p=128)
    mv = self.M.t.ap().rearrange("(k p) t -> p k t", p=128)
    qscale_ln = math.log(128.0 ** -0.5)
    lat = list(range(0, SEQ // 128))
    ctxc = list(range(SEQ // 128, NCH))
    orders = [ctxc + lat, list(range(NCH - 1, -1, -1))]
    pbs = self.pb
    it = 0
    CUT = int(os.environ.get("GLA_CUT", "0"))
    for d in range(2):
        if CUT in (5, 6) and d == 1:
            break
        self.op("pool", lambda g: g.memset(S.t[:], 0.0), [], [S])
        self.op("pool", lambda g: g.memset(Sbf[it % 2].t[:], 0.0), [], [Sbf[it % 2]])
        for c in orders[d]:
            i = it % 2
            t0 = c * 128
            h_t = hT[i]
            self.dma("sp", h_t.t[:], hv[:, :, t0:t0 + 128], reads=[self.H], writes=[h_t])
            bq, bk, br, bg, bv0, bv1 = pbs[0], pbs[1], pbs[2], pbs[3], pbs[4], pbs[5]
            for hd in range(4):
                for k in range(8):
                    self.op("pe", lambda g: g.matmul(bq.t[:, hd * 128:(hd + 1) * 128], win.t[:, k, hd * 128:(hd + 1) * 128], h_t.t[:, k, :],
                                                      start=(k == 0), stop=(k == 7)), [win, h_t], [bq], pe_acc=True)
            for hd in range(4):
                for k in range(8):
                    self.op("pe", lambda g: g.matmul(bk.t[:, hd * 128:(hd + 1) * 128], win.t[:, k, 512 + hd * 128:512 + (hd + 1) * 128], h_t.t[:, k, :],
                                                      start=(k == 0), stop=(k == 7)), [win, h_t], [bk], pe_acc=True)
            for k in range(8):
                self.op("pe", lambda g: g.matmul(br.t[:16, :128], win.t[:, k, 3072 + d * 16:3072 + (d + 1) * 16], h_t.t[:, k, :],
                                                  start=(k == 0), stop=(k == 7)), [win, h_t], [br], pe_acc=True)
            self.op("act", lambda g: g.activation(out=r_sb[i].t[:, :], in_=br.t[:16, :128], func=AF.Copy), [br], [r_sb[i]])
            for hd in range(4):
                self.op("pe", lambda g: g.matmul(bg.t[:, hd * 128:(hd + 1) * 128], w2.t[:, d, hd * 128:(hd + 1) * 128], r_sb[i].t[:, :],
                                                  start=True, stop=True), [w2, r_sb[i]], [bg], pe_acc=True)
            for half in range(2):
                bv = (bv0, bv1)[half]
                for k in range(8):
                    self.op("pe", lambda g: g.matmul(bv.t[:, :], h_t.t[:, k, :], win.t[:, k, 1024 + half * 512:1024 + (half + 1) * 512],
                                                      start=(k == 0), stop=(k == 7)), [win, h_t], [bv], pe_acc=True)
                self.op("act", lambda g: g.activation(out=vsb[i].t[:, half * 512:(half + 1) * 512], in_=bv.t[:, :], func=AF.Copy), [bv], [vsb[i]])
            if CUT == 1:
                self.end_phase(); return
            for hd in range(4):
                self.op("act", lambda g: g.activation(out=e1[i].t[:, hd, :], in_=bg.t[:, hd * 128:(hd + 1) * 128], func=AF.Exp, scale=-1.0,
                                                      bias=ngb.t[:, d, hd:hd + 1]), [bg, ngb], [e1[i]])
            one_c = self.ccol(1.0)
            self.op("act", lambda g: g.activation(out=ls[i].t[:], in_=e1[i].t[:], func=AF.Ln, bias=one_c.t[:, 0:1]), [e1[i], one_c], [ls[i]])
            lsf = ls[i].t[:].rearrange("p h t -> p (h t)")
            csf = cs[i].t[:].rearrange("p h t -> p (h t)")
            self.op("dve", lambda g: g.tensor_tensor_scan(out=csf, data0=seg.t[:, :], data1=lsf, initial=0.0, op0=ALU.mult, op1=ALU.add),
                    [seg, ls[i]], [cs[i]])
            if d == 0:
                cum = cs[i]
                last = 127
            else:
                self.op("dve", lambda g: g.tensor_tensor(out=tmp[i].t[:], in0=ls[i].t[:], in1=cs[i].t[:], op=ALU.subtract), [ls[i], cs[i]], [tmp[i]])
                for hd in range(4):
                    self.op("dve", lambda g: g.tensor_scalar(out=tmp[i].t[:, hd, :], in0=tmp[i].t[:, hd, :], scalar1=cs[i].t[:, hd, 127:128],
                                                             scalar2=None, op0=ALU.add), [tmp[i], cs[i]], [tmp[i]])
                cum = tmp[i]
                last = 0
            qs_c = self.ccol(qscale_ln)
            self.op("act", lambda g: g.activation(out=Ep[i].t[:], in_=cum.t[:], func=AF.Exp, scale=-1.0 / 16, bias=qs_c.t[:, 0:1]), [cum, qs_c], [Ep[i]])
            self.op("act", lambda g: g.activation(out=Em[i].t[:], in_=cum.t[:], func=AF.Exp, scale=1.0 / 16), [cum], [Em[i]])
            self.op("act", lambda g: g.activation(out=dec[i].t[:, :], in_=cum.t[:, :, last], func=AF.Exp, scale=-1.0 / 16), [cum], [dec[i]])
            if CUT == 2:
                self.end_phase(); return
            bqv = bq.t[:, :].rearrange("p (h t) -> p h t", h=4)
            bkv = bk.t[:, :].rearrange("p (h t) -> p h t", h=4)
            self.op("dve", lambda g: g.tensor_tensor(out=qt[i].t[:], in0=bqv, in1=Ep[i].t[:], op=ALU.mult), [bq, Ep[i]], [qt[i]])
            self.op("dve", lambda g: g.tensor_tensor(out=kt32[i].t[:], in0=bkv, in1=Em[i].t[:], op=ALU.mult), [bk, Em[i]], [kt32[i]])
            self.op("act", lambda g: g.activation(out=ktb[i].t[:], in_=kt32[i].t[:], func=AF.Copy), [kt32[i]], [ktb[i]])
            ba, bt = pbs[6], pbs[7]
            for hd in range(4):
                self.op("pe", lambda g: g.matmul(ba.t[:, hd * 128:(hd + 1) * 128], ktb[i].t[:, hd, :], qt[i].t[:, hd, :], start=True, stop=True),
                        [ktb[i], qt[i]], [ba], pe_acc=True)
            for hd in range(4):
                self.op("pe", lambda g: g.transpose(out=bt.t[:, hd * 128:(hd + 1) * 128], in_=kt32[i].t[:, hd, :], identity=ident.t[:, :]),
                        [kt32[i], ident], [bt], pe_acc=True)
            bav = ba.t[:, :].rearrange("p (h t) -> p h t", h=4)
            self.op("dve", lambda g: g.tensor_tensor(out=attm[i].t[:], in0=bav, in1=msk.t[:, d, :, :], op=ALU.mult), [ba, msk], [attm[i]])
            self.op("act", lambda g: g.activation(out=ktok[i].t[:].rearrange("p h t -> p (h t)"), in_=bt.t[:, :], func=AF.Copy), [bt], [ktok[i]])
            if CUT == 3:
                self.end_phase(); return
            sb_cur = Sbf[it % 2]
            sb_nxt = Sbf[(it + 1) % 2]
            bo = (pbs[0], pbs[1])
            for hd in range(4):
                for j in range(2):
                    b_o = bo[hd // 2]
                    col = ((hd % 2) * 2 + j) * 128
                    self.op("pe", lambda g: g.matmul(b_o.t[:, col:col + 128], vsb[i].t[:, hd * 256 + j * 128:hd * 256 + (j + 1) * 128], attm[i].t[:, hd, :],
                                                      start=True, stop=False), [vsb[i], attm[i]], [b_o], pe_acc=True)
                    self.op("pe", lambda g: g.matmul(b_o.t[:, col:col + 128], sb_cur.t[:, hd, j * 128:(j + 1) * 128], qt[i].t[:, hd, :],
                                                      start=False, stop=True), [sb_cur, qt[i]], [b_o], pe_acc=True)
            for half in range(2):
                self.op("act" if half == 0 else "dve", (lambda g: g.activation(out=osb[i].t[:, half * 4:(half + 1) * 4, :].rearrange("p j t -> p (j t)"),
                        in_=bo[half].t[:, :], func=AF.Copy)) if half == 0 else
                        (lambda g: g.tensor_copy(out=osb[i].t[:, half * 4:(half + 1) * 4, :].rearrange("p j t -> p (j t)"), in_=bo[half].t[:, :])),
                        [bo[half]], [osb[i]])
            bs = (pbs[2], pbs[3])
            for hd in range(4):
                b_s = bs[hd // 2]
                col = (hd % 2) * 256
                self.op("pe", lambda g: g.matmul(b_s.t[:, col:col + 256], ktok[i].t[:, hd, :], vsb[i].t[:, hd * 256:(hd + 1) * 256], start=True, stop=True),
                        [ktok[i], vsb[i]], [b_s], pe_acc=True)
            for half in range(2):
                self.op("dve", lambda g: g.tensor_tensor(out=ssum[i].t[:, half * 2:(half + 1) * 2, :].rearrange("p h v -> p (h v)"),
                        in0=bs[half].t[:, :], in1=S.t[:, half * 2:(half + 1) * 2, :].rearrange("p h v -> p (h v)"), op=ALU.add), [bs[half], S], [ssum[i]])
            for hd in range(4):
                self.op("dve", lambda g: g.tensor_scalar(out=S.t[:, hd, :], in0=ssum[i].t[:, hd, :], scalar1=dec[i].t[:, hd:hd + 1], scalar2=None,
                                                         op0=ALU.mult), [ssum[i], dec[i]], [S])
            self.op("act", lambda g: g.activation(out=sb_nxt.t[:], in_=S.t[:], func=AF.Copy), [S], [sb_nxt])
            if CUT == 4:
                self.end_phase(); return
            ofv = OF.t.ap()
            if d == 0:
                if CUT != 6:
                    self.dma("act", ofv[:, :, t0:t0 + 128], osb[i].t[:], reads=[osb[i]], writes=[OF], owner=OF)
                if CUT == 7:
                    self.end_phase(); return
            else:
                self.dma("act", ofin[i].t[:], ofv[:, :, t0:t0 + 128], reads=[OF], writes=[ofin[i]])
                self.op("pool", lambda g: g.tensor_tensor(out=osb[i].t[:], in0=osb[i].t[:], in1=ofin[i].t[:], op=ALU.add), [osb[i], ofin[i]], [osb[i]])
                self.op("act", lambda g: g.activation(out=sq[i].t[:], in_=osb[i].t[:], func=AF.Square), [osb[i]], [sq[i]])
                b_n = pbs[4]
                for hd in range(4):
                    for j in range(2):
                        self.op("pe", lambda g: g.matmul(b_n.t[:, hd * 128:(hd + 1) * 128], self.ones.t[:, :], sq[i].t[:, hd * 2 + j, :],
                                                          start=(j == 0), stop=(j == 1)), [self.ones, sq[i]], [b_n], pe_acc=True)
                rf = rstd[i].t[:].rearrange("p h t -> p (h t)")
                self.op("dve", lambda g: g.tensor_scalar(out=rf, in0=b_n.t[:, :], scalar1=1.0 / 256, scalar2=cfg.EPS, op0=ALU.mult, op1=ALU.add), [b_n], [rstd[i]])
                self.op("act", lambda g: g.activation(out=rf, in_=rf, func=AF.Sqrt), [rstd[i]], [rstd[i]])
                self.op("dve", lambda g: g.reciprocal(out=rf, in_=rf), [rstd[i]], [rstd[i]])
                for half in range(2):
                    b_g = (pbs[5], pbs[6])[half]
                    for jj in range(4):
                        ch = half * 4 + jj
                        for k in range(8):
                            self.op("pe", lambda g: g.matmul(b_g.t[:, jj * 128:(jj + 1) * 128], win.t[:, k, 2048 + ch * 128:2048 + (ch + 1) * 128], h_t.t[:, k, :],
                                                              start=(k == 0), stop=(k == 7)), [win, h_t], [b_g], pe_acc=True)
                    self.op("act", lambda g: g.activation(out=ogs[i].t[:, half * 4:(half + 1) * 4, :].rearrange("p j t -> p (j t)"), in_=b_g.t[:, :], func=AF.Silu),
                            [b_g], [ogs[i]])
                for hd in range(4):
                    for j in range(2):
                        ch = hd * 2 + j
                        self.op("dve", lambda g: g.scalar_tensor_tensor(out=osb[i].t[:, ch, :], in0=osb[i].t[:, ch, :], scalar=ong.t[:, j:j + 1],
                                in1=rstd[i].t[:, hd, :], op0=ALU.mult, op1=ALU.mult), [osb[i], ong, rstd[i]], [osb[i]])
                self.op("pool", lambda g: g.tensor_tensor(out=fin[i].t[:], in0=osb[i].t[:], in1=ogs[i].t[:], op=ALU.mult), [osb[i], ogs[i]], [fin[i]])
                self.dma("sp", mv[:, :, t0:t0 + 128], fin[i].t[:], reads=[fin[i]], writes=[self.M], owner=self.M)
            it += 1
            if GLA_BARRIER:
                self.barrier()
    self.end_phase()


Net.phase_gla = _gla_phase


def gla_consts():
    s = np.arange(128)[:, None]
    c = np.arange(128)[None, :]
    mf = (s <= c).astype(np.float32)
    mb = (s >= c).astype(np.float32)
    masks = np.stack([np.broadcast_to(mf[:, None, :], (128, 4, 128)), np.broadcast_to(mb[:, None, :], (128, 4, 128))], axis=1)
    seg = np.ones((128, 512), np.float32)
    seg[:, ::128] = 0.0
    return np.ascontiguousarray(masks), seg, np.eye(128, dtype=np.float32)


def hyena_consts(L):
    N = 2 * L
    N1 = N // 128
    P1 = N1 // 2
    t = np.linspace(0.0, 1.0, L, dtype=np.float32)
    bands = 16
    w = (2.0 * math.pi * np.arange(L, dtype=np.float32) / L).astype(np.float32)
    f = np.linspace(1e-4, bands - 1, bands, dtype=np.float32)
    z = np.concatenate([t[:, None], np.cos(f[None, :] * w[:, None]), -np.sin(f[None, :] * w[:, None])], axis=-1).astype(np.float32)
    idx = np.concatenate([np.arange(L), [0], np.arange(L - 1, 0, -1)])
    z2T = np.ascontiguousarray(z[idx].T)
    t2 = t[idx].copy()
    t2[L] = 0.0
    t2row = np.ascontiguousarray(np.broadcast_to(t2[None, :], (128, N))).astype(np.float32)
    n1 = np.arange(N1)[:, None].astype(np.float64)
    k1 = np.arange(N1)[None, :].astype(np.float64)
    th = 2 * math.pi * n1 * k1 / N1
    Fa = np.concatenate([np.cos(th), -np.sin(th)], axis=1)
    n2 = np.arange(128, dtype=np.float64)
    k2 = np.arange(128, dtype=np.float64)
    Ec = np.zeros((N1, 128, 3, 128), np.float32)
    Ei = np.zeros((N1, 128, 3, 128), np.float32)
    for kk in range(N1):
        thc = 2 * math.pi * n2[:, None] * (kk + N1 * k2[None, :]) / N
        c, s = np.cos(thc), np.sin(thc)
        Ec[kk, :, 0], Ec[kk, :, 1], Ec[kk, :, 2] = c, -s, s
        Ei[kk, :, 0], Ei[kk, :, 1], Ei[kk, :, 2] = c.T, s.T, -s.T
    thg = 2 * math.pi * np.arange(N1)[:, None] * np.arange(P1)[None, :] / N1
    G = np.stack([np.cos(thg) / N, -np.sin(thg) / N], axis=1)
    import ml_dtypes
    bf = ml_dtypes.bfloat16
    return dict(z2T=z2T, t2row=t2row, Fa=Fa.astype(np.float32), Ec=Ec.astype(bf), Ei=Ei.astype(bf), G=G.astype(np.float32))


def hyena_negdelta():
    max_decay = math.log(1e-2) / 0.3
    min_decay = math.log(1e-2) / 1.5
    deltas = np.linspace(min_decay, max_decay, 1024, dtype=np.float32)
    nd = -np.abs(deltas)
    return np.ascontiguousarray(nd.reshape(8, 128).T)


def _hy_inproj(self, w_in, b_in, conv_w, conv_b, Z, with_ctx):
    cfg = self.cfg
    self.begin_phase()
    stg = [self.sb("hi_stg%d" % i, [128, 3072], F32) for i in range(2)]
    win = self.sb("hi_win", [128, 8, 3072], BF16)
    wv = w_in.rearrange("(k p) n -> p k n", p=128)
    for k in range(8):
        st = stg[k % 2]
        self.dma("sp" if k % 2 == 0 else "act", st.t[:, :], wv[:, k, :], writes=[st])
        self.copy(ENG3[k % 3], win, win.t[:, k, :], st, st.t[:, :])
    bi = self.sb("hi_bi", [128, 24], F32)
    cb = self.sb("hi_cb", [128, 24], F32)
    cw = self.sb("hi_cw", [128, 3, 24], F32)
    self.dma("sp", bi.t[:], b_in, writes=[bi])
    self.dma("sp", cb.t[:], conv_b, writes=[cb])
    self.dma("sp", cw.t[:], conv_w, writes=[cw])
    TBH = 510
    hts = [self.sb("hi_h%d" % i, [128, 8, 512], BF16) for i in range(2)]
    zp = [self.sb("hi_zp%d" % i, [128, 512], F32) for i in range(3)]
    zo = [self.sb("hi_zo%d" % i, [128, 24, TBH], F32) for i in range(2)]
    hv = self.H.t.ap().rearrange("(k p) t -> p k t", p=128)
    zv = Z.t.ap().rearrange("(k p) t -> p k t", p=128)
    seqs = [(0, cfg.SEQ)] + ([(cfg.SEQ, cfg.CTX)] if with_ctx else [])
    it = 0
    for (s0, sl) in seqs:
        for a in range(0, sl, TBH):
            b = min(a + TBH, sl)
            n = b - a
            lo = max(a - 1, 0)
            hi = min(b + 1, sl)
            off = lo - (a - 1)
            ncol = hi - lo
            ht = hts[it % 2]
            zt = zo[it % 2]
            it += 1
            self.dma("sp", ht.t[:, :, off:off + ncol], hv[:, :, s0 + lo:s0 + hi], reads=[self.H], writes=[ht])
            for ch in range(24):
                pbk = self.bank()
                for k in range(8):
                    self.op("pe", lambda g: g.matmul(pbk.t[:, off:off + ncol], win.t[:, k, ch * 128:(ch + 1) * 128], ht.t[:, k, off:off + ncol],
                                                      start=(k == 0), stop=(k == 7)), [win, ht], [pbk], pe_acc=True)
                z = zp[ch % 3]
                if off == 1:
                    self.op("pool", lambda g: g.memset(z.t[:, 0:1], 0.0), [], [z])
                if hi == sl and (b + 1) > sl:
                    self.op("pool", lambda g: g.memset(z.t[:, n + 1:n + 2], 0.0), [], [z])
                self.op("act", lambda g: g.activation(out=z.t[:, off:off + ncol], in_=pbk.t[:, off:off + ncol], func=AF.Identity,
                                                      bias=bi.t[:, ch:ch + 1]), [pbk, bi], [z])
                self.op("dve", lambda g: g.tensor_scalar(out=zt.t[:, ch, :n], in0=z.t[:, 0:n], scalar1=cw.t[:, 0, ch:ch + 1],
                        scalar2=cb.t[:, ch:ch + 1], op0=ALU.mult, op1=ALU.add), [z, cw, cb], [zt])
                self.op("dve", lambda g: g.scalar_tensor_tensor(out=zt.t[:, ch, :n], in0=z.t[:, 1:n + 1], scalar=cw.t[:, 1, ch:ch + 1],
                        in1=zt.t[:, ch, :n], op0=ALU.mult, op1=ALU.add), [z, cw, zt], [zt])
                self.op("dve", lambda g: g.scalar_tensor_tensor(out=zt.t[:, ch, :n], in0=z.t[:, 2:n + 2], scalar=cw.t[:, 2, ch:ch + 1],
                        in1=zt.t[:, ch, :n], op0=ALU.mult, op1=ALU.add), [z, cw, zt], [zt])
            self.dma("act", zv[:, :, s0 + a:s0 + b], zt.t[:, :, :n], reads=[zt], writes=[Z], owner=Z)
    self.end_phase()


def _hy_filter(self, L, hc, fw, Hf, skip):
    N = 2 * L
    BS = min(512, L)
    NB = N // BS
    self.begin_phase()
    w1 = self.sb("hf_w1", [33, 64], F32)
    w2 = self.sb("hf_w2", [64, 64], F32)
    w3 = self.sb("hf_w3", [64, 64], F32)
    w4 = self.sb("hf_w4", [64, 4096], F32)
    fr = self.sb("hf_fr", [64, 1], F32)
    bs = self.sb("hf_bs", [64, 3], F32)
    fb = self.sb("hf_fb", [64, 3], F32)
    self.dma("sp", w1.t[:], fw["w1"], writes=[w1])
    self.dma("sp", w2.t[:], fw["w2"], writes=[w2])
    self.dma("sp", w3.t[:], fw["w3"], writes=[w3])
    self.dma("sp", w4.t[:], fw["w4"], writes=[w4])
    self.dma("sp", fr.t[:], fw["freq"], writes=[fr])
    for i, nm in enumerate(("b1", "b2", "b3")):
        self.dma("sp", bs.t[:, i:i + 1], fw[nm], writes=[bs])
    self.op("dve", lambda g: g.tensor_scalar(out=fb.t[:], in0=bs.t[:], scalar1=fr.t[:, 0:1], scalar2=16 * math.pi, op0=ALU.mult, op1=ALU.add), [bs, fr], [fb])
    nd = self.sb("hf_nd", [128, 8], F32)
    self.dma("sp", nd.t[:], hc["negdelta"].t.ap(), writes=[nd])
    sk = self.sb("hf_sk", [128, 2, 8], F32)
    self.dma("sp", sk.t[:], skip, writes=[sk])
    qq = self.sb("hf_qq", [64, 512], F32)
    ki = self.sb("hf_ki", [64, 512], mybir.dt.int32)
    h3T = self.sb("hf_h3T", [64, N], F32)
    zt = [self.sb("hf_z%d" % i, [33, 512], F32) for i in range(2)]
    aa = [self.sb("hf_a%d" % i, [64, 512], F32) for i in range(2)]
    ss = [self.sb("hf_s%d" % i, [64, 512], F32) for i in range(2)]
    ws = (w1, w2, w3)
    for blk in range(NB):
        z = zt[blk % 2]
        self.dma("sp", z.t[:, :BS], hc["z2T"].t.ap()[:, blk * BS:(blk + 1) * BS], writes=[z])
        src = z
        for li in range(3):
            pbk = self.bank()
            kk = 33 if li == 0 else 64
            self.op("pe", lambda g: g.matmul(pbk.t[:64, :BS], ws[li].t[:kk, :], src.t[:kk, :BS], start=True, stop=True), [ws[li], src], [pbk], pe_acc=True)
            a = aa[li % 2]
            self.op("act", lambda g: g.activation(out=a.t[:, :BS], in_=pbk.t[:64, :BS], func=AF.Identity, scale=fr.t[:, 0:1], bias=fb.t[:, li:li + 1]),
                    [pbk, fr, fb], [a])
            self.op("dve", lambda g: g.tensor_scalar(out=qq.t[:, :BS], in0=a.t[:, :BS], scalar1=1.0 / (2 * math.pi), scalar2=0.5, op0=ALU.mult, op1=ALU.add), [a], [qq])
            self.op("dve", lambda g: g.tensor_copy(out=ki.t[:, :BS], in_=qq.t[:, :BS]), [qq], [ki])
            self.op("dve", lambda g: g.tensor_copy(out=qq.t[:, :BS], in_=ki.t[:, :BS]), [ki], [qq])
            self.op("dve", lambda g: g.scalar_tensor_tensor(out=a.t[:, :BS], in0=qq.t[:, :BS], scalar=-2 * math.pi, in1=a.t[:, :BS], op0=ALU.mult, op1=ALU.add), [qq, a], [a])
            self.op("dve", lambda g: g.tensor_scalar(out=qq.t[:, :BS], in0=a.t[:, :BS], scalar1=-math.pi, scalar2=2 * math.pi, op0=ALU.is_lt, op1=ALU.mult), [a], [qq])
            self.op("pool", lambda g: g.tensor_tensor(out=a.t[:, :BS], in0=a.t[:, :BS], in1=qq.t[:, :BS], op=ALU.add), [a, qq], [a])
            self.op("dve", lambda g: g.tensor_scalar(out=qq.t[:, :BS], in0=a.t[:, :BS], scalar1=math.pi, scalar2=-2 * math.pi, op0=ALU.is_gt, op1=ALU.mult), [a], [qq])
            self.op("pool", lambda g: g.tensor_tensor(out=a.t[:, :BS], in0=a.t[:, :BS], in1=qq.t[:, :BS], op=ALU.add), [a, qq], [a])
            if li < 2:
                dst = ss[li % 2]
                self.op("act", lambda g: g.activation(out=dst.t[:, :BS], in_=a.t[:, :BS], func=AF.Sin), [a], [dst])
                src = dst
            else:
                self.op("act", lambda g: g.activation(out=h3T.t[:, blk * BS:(blk + 1) * BS], in_=a.t[:, :BS], func=AF.Sin), [a], [h3T])
    hraw = self.sb("hf_hraw", [128, N], F32)
    t2 = [self.sb("hf_t2%d" % i, [128, 512], F32) for i in range(2)]
    dc = [self.sb("hf_dc%d" % i, [128, 512], F32) for i in range(2)]
    sab = self.sb("hf_sab", [128, 1], F32)
    hfv = Hf.t.ap()
    it = 0
    for o in range(2):
        for cc in range(8):
            for blk in range(NB):
                d = 0 if blk < NB // 2 else 1
                col0 = o * 2048 + d * 1024 + cc * 128
                pbk = self.bank()
                self.op("pe", lambda g: g.matmul(pbk.t[:, :BS], w4.t[:, col0:col0 + 128], h3T.t[:, blk * BS:(blk + 1) * BS], start=True, stop=True),
                        [w4, h3T], [pbk], pe_acc=True)
                tt = t2[it % 2]
                dd = dc[it % 2]
                it += 1
                self.dma("act", tt.t[:, :BS], hc["t2row"].t.ap()[:, blk * BS:(blk + 1) * BS], writes=[tt])
                self.op("act", lambda g: g.activation(out=dd.t[:, :BS], in_=tt.t[:, :BS], func=AF.Exp, scale=nd.t[:, cc:cc + 1]), [tt, nd], [dd])
                self.op("dve", lambda g: g.tensor_tensor(out=hraw.t[:, blk * BS:(blk + 1) * BS], in0=pbk.t[:, :BS], in1=dd.t[:, :BS], op=ALU.mult),
                        [pbk, dd], [hraw])
            self.op("pool", lambda g: g.memset(hraw.t[:, L:L + 1], 0.0), [], [hraw])
            self.op("dve", lambda g: g.tensor_reduce(out=sab.t[:], in_=hraw.t[:], axis=AX.X, op=ALU.add, apply_absolute_value=True), [hraw], [sab])
            self.op("dve", lambda g: g.reciprocal(out=sab.t[:], in_=sab.t[:]), [sab], [sab])
            self.op("dve", lambda g: g.tensor_scalar(out=hraw.t[:], in0=hraw.t[:], scalar1=sab.t[:, 0:1], scalar2=None, op0=ALU.mult), [hraw, sab], [hraw])
            self.op("dve", lambda g: g.tensor_scalar(out=hraw.t[:, 0:1], in0=hraw.t[:, 0:1], scalar1=sk.t[:, o, cc:cc + 1], scalar2=None, op0=ALU.add),
                    [hraw, sk], [hraw])
            self.dma("sp", hfv[o, cc * 128:(cc + 1) * 128, :], hraw.t[:], reads=[hraw], writes=[Hf], owner=Hf)
    self.end_phase()


Net.hy_inproj = _hy_inproj
Net.hy_filter = _hy_filter


def _hy_fft_tables(self, hc, N1):
    P1 = N1 // 2
    fas = self.sb("hx_fas", [N1, 2 * N1], F32)
    fa = self.sb("hx_fa", [N1, 2 * N1], BF16)
    gs = self.sb("hx_gs", [N1, 2, P1], F32)
    gt = self.sb("hx_g", [N1, 2, P1], BF16)
    self.dma("sp", fas.t[:], hc["Fa"].t.ap(), writes=[fas])
    self.dma("sp", gs.t[:], hc["G"].t.ap(), writes=[gs])
    self.copy("dve", fa, fa.t[:], fas, fas.t[:])
    self.copy("dve", gt, gt.t[:], gs, gs.t[:])
    return fa, gt


def _hy_stageA(self, ub, nrow, fa, A_sb, N1, CG):
    W = 2 * N1
    per = max(1, 512 // W)
    for c0 in range(0, CG, per):
        pbk = self.pb[c0 // per % 2]
        for j in range(per):
            c = c0 + j
            self.op("pe", lambda g: g.matmul(pbk.t[:, j * W:(j + 1) * W], ub.t[:nrow, c, :], fa.t[:nrow, :], start=True, stop=True),
                    [ub, fa], [pbk], pe_acc=True)
        self.op("act", lambda g: g.activation(out=A_sb.t[:, c0:c0 + per, :], in_=pbk.t[:, :per * W].rearrange("p (j w) -> p j w", w=W), func=AF.Copy),
                [pbk], [A_sb])


def _hy_stageC(self, A_sb, ect, k1_0, KB, N1, CG, bre, bim):
    for kk in range(KB):
        k1 = k1_0 + kk
        a_re = A_sb.t[:, :, k1]
        a_im = A_sb.t[:, :, N1 + k1]
        osl = slice(kk * CG, (kk + 1) * CG)
        self.op("pe", lambda g: g.matmul(bre.t[:, osl], ect.t[:, kk, 0, :], a_re, start=True, stop=False), [ect, A_sb], [bre], pe_acc=True)
        self.op("pe", lambda g: g.matmul(bre.t[:, osl], ect.t[:, kk, 2, :], a_im, start=False, stop=True), [ect, A_sb], [bre], pe_acc=True)
        self.op("pe", lambda g: g.matmul(bim.t[:, osl], ect.t[:, kk, 1, :], a_re, start=True, stop=False), [ect, A_sb], [bim], pe_acc=True)
        self.op("pe", lambda g: g.matmul(bim.t[:, osl], ect.t[:, kk, 0, :], a_im, start=False, stop=True), [ect, A_sb], [bim], pe_acc=True)


def _hy_spectrum(self, L, hc, Hf, Hspec):
    N = 2 * L
    N1 = N // 128
    CG = 64
    KB = min(8, N1)
    self.begin_phase()
    fa, gt = _hy_fft_tables(self, hc, N1)
    uf = [self.sb("hs_uf%d" % i, [N1, CG, 128], F32) for i in range(1)]
    ub = self.sb("hs_ub", [N1, CG, 128], BF16)
    A_sb = self.sb("hs_A", [128, CG, 2 * N1], BF16)
    ects = [self.sb("hs_ec%d" % i, [128, KB, 3, 128], BF16) for i in range(2)]
    hso = [self.sb("hs_o%d" % i, [128, KB, 2, CG], F32) for i in range(2)]
    bre, bim = self.pb[2], self.pb[3]
    ecv = hc["Ec"].t.ap().rearrange("k p j n -> p k j n")
    it = 0
    for o in range(2):
        for gi in range(1024 // CG):
            c0 = gi * CG
            u = uf[0]
            self.dma("sp", u.t[:], Hf.t.ap()[o, c0:c0 + CG, :].rearrange("c (a b) -> a c b", b=128), reads=[Hf], writes=[u])
            self.copy("dve", ub, ub.t[:], u, u.t[:])
            _hy_stageA(self, ub, N1, fa, A_sb, N1, CG)
            for k1_0 in range(0, N1, KB):
                ect = ects[it % 2]
                ho = hso[it % 2]
                it += 1
                self.dma("act", ect.t[:], ecv[:, k1_0:k1_0 + KB], writes=[ect])
                _hy_stageC(self, A_sb, ect, k1_0, KB, N1, CG, bre, bim)
                self.op("act", lambda g: g.activation(out=ho.t[:, :, 0, :], in_=bre.t[:, :KB * CG].rearrange("p (k c) -> p k c", c=CG), func=AF.Copy), [bre], [ho])
                self.op("dve", lambda g: g.tensor_copy(out=ho.t[:, :, 1, :], in_=bim.t[:, :KB * CG].rearrange("p (k c) -> p k c", c=CG)), [bim], [ho])
                self.dma("sp", Hspec.t.ap()[o, gi, k1_0:k1_0 + KB].rearrange("k p x -> p k x"), ho.t[:].rearrange("p k a c -> p k (a c)"),
                         reads=[ho], writes=[Hspec], owner=Hspec)
    self.end_phase()


def _hy_conv(self, L, tok0, hc, Z, Hspec):
    N = 2 * L
    N1 = N // 128
    P1 = N1 // 2
    CG = 64
    KB = min(8, N1)
    self.begin_phase()
    fa, gt = _hy_fft_tables(self, hc, N1)
    U1 = self.sb("hv_u", [P1, CG, 128], F32)
    ub = self.sb("hv_ub", [P1, CG, 128], BF16)
    A_sb = self.sb("hv_A", [128, CG, 2 * N1], BF16)
    D_sb = self.sb("hv_D", [128, CG, 2, N1], F32)
    ects = [self.sb("hv_ec%d" % i, [128, KB, 3, 128], BF16) for i in range(2)]
    eits = [self.sb("hv_ei%d" % i, [128, KB, 3, 128], BF16) for i in range(2)]
    hsp = [self.sb("hv_h%d" % i, [128, KB, 2, CG], F32) for i in range(2)]
    tm = [self.sb("hv_t%d" % i, [128, KB, CG], F32) for i in range(4)]
    zz = [self.sb("hv_z%d" % i, [128, 2, KB, CG], BF16) for i in range(2)]
    dts = [self.sb("hv_dt%d" % i, [N1, 2, 4, 128], BF16) for i in range(2)]
    gts = [self.sb("hv_g%d" % i, [P1, 4, 128], F32) for i in range(2)]
    yo = [self.sb("hv_y%d" % i, [P1, 4, 128], BF16) for i in range(2)]
    bre, bim, dre, dim_, tre, tim = self.pb[2], self.pb[3], self.pb[4], self.pb[5], self.pb[6], self.pb[7]
    ecv = hc["Ec"].t.ap().rearrange("k p j n -> p k j n")
    eiv = hc["Ei"].t.ap().rearrange("k p j n -> p k j n")
    zv = Z.t.ap()
    mv = self.M.t.ap()
    it = 0
    it2 = 0
    for gi in range(1024 // CG):
        c0 = gi * CG
        self.dma("sp", U1.t[:], zv[c0:c0 + CG, tok0:tok0 + L].rearrange("c (a b) -> a c b", b=128), reads=[Z], writes=[U1])
        for o in range(2):
            self.copy("dve", ub, ub.t[:], U1, U1.t[:])
            _hy_stageA(self, ub, P1, fa, A_sb, N1, CG)
            for k1_0 in range(0, N1, KB):
                ect, eit, hs = ects[it % 2], eits[it % 2], hsp[it % 2]
                z = zz[it % 2]
                it += 1
                self.dma("act", ect.t[:], ecv[:, k1_0:k1_0 + KB], writes=[ect])
                self.dma("act", eit.t[:], eiv[:, k1_0:k1_0 + KB], writes=[eit])
                self.dma("sp", hs.t[:].rearrange("p k a c -> p k (a c)"), Hspec.t.ap()[o, gi, k1_0:k1_0 + KB].rearrange("k p x -> p k x"),
                         reads=[Hspec], writes=[hs])
                _hy_stageC(self, A_sb, ect, k1_0, KB, N1, CG, bre, bim)
                xre = bre.t[:, :KB * CG].rearrange("p (k c) -> p k c", c=CG)
                xim = bim.t[:, :KB * CG].rearrange("p (k c) -> p k c", c=CG)
                hre, him = hs.t[:, :, 0, :], hs.t[:, :, 1, :]
                self.op("dve", lambda g: g.tensor_tensor(out=tm[0].t[:], in0=xre, in1=hre, op=ALU.mult), [bre, hs], [tm[0]])
                self.op("dve", lambda g: g.tensor_tensor(out=tm[1].t[:], in0=xim, in1=him, op=ALU.mult), [bim, hs], [tm[1]])
                self.op("pool", lambda g: g.tensor_tensor(out=z.t[:, 0], in0=tm[0].t[:], in1=tm[1].t[:], op=ALU.subtract), [tm[0], tm[1]], [z])
                self.op("dve", lambda g: g.tensor_tensor(out=tm[2].t[:], in0=xre, in1=him, op=ALU.mult), [bre, hs], [tm[2]])
                self.op("dve", lambda g: g.tensor_tensor(out=tm[3].t[:], in0=xim, in1=hre, op=ALU.mult), [bim, hs], [tm[3]])
                self.op("pool", lambda g: g.tensor_tensor(out=z.t[:, 1], in0=tm[2].t[:], in1=tm[3].t[:], op=ALU.add), [tm[2], tm[3]], [z])
                for kk in range(KB):
                    osl = slice(kk * CG, (kk + 1) * CG)
                    self.op("pe", lambda g: g.matmul(dre.t[:, osl], eit.t[:, kk, 0, :], z.t[:, 0, kk, :], start=True, stop=False), [eit, z], [dre], pe_acc=True)
                    self.op("pe", lambda g: g.matmul(dre.t[:, osl], eit.t[:, kk, 2, :], z.t[:, 1, kk, :], start=False, stop=True), [eit, z], [dre], pe_acc=True)
                    self.op("pe", lambda g: g.matmul(dim_.t[:, osl], eit.t[:, kk, 1, :], z.t[:, 0, kk, :], start=True, stop=False), [eit, z], [dim_], pe_acc=True)
                    self.op("pe", lambda g: g.matmul(dim_.t[:, osl], eit.t[:, kk, 0, :], z.t[:, 1, kk, :], start=False, stop=True), [eit, z], [dim_], pe_acc=True)
                self.op("act", lambda g: g.activation(out=D_sb.t[:, :, 0, k1_0:k1_0 + KB], in_=dre.t[:, :KB * CG].rearrange("p (k c) -> p c k", c=CG), func=AF.Copy),
                        [dre], [D_sb])
                self.op("act", lambda g: g.activation(out=D_sb.t[:, :, 1, k1_0:k1_0 + KB], in_=dim_.t[:, :KB * CG].rearrange("p (k c) -> p c k", c=CG), func=AF.Copy),
                        [dim_], [D_sb])
            for s0 in range(0, CG, 4):
                dt = dts[it2 % 2]
                gtile = gts[it2 % 2]
                y_o = yo[it2 % 2]
                it2 += 1
                for j in range(4):
                    self.op("pe", lambda g: g.transpose(out=tre.t[:N1, j * 128:(j + 1) * 128], in_=D_sb.t[:, s0 + j, 0, :], identity=self.ident.t[:, :]),
                            [D_sb, self.ident], [tre], pe_acc=True)
                    self.op("pe", lambda g: g.transpose(out=tim.t[:N1, j * 128:(j + 1) * 128], in_=D_sb.t[:, s0 + j, 1, :], identity=self.ident.t[:, :]),
                            [D_sb, self.ident], [tim], pe_acc=True)
                self.op("act", lambda g: g.activation(out=dt.t[:, 0].rearrange("p j n -> p (j n)"), in_=tre.t[:N1, :], func=AF.Copy), [tre], [dt])
                self.op("dve", lambda g: g.tensor_copy(out=dt.t[:, 1].rearrange("p j n -> p (j n)"), in_=tim.t[:N1, :]), [tim], [dt])
                pby = self.pb[s0 // 4 % 2]
                self.op("pe", lambda g: g.matmul(pby.t[:P1, :], gt.t[:, 0, :], dt.t[:, 0].rearrange("p j n -> p (j n)"), start=True, stop=False), [gt, dt], [pby], pe_acc=True)
                self.op("pe", lambda g: g.matmul(pby.t[:P1, :], gt.t[:, 1, :], dt.t[:, 1].rearrange("p j n -> p (j n)"), start=False, stop=True), [gt, dt], [pby], pe_acc=True)
                cg0 = (1 + o) * 1024 + c0 + s0
                self.dma("sp", gtile.t[:], zv[cg0:cg0 + 4, tok0:tok0 + L].rearrange("c (a b) -> a c b", b=128), reads=[Z], writes=[gtile])
                if o == 0:
                    self.op("dve", lambda g: g.tensor_tensor(out=U1.t[:, s0:s0 + 4, :].rearrange("p j n -> p (j n)"), in0=pby.t[:P1, :],
                            in1=gtile.t[:].rearrange("p j n -> p (j n)"), op=ALU.mult), [pby, gtile], [U1])
                else:
                    self.op("dve", lambda g: g.tensor_tensor(out=y_o.t[:].rearrange("p j n -> p (j n)"), in0=pby.t[:P1, :],
                            in1=gtile.t[:].rearrange("p j n -> p (j n)"), op=ALU.mult), [pby, gtile], [y_o])
                    self.dma("act", mv[c0 + s0:c0 + s0 + 4, tok0:tok0 + L].rearrange("c (a b) -> a c b", b=128), y_o.t[:], reads=[y_o], writes=[self.M], owner=self.M)
    self.end_phase()


Net.hy_spectrum = _hy_spectrum
Net.hy_conv = _hy_conv


_FWN = {"w1": [33, 64], "b1": [64, 1], "w2": [64, 64], "b2": [64, 1], "w3": [64, 64], "b3": [64, 1], "w4": [64, 4096], "freq": [64, 1]}


def build_full(cfg):
    import ml_dtypes
    net = Net(cfg)
    net.setup_common()
    net.build_ada()
    SEQ, CTX, NTOK = cfg.SEQ, cfg.CTX, cfg.NTOK
    consts = {}
    mlp_w1 = net.inp("mlp_w1", [4, 1024, 4096]).t.ap()
    mlp_w2 = net.inp("mlp_w2", [4, 4096, 1024]).t.ap()
    hy_w_in = net.inp("hy_w_in", [2, 1024, 3072]).t.ap()
    hy_b_in = net.inp("hy_b_in", [2, 128, 24]).t.ap()
    hy_cw = net.inp("hy_cw", [2, 128, 3, 24]).t.ap()
    hy_cb = net.inp("hy_cb", [2, 128, 24]).t.ap()
    hy_f = {k: net.inp("hy_f_" + k, [2] + v).t.ap() for k, v in _FWN.items()}
    hy_skip = net.inp("hy_skip", [2, 128, 2, 8]).t.ap()
    hy_w_out = net.inp("hy_w_out", [2, 1024, 1024]).t.ap()
    hy_b_out = net.inp("hy_b_out", [128, 2, 8])
    nd = net.inp("negdelta", [128, 8])
    consts["negdelta"] = hyena_negdelta()

    def hconsts(L, tag):
        hcn = hyena_consts(L)
        tk = {}
        for k, v in hcn.items():
            nm = "hc%s_%s" % (tag, k)
            tk[k] = net.inp(nm, list(v.shape), BF16 if v.dtype == ml_dtypes.bfloat16 else F32)
            consts[nm] = v
        tk["negdelta"] = nd
        return tk
    tkL = hconsts(SEQ, "L")
    tkC = hconsts(CTX, "C")
    gla_w_in = net.inp("gla_w_in", [1024, 3104])
    gla_gk_w2 = net.inp("gla_gk_w2", [2, 16, 512])
    gla_gk_b = net.inp("gla_gk_b", [128, 2, 4])
    gla_onorm = net.inp("gla_onorm", [128, 2])
    gla_wo = net.inp("gla_wo", [1024, 1024]).t.ap()
    masks = net.inp("masks", [128, 2, 4, 128])
    seg = net.inp("seg", [128, 512])
    mk, sg, _ = gla_consts()
    consts["masks"], consts["seg"], consts["ident"] = mk, sg, np.eye(128, dtype=np.float32)
    mla_w_down = net.inp("mla_w_down", [1024, 704])
    mla_qnorm = net.inp("mla_qnorm", [128, 3])
    mla_w_uq = net.inp("mla_w_uq", [384, 1536])
    mla_kvnorm = net.inp("mla_kvnorm", [128, 2])
    mla_w_ukv = net.inp("mla_w_ukv", [256, 2048])
    mla_wo = net.inp("mla_wo", [1024, 1024]).t.ap()
    cos_t = net.inp("cos_t", [64, NTOK])
    sin_t = net.inp("sin_t", [64, NTOK])
    consts["cos_t"], consts["sin_t"] = rope_tables(SEQ, NTOK)
    bo = net.sb("hy_bo", [128, 2, 8], F32)
    net.dma("sp", bo.t[:], hy_b_out.t.ap(), writes=[bo])
    bo_j = []
    for j in range(2):
        t = net.sb("hy_bo%d" % j, [128, 8], F32)
        net.op("dve", lambda g: g.tensor_copy(out=t.t[:], in_=bo.t[:, j, :]), [bo], [t])
        bo_j.append(t)
    Z = net.dram("Z", [3072, NTOK], F32)
    HfL = net.dram("HfL", [2, 1024, 2 * SEQ], F32)
    HsL = net.dram("HsL", [2, 16, 2 * SEQ // 128, 128, 128], F32)
    HfC = net.dram("HfC", [2, 1024, 2 * CTX], F32)
    HsC = net.dram("HsC", [2, 16, 2 * CTX // 128, 128, 128], F32)

    def hyena(j, with_ctx):
        net.hy_inproj(hy_w_in[j], hy_b_in[j], hy_cw[j], hy_cb[j], Z, with_ctx)
        fw = {k: v[j] for k, v in hy_f.items()}
        net.hy_filter(SEQ, tkL, fw, HfL, hy_skip[j])
        net.hy_spectrum(SEQ, tkL, HfL, HsL)
        net.hy_conv(SEQ, 0, tkL, Z, HsL)
        if with_ctx:
            net.hy_filter(CTX, tkC, fw, HfC, hy_skip[j])
            net.hy_spectrum(CTX, tkC, HfC, HsC)
            net.hy_conv(CTX, SEQ, tkC, Z, HsC)

    X = net.X
    net.phase_A0()
    hyena(0, True)
    net.phase_C(0, hy_w_out[0], bo_j[0], mlp_w1[0], mlp_w2[0], net.xT, X[0], True, True, False)
    net.phase_gla(gla_w_in, gla_gk_w2, gla_gk_b, gla_onorm, masks, seg)
    net.phase_C(1, gla_wo, net.zero8, mlp_w1[1], mlp_w2[1], X[0], X[1], True, True, False)
    net.phase_mla(mla_w_down, mla_qnorm, mla_w_uq, mla_kvnorm, mla_w_ukv, cos_t, sin_t)
    net.phase_C(2, mla_wo, net.zero8, mlp_w1[2], mlp_w2[2], X[1], X[0], False, False, False)
    hyena(1, False)
    net.phase_C(3, hy_w_out[1], bo_j[1], mlp_w1[3], mlp_w2[3], X[0], X[1], False, False, True)
    net.finish()
    return net, consts


def _pc(v, nch):
    return np.ascontiguousarray(np.asarray(v, np.float32).reshape(nch, 128).T)


def make_shared(inp, consts):
    f = lambda a: np.ascontiguousarray(np.asarray(a, np.float32))
    sh = dict(consts)
    sh["ada_w"] = f(inp["ada_w"])
    sh["ada_b"] = np.ascontiguousarray(np.stack([_pc(inp["ada_b"][l], 48) for l in range(4)], axis=1))
    sh["n1g"] = np.ascontiguousarray(np.stack([_pc(inp["norm1_g"][l], 8) for l in range(4)], axis=1))
    sh["n2g"] = np.ascontiguousarray(np.stack([_pc(inp["norm2_g"][l], 8) for l in range(4)], axis=1))
    sh["fing"] = _pc(inp["final_g"], 8)[:, :, None].copy()
    sh["mlp_w1"] = f(inp["mlp_w1"])
    sh["mlp_w2"] = f(inp["mlp_w2"])
    sh["hy_w_in"] = f(inp["hy_w_in"])
    sh["hy_b_in"] = np.ascontiguousarray(np.stack([_pc(inp["hy_b_in"][j], 24) for j in range(2)]))
    sh["hy_cb"] = np.ascontiguousarray(np.stack([_pc(inp["hy_conv_b"][j], 24) for j in range(2)]))
    sh["hy_cw"] = np.ascontiguousarray(np.stack([np.stack([_pc(inp["hy_conv_w"][j][t], 24) for t in range(3)], axis=1) for j in range(2)]))
    sh["hy_f_w1"] = f(inp["hy_f_w1"])
    sh["hy_f_w2"] = f(inp["hy_f_w2"])
    sh["hy_f_w3"] = f(inp["hy_f_w3"])
    sh["hy_f_w4"] = f(inp["hy_f_w4"])
    sh["hy_f_b1"] = f(inp["hy_f_b1"])[:, :, None].copy()
    sh["hy_f_b2"] = f(inp["hy_f_b2"])[:, :, None].copy()
    sh["hy_f_b3"] = f(inp["hy_f_b3"])[:, :, None].copy()
    sh["hy_f_freq"] = f(inp["hy_freq"])[:, :, None].copy()
    sh["hy_skip"] = np.ascontiguousarray(np.stack([np.stack([_pc(inp["hy_bias"][j][o], 8) for o in range(2)], axis=1) for j in range(2)]))
    sh["hy_w_out"] = f(inp["hy_w_out"])
    sh["hy_b_out"] = np.ascontiguousarray(np.stack([_pc(inp["hy_b_out"][j], 8) for j in range(2)], axis=1))
    sh["gla_w_in"] = f(inp["gla_w_in"][0])
    sh["gla_gk_w2"] = f(inp["gla_gk_w2"][0])
    sh["gla_gk_b"] = np.ascontiguousarray(np.asarray(inp["gla_gk_b"][0], np.float32).reshape(2, 4, 128).transpose(2, 0, 1))
    sh["gla_onorm"] = _pc(inp["gla_onorm"][0], 2)
    sh["gla_wo"] = f(inp["gla_wo"][0])
    sh["mla_w_down"] = f(inp["mla_w_down"][0])
    sh["mla_qnorm"] = _pc(inp["mla_qnorm"][0], 3)
    sh["mla_w_uq"] = f(inp["mla_w_uq"][0])
    sh["mla_kvnorm"] = _pc(inp["mla_kvnorm"][0], 2)
    sh["mla_w_ukv"] = f(inp["mla_w_ukv"][0])
    sh["mla_wo"] = f(inp["mla_wo"][0])
    return sh


def make_core(inp, b):
    x = np.asarray(inp["x"][b], np.float32)
    cx = np.asarray(inp["ctx"][b], np.float32)
    xT = np.ascontiguousarray(np.concatenate([x, cx], axis=0).T)
    cc = np.ascontiguousarray(np.stack([_pc(inp["c"][b], 8), _pc(inp["c_ctx"], 8)], axis=-1))
    return {"xT": xT, "cc": cc}


_CACHE = {}
NCORES = int(os.environ.get('NCORES', '4'))


def kernel(**inputs):
    SEQ = inputs["x"].shape[1]
    CTX = inputs["ctx"].shape[1]
    B = inputs["x"].shape[0]
    key = (SEQ, CTX)
    if key not in _CACHE:
        _CACHE[key] = build_full(Cfg(SEQ=SEQ, CTX=CTX, TB=256))
    net, consts = _CACHE[key]
    shared = make_shared(inputs, consts)
    n = NCORES
    in_maps = []
    for core in range(n):
        m = dict(shared)
        m.update(make_core(inputs, core % B))
        in_maps.append(m)
    res = run_bass_kernel_spmd(net.nc, in_maps, core_ids=list(range(n)))
    out = np.stack([np.ascontiguousarray(res.results[b]["outT"].T) for b in range(B)], axis=0)
    return out.astype(np.float32)
```
